# Optimizing a Trainium2 kernel written in Bass

```python
import jax, jax.numpy as jnp
from jax import lax
import numpy as np

D_MODEL = 1024
BATCH = 8
SEQ = 8192
DEPTH = 4
DEC_BATCH = 16
DEC_SEQ = 16
PAST_LEN = 1024

CHUNK = 64
CONV_WIDTH = 4
EPS = 1e-5
D_MIX = 2 * D_MODEL
D_GROUP = D_MIX // 4
SSD_HEAD_DIM = 64
SSD_HEADS = D_GROUP // SSD_HEAD_DIM
SSD_GROUPS = 2
SSD_STATE = 64
SSD_CONV_DIM = D_GROUP + 2 * SSD_GROUPS * SSD_STATE
MLSTM_HEADS = 4
MLSTM_DK = D_GROUP // MLSTM_HEADS
MLSTM_DV = D_GROUP // MLSTM_HEADS
RGLRU_BLOCKS = 8
RGLRU_BLOCK_DIM = D_GROUP // RGLRU_BLOCKS
RGLRU_C = 8.0
GLA_HEADS = 4
GLA_DV = D_GROUP // GLA_HEADS
GLA_DK = GLA_DV // 2
GLA_GATE_RANK = 16
GLA_TAU = 16.0
D_FF = 4 * D_MODEL
DEEPNORM_ALPHA = (2 * DEPTH) ** 0.25
DEEPNORM_BETA = (8 * DEPTH) ** -0.25

IN_SPLIT_SIZES = (
    D_GROUP,
    SSD_CONV_DIM,
    SSD_HEADS,
    MLSTM_HEADS * MLSTM_DK,
    MLSTM_HEADS * MLSTM_DK,
    MLSTM_HEADS * MLSTM_DV,
    2 * MLSTM_HEADS,
    MLSTM_HEADS * MLSTM_DV,
    D_GROUP,
    D_GROUP,
    GLA_HEADS * GLA_DK,
    GLA_HEADS * GLA_DK,
    GLA_HEADS * GLA_DV,
    GLA_GATE_RANK,
    GLA_HEADS * GLA_DV,
)
D_IN = sum(IN_SPLIT_SIZES)
IN_SPLIT_POINTS = tuple(int(s) for s in np.cumsum(IN_SPLIT_SIZES)[:-1])

kernel_name = "hybrid_ssd_mlstm_rglru_gla_stream_step"


def layer_norm(x, g, b):
    xf = x.astype(jnp.float32)
    mu = jnp.mean(xf, -1, keepdims=True)
    var = jnp.mean(jnp.square(xf - mu), -1, keepdims=True)
    return ((xf - mu) * lax.rsqrt(var + EPS) * g + b).astype(x.dtype)


def rms_norm(x, g):
    return x * lax.rsqrt(jnp.mean(x * x, -1, keepdims=True) + EPS) * g


def head_rms_norm(t, g):
    t = t * lax.rsqrt(jnp.mean(t * t, -1, keepdims=True) + EPS)
    return t.reshape(t.shape[:2] + (-1,)) * g


def causal_conv(u, buf, w, b):
    seq = u.shape[1]
    up = jnp.concatenate([buf, u], axis=1)
    out = b + up[:, 0:seq] * w[0]
    for j in range(1, CONV_WIDTH):
        out = out + up[:, j:j + seq] * w[j]
    return out, up[:, seq:]


def _chunk_len(seq):
    return CHUNK if seq % CHUNK == 0 else seq


def _chunks(t, nc, cs):
    return t.reshape((t.shape[0], nc, cs) + t.shape[2:])


def ssd_scan(xs, bm, cm, dt, a_neg, d_skip, h0):
    bsz, seq, nh, _ = xs.shape
    cs = _chunk_len(seq)
    nc = seq // cs
    rep = nh // SSD_GROUPS
    bh = jnp.repeat(bm, rep, axis=2)
    ch = jnp.repeat(cm, rep, axis=2)
    xc, bc, cc, dtc = (_chunks(t, nc, cs) for t in (xs, bh, ch, dt))
    acum = jnp.cumsum(dtc * a_neg, axis=2)
    tri = jnp.tril(jnp.ones((cs, cs), dtype=bool))[:, :, None]
    decay = jnp.exp(jnp.where(tri, acum[:, :, :, None, :] - acum[:, :, None, :, :], -jnp.inf))
    scores = jnp.einsum('bcthn,bcshn->bctsh', cc, bc) * decay * dtc[:, :, None, :, :]
    y_intra = jnp.einsum('bctsh,bcshp->bcthp', scores, xc)
    w_last = jnp.exp(acum[:, :, -1:, :] - acum) * dtc
    s_chunk = jnp.einsum('bcsh,bcshp,bcshn->bchpn', w_last, xc, bc)
    d_chunk = jnp.exp(acum[:, :, -1, :])

    def step(h, inp):
        s_c, d_c = inp
        return d_c[:, :, None, None] * h + s_c, h

    h_last, h_in = lax.scan(step, h0, (jnp.moveaxis(s_chunk, 1, 0), jnp.moveaxis(d_chunk, 1, 0)))
    h_in = jnp.moveaxis(h_in, 0, 1)
    y_inter = jnp.einsum('bcthn,bchpn->bcthp', cc * jnp.exp(acum)[..., None], h_in)
    y = (y_intra + y_inter).reshape(xs.shape) + d_skip[:, None] * xs
    return y, h_last


def mlstm_scan(q, k, v, i_pre, f_pre, c0, n0, m0):
    bsz, seq, nh, dk = q.shape
    cs = _chunk_len(seq)
    nc = seq // cs
    k = k * (dk ** -0.5)
    logf = jax.nn.log_sigmoid(f_pre)
    qc, kc, vc, ic, lfc = (_chunks(t, nc, cs) for t in (q, k, v, i_pre, logf))
    bcum = jnp.cumsum(lfc, axis=2)
    tri = jnp.tril(jnp.ones((cs, cs), dtype=bool))[:, :, None]
    dmat = jnp.where(tri, bcum[:, :, :, None, :] - bcum[:, :, None, :, :] + ic[:, :, None, :, :], -jnp.inf)
    m_intra = jnp.max(dmat, axis=3)
    qk = jnp.einsum('bcthd,bcshd->bctsh', qc, kc) * jnp.exp(dmat - m_intra[:, :, :, None, :])
    num_intra = jnp.einsum('bctsh,bcshv->bcthv', qk, vc)
    den_intra = jnp.sum(qk, axis=3)
    b_last = bcum[:, :, -1, :]
    lw = b_last[:, :, None, :] - bcum + ic
    m_loc = jnp.max(lw, axis=2)
    wl = jnp.exp(lw - m_loc[:, :, None, :])
    c_chunk = jnp.einsum('bcsh,bcshd,bcshv->bchdv', wl, kc, vc)
    n_chunk = jnp.einsum('bcsh,bcshd->bchd', wl, kc)

    def step(carry, inp):
        c_s, n_s, m_s = carry
        c_c, n_c, m_c, bl = inp
        m_new = jnp.maximum(bl + m_s, m_c)
        a_old = jnp.exp(bl + m_s - m_new)
        a_new = jnp.exp(m_c - m_new)
        c_out = a_old[..., None, None] * c_s + a_new[..., None, None] * c_c
        n_out = a_old[..., None] * n_s + a_new[..., None] * n_c
        return (c_out, n_out, m_new), (c_s, n_s, m_s)

    xs_in = tuple(jnp.moveaxis(t, 1, 0) for t in (c_chunk, n_chunk, m_loc, b_last))
    (c_last, n_last, m_last), (c_in, n_in, m_in) = lax.scan(step, (c0, n0, m0), xs_in)
    c_in = jnp.moveaxis(c_in, 0, 1)
    n_in = jnp.moveaxis(n_in, 0, 1)
    m_in = jnp.moveaxis(m_in, 0, 1)
    g = bcum + m_in[:, :, None, :]
    m_t = jnp.maximum(g, m_intra)
    a_inter = jnp.exp(g - m_t)
    a_intra = jnp.exp(m_intra - m_t)
    num = a_inter[..., None] * jnp.einsum('bcthd,bchdv->bcthv', qc, c_in) + a_intra[..., None] * num_intra
    den = a_inter * jnp.einsum('bcthd,bchd->bcth', qc, n_in) + a_intra * den_intra
    h = num / jnp.maximum(jnp.abs(den), jnp.exp(-m_t))[..., None]
    return h.reshape(v.shape), c_last, n_last, m_last


def rglru_scan(xr, w_a, b_a, w_x, b_x, lam, h0):
    bsz, seq, dim = xr.shape
    xb = xr.reshape(bsz, seq, RGLRU_BLOCKS, RGLRU_BLOCK_DIM)
    r = jax.nn.sigmoid(jnp.einsum('blnk,nkj->blnj', xb, w_a).reshape(bsz, seq, dim) + b_a)
    i = jax.nn.sigmoid(jnp.einsum('blnk,nkj->blnj', xb, w_x).reshape(bsz, seq, dim) + b_x)
    log_a = RGLRU_C * r * jax.nn.log_sigmoid(lam)
    a = jnp.exp(log_a)
    bterm = jnp.sqrt(-jnp.expm1(2.0 * log_a)) * (i * xr)
    bterm = bterm.at[:, 0].add(a[:, 0] * h0)

    def combine(left, right):
        a1, b1 = left
        a2, b2 = right
        return a1 * a2, a2 * b1 + b2

    _, h = lax.associative_scan(combine, (a, bterm), axis=1)
    return h, h[:, -1]


def gla_scan(q, k, v, log_alpha, s0):
    bsz, seq, nh, dk = q.shape
    cs = _chunk_len(seq)
    nc = seq // cs
    q = q * (dk ** -0.5)
    qc, kc, vc, lac = (_chunks(t, nc, cs) for t in (q, k, v, log_alpha))
    bcum = jnp.cumsum(lac, axis=2)
    q_t = qc * jnp.exp(bcum)
    k_t = kc * jnp.exp(-bcum)
    tri = jnp.tril(jnp.ones((cs, cs), dtype=bool))[:, :, None]
    att = jnp.where(tri, jnp.einsum('bcthd,bcshd->bctsh', q_t, k_t), 0.0)
    o_intra = jnp.einsum('bctsh,bcshv->bcthv', att, vc)
    b_last = bcum[:, :, -1]
    k_d = kc * jnp.exp(b_last[:, :, None] - bcum)
    s_chunk = jnp.einsum('bcshd,bcshv->bchdv', k_d, vc)
    d_chunk = jnp.exp(b_last)

    def step(s, inp):
        s_c, d_c = inp
        return d_c[..., None] * s + s_c, s

    s_last, s_in = lax.scan(step, s0, (jnp.moveaxis(s_chunk, 1, 0), jnp.moveaxis(d_chunk, 1, 0)))
    s_in = jnp.moveaxis(s_in, 0, 1)
    o_inter = jnp.einsum('bcthd,bchdv->bcthv', q_t, s_in)
    return (o_intra + o_inter).reshape(v.shape), s_last


def state_shapes(bsz):
    return (
        (bsz, SSD_HEADS, SSD_HEAD_DIM, SSD_STATE),
        (bsz, CONV_WIDTH - 1, SSD_CONV_DIM),
        (bsz, MLSTM_HEADS, MLSTM_DK, MLSTM_DV),
        (bsz, MLSTM_HEADS, MLSTM_DK),
        (bsz, MLSTM_HEADS),
        (bsz, D_GROUP),
        (bsz, CONV_WIDTH - 1, D_GROUP),
        (bsz, GLA_HEADS, GLA_DK, GLA_DV),
    )


def trunk_layer(x, states, p):
    f32 = jnp.float32
    bsz, seq, _ = x.shape
    s_ssd_h, s_ssd_conv, s_c, s_n, s_m, s_rg_h, s_rg_conv, s_gla = (s.astype(f32) for s in states)
    proj = jnp.einsum('bld,de->ble', x, p['w_in']).astype(f32)
    (z_s, xbc_s, dt_s, q_m, k_m, v_m, if_m, o_m,
     x_r, y_r, q_g, k_g, v_g, a_g, g_g) = jnp.split(proj, IN_SPLIT_POINTS, axis=-1)

    xbc, new_ssd_conv = causal_conv(xbc_s, s_ssd_conv, p['ssd_conv_w'], p['ssd_conv_b'])
    xbc = jax.nn.silu(xbc)
    xs, bm, cm = jnp.split(xbc, (D_GROUP, D_GROUP + SSD_GROUPS * SSD_STATE), axis=-1)
    dt = jax.nn.softplus(dt_s + p['ssd_dt_bias'])
    a_neg = -jnp.exp(p['ssd_A_log'].astype(f32))
    y_ssd, new_ssd_h = ssd_scan(
        xs.reshape(bsz, seq, SSD_HEADS, SSD_HEAD_DIM),
        bm.reshape(bsz, seq, SSD_GROUPS, SSD_STATE),
        cm.reshape(bsz, seq, SSD_GROUPS, SSD_STATE),
        dt, a_neg, p['ssd_D'], s_ssd_h)
    y_ssd = rms_norm(y_ssd.reshape(bsz, seq, D_GROUP) * jax.nn.silu(z_s), p['ssd_norm_w'])

    ifg = if_m + p['mlstm_if_b']
    h_m, new_c, new_n, new_m = mlstm_scan(
        q_m.reshape(bsz, seq, MLSTM_HEADS, MLSTM_DK),
        k_m.reshape(bsz, seq, MLSTM_HEADS, MLSTM_DK),
        v_m.reshape(bsz, seq, MLSTM_HEADS, MLSTM_DV),
        ifg[..., :MLSTM_HEADS], ifg[..., MLSTM_HEADS:], s_c, s_n, s_m)
    y_m = jax.nn.sigmoid(o_m) * head_rms_norm(h_m, p['mlstm_norm_w'])

    xr, new_rg_conv = causal_conv(x_r, s_rg_conv, p['rg_conv_w'], p['rg_conv_b'])
    h_r, new_rg_h = rglru_scan(xr, p['rg_gate_a_w'], p['rg_gate_a_b'], p['rg_gate_x_w'],
                               p['rg_gate_x_b'], p['rg_lambda'], s_rg_h)
    y_rg = h_r * jax.nn.gelu(y_r)

    log_alpha = jax.nn.log_sigmoid(a_g @ p['gla_gate_w2'] + p['gla_gate_b']) / GLA_TAU
    o_g, new_gla = gla_scan(
        q_g.reshape(bsz, seq, GLA_HEADS, GLA_DK),
        k_g.reshape(bsz, seq, GLA_HEADS, GLA_DK),
        v_g.reshape(bsz, seq, GLA_HEADS, GLA_DV),
        log_alpha.reshape(bsz, seq, GLA_HEADS, GLA_DK), s_gla)
    y_g = head_rms_norm(o_g, p['gla_norm_w']) * jax.nn.silu(g_g)

    mix = jnp.concatenate([y_ssd, y_m, y_rg, y_g], axis=-1).astype(x.dtype)
    x = layer_norm(DEEPNORM_ALPHA * x + mix @ p['w_out'], p['ln1_g'], p['ln1_b'])
    hid = jnp.square(jax.nn.relu(x @ p['mlp_w1'] + p['mlp_b1']))
    x = layer_norm(DEEPNORM_ALPHA * x + hid @ p['mlp_w2'] + p['mlp_b2'], p['ln2_g'], p['ln2_b'])
    new_states = tuple(s.astype(x.dtype) for s in
                       (new_ssd_h, new_ssd_conv, new_c, new_n, new_m, new_rg_h, new_rg_conv, new_gla))
    return x, new_states


def setup_inputs(seed: int = 0) -> dict:
    key = jax.random.key(seed)
    ks = iter(jax.random.split(key, 64))

    def nrm(shape, scale=1.0):
        return scale * jax.random.normal(next(ks), shape, jnp.float32)

    def unif(shape, lo, hi):
        return jax.random.uniform(next(ks), shape, jnp.float32, lo, hi)

    L = DEPTH
    sh = state_shapes(DEC_BATCH)
    dt0 = jnp.exp(unif((L, SSD_HEADS), float(np.log(1e-3)), float(np.log(1e-1))))
    a_pow = unif((L, D_GROUP), 0.9, 0.999)
    s_base = a_pow ** (1.0 / RGLRU_C)
    f_bias = jnp.linspace(3.0, 6.0, MLSTM_HEADS, dtype=jnp.float32)[None, :] + nrm((L, MLSTM_HEADS), 0.1)
    i_bias = nrm((L, MLSTM_HEADS), 0.1)
    return {
        'x_prompt': nrm((BATCH, SEQ, D_MODEL)),
        'x_sample': nrm((DEC_BATCH, DEC_SEQ, D_MODEL)),
        'state_ssd_h': nrm((L,) + sh[0], 0.1),
        'state_ssd_conv': nrm((L,) + sh[1]),
        'state_mlstm_C': nrm((L,) + sh[2], 0.1),
        'state_mlstm_n': nrm((L,) + sh[3], 0.1),
        'state_mlstm_m': nrm((L,) + sh[4], 0.5),
        'state_rglru_h': nrm((L,) + sh[5], 0.5),
        'state_rglru_conv': nrm((L,) + sh[6]),
        'state_gla_S': nrm((L,) + sh[7], 0.1),
        'ln_in_g': 1.0 + nrm((D_MODEL,), 0.02),
        'ln_in_b': nrm((D_MODEL,), 0.02),
        'w_in': nrm((L, D_MODEL, D_IN), D_MODEL ** -0.5),
        'ssd_conv_w': nrm((L, CONV_WIDTH, SSD_CONV_DIM), CONV_WIDTH ** -0.5),
        'ssd_conv_b': nrm((L, SSD_CONV_DIM), 0.02),
        'ssd_dt_bias': dt0 + jnp.log(-jnp.expm1(-dt0)),
        'ssd_A_log': jnp.log(unif((L, SSD_HEADS), 1.0, 16.0)),
        'ssd_D': 1.0 + nrm((L, SSD_HEADS), 0.1),
        'ssd_norm_w': 1.0 + nrm((L, D_GROUP), 0.02),
        'mlstm_if_b': jnp.concatenate([i_bias, f_bias], axis=-1),
        'mlstm_norm_w': 1.0 + nrm((L, D_GROUP), 0.02),
        'rg_conv_w': nrm((L, CONV_WIDTH, D_GROUP), CONV_WIDTH ** -0.5),
        'rg_conv_b': nrm((L, D_GROUP), 0.02),
        'rg_gate_a_w': nrm((L, RGLRU_BLOCKS, RGLRU_BLOCK_DIM, RGLRU_BLOCK_DIM), RGLRU_BLOCK_DIM ** -0.5),
        'rg_gate_a_b': nrm((L, D_GROUP), 0.02),
        'rg_gate_x_w': nrm((L, RGLRU_BLOCKS, RGLRU_BLOCK_DIM, RGLRU_BLOCK_DIM), RGLRU_BLOCK_DIM ** -0.5),
        'rg_gate_x_b': nrm((L, D_GROUP), 0.02),
        'rg_lambda': jnp.log(s_base) - jnp.log1p(-s_base),
        'gla_gate_w2': nrm((L, GLA_GATE_RANK, GLA_HEADS * GLA_DK), GLA_GATE_RANK ** -0.5),
        'gla_gate_b': nrm((L, GLA_HEADS * GLA_DK), 0.1),
        'gla_norm_w': 1.0 + nrm((L, D_GROUP), 0.02),
        'w_out': nrm((L, D_MIX, D_MODEL), DEEPNORM_BETA * D_MIX ** -0.5),
        'ln1_g': 1.0 + nrm((L, D_MODEL), 0.02),
        'ln1_b': nrm((L, D_MODEL), 0.02),
        'mlp_w1': nrm((L, D_MODEL, D_FF), D_MODEL ** -0.5),
        'mlp_b1': nrm((L, D_FF), 0.02),
        'mlp_w2': nrm((L, D_FF, D_MODEL), DEEPNORM_BETA * D_FF ** -0.5),
        'mlp_b2': nrm((L, D_MODEL), 0.02),
        'ln2_g': 1.0 + nrm((L, D_MODEL), 0.02),
        'ln2_b': nrm((L, D_MODEL), 0.02),
    }


def reference(x_prompt, x_sample, state_ssd_h, state_ssd_conv, state_mlstm_C, state_mlstm_n,
              state_mlstm_m, state_rglru_h, state_rglru_conv, state_gla_S, ln_in_g, ln_in_b,
              w_in, ssd_conv_w, ssd_conv_b, ssd_dt_bias, ssd_A_log, ssd_D, ssd_norm_w,
              mlstm_if_b, mlstm_norm_w, rg_conv_w, rg_conv_b, rg_gate_a_w, rg_gate_a_b,
              rg_gate_x_w, rg_gate_x_b, rg_lambda, gla_gate_w2, gla_gate_b, gla_norm_w,
              w_out, ln1_g, ln1_b, mlp_w1, mlp_b1, mlp_w2, mlp_b2, ln2_g, ln2_b):
    xp = layer_norm(x_prompt, ln_in_g, ln_in_b)
    xs = layer_norm(x_sample, ln_in_g, ln_in_b)
    prompt_states = []
    sample_states = []
    for l in range(DEPTH):
        p = dict(
            w_in=w_in[l], ssd_conv_w=ssd_conv_w[l], ssd_conv_b=ssd_conv_b[l],
            ssd_dt_bias=ssd_dt_bias[l], ssd_A_log=ssd_A_log[l], ssd_D=ssd_D[l],
            ssd_norm_w=ssd_norm_w[l], mlstm_if_b=mlstm_if_b[l], mlstm_norm_w=mlstm_norm_w[l],
            rg_conv_w=rg_conv_w[l], rg_conv_b=rg_conv_b[l], rg_gate_a_w=rg_gate_a_w[l],
            rg_gate_a_b=rg_gate_a_b[l], rg_gate_x_w=rg_gate_x_w[l], rg_gate_x_b=rg_gate_x_b[l],
            rg_lambda=rg_lambda[l], gla_gate_w2=gla_gate_w2[l], gla_gate_b=gla_gate_b[l],
            gla_norm_w=gla_norm_w[l], w_out=w_out[l], ln1_g=ln1_g[l], ln1_b=ln1_b[l],
            mlp_w1=mlp_w1[l], mlp_b1=mlp_b1[l], mlp_w2=mlp_w2[l], mlp_b2=mlp_b2[l],
            ln2_g=ln2_g[l], ln2_b=ln2_b[l])
        zero_states = tuple(jnp.zeros(s, xp.dtype) for s in state_shapes(xp.shape[0]))
        xp, st_p = trunk_layer(xp, zero_states, p)
        cached = (state_ssd_h[l], state_ssd_conv[l], state_mlstm_C[l], state_mlstm_n[l],
                  state_mlstm_m[l], state_rglru_h[l], state_rglru_conv[l], state_gla_S[l])
        xs, st_s = trunk_layer(xs, cached, p)
        prompt_states.append(st_p)
        sample_states.append(st_s)
    (p_ssd_h, p_ssd_conv, p_mlstm_C, p_mlstm_n, p_mlstm_m, p_rglru_h, p_rglru_conv,
     p_gla_S) = [jnp.stack(v, axis=0) for v in zip(*prompt_states)]
    (s_ssd_h, s_ssd_conv, s_mlstm_C, s_mlstm_n, s_mlstm_m, s_rglru_h, s_rglru_conv,
     s_gla_S) = [jnp.stack(v, axis=0) for v in zip(*sample_states)]
    return (xp, xs,
            p_ssd_h, p_ssd_conv, p_mlstm_C, p_mlstm_n, p_mlstm_m, p_rglru_h, p_rglru_conv, p_gla_S,
            s_ssd_h, s_ssd_conv, s_mlstm_C, s_mlstm_n, s_mlstm_m, s_rglru_h, s_rglru_conv, s_gla_S)
```

```python
import numpy as np
import concourse.bass as bass
import concourse.mybir as mybir
from concourse.bass_utils import run_bass_kernel_spmd
from contextlib import ExitStack

F32 = mybir.dt.float32
F32R = mybir.dt.float32r
BF16 = mybir.dt.bfloat16
ALU = mybir.AluOpType
AF = mybir.ActivationFunctionType
AX = mybir.AxisListType

SAME_ENG_SYNC = True

D = 1024
DEPTH = 4
SEQ = 8192
NCORE = 8
EPS = 1e-5
ALPHA = (2 * DEPTH) ** 0.25
D_IN = 5920
NSLAB = 32
NSLOT = 5

WIN_SLABS = [
    [(0, 512, 512)],
    [(0, 1024, 256), (256, 1280, 8), (264, 2824, 8), (272, 5392, 16)],
    [(0, 0, 512)],
    [(0, 1288, 512)],
    [(0, 1800, 512)],
    [(0, 2312, 512)],
    [(0, 2832, 512)],
    [(0, 3344, 512)],
    [(0, 3856, 512)],
    [(0, 4368, 512)],
    [(0, 4880, 512)],
    [(0, 5408, 512)],
]

C_ID, C_MNEG, C_M01, C_ODIV, C_ONES, C_SEL = 0, 128, 256, 384, 512, 640
C_ZERO = 640 + 1024
C_HM = C_ZERO + 128
C_N = C_HM + 4


def make_consts():
    c = np.zeros((128, C_N), np.float32)
    c[:, C_ID:C_ID + 128] = np.eye(128, dtype=np.float32)
    s = np.arange(128)[:, None]
    t = np.arange(128)[None, :]
    c[:, C_MNEG:C_MNEG + 128] = np.where(t >= s, 0.0, -30000.0)
    c[:, C_M01:C_M01 + 128] = np.where(t >= s, 1.0, 0.0)
    c[:, C_ODIV:C_ODIV + 128] = 1.0 / 1024.0
    c[:, C_ONES:C_ONES + 128] = 1.0
    for h in range(8):
        c[h, C_SEL + h * 128:C_SEL + (h + 1) * 128] = 1.0
    c[0:64, C_HM] = 1.0
    c[64:128, C_HM + 1] = 1.0
    return c


P_LN1G, P_LN1B, P_LN2G, P_LN2B, P_B1 = 0, 8, 16, 24, 32
P_SCW, P_SCB, P_SNW, P_MNW = 64, 88, 94, 98
P_RCW, P_RCB, P_RBA, P_RBX, P_RLAM, P_GGB, P_GNW = 102, 118, 122, 126, 130, 134, 136
P_SCWH, P_SCBH, P_RC4, P_RC8, P_RBAH, P_RBXH, P_GGBH = 140, 164, 170, 174, 178, 182, 186
P_B2A = 188
PN = 200


class Reg:
    __slots__ = ("name", "lw", "rd")

    def __init__(self, name):
        self.name = name
        self.lw = None
        self.rd = []


class Prog:
    ENGS = ("pe", "act", "dve", "pool", "sp")

    def __init__(self, nc):
        self.nc = nc
        self.es = ExitStack()
        self.streams = {e: [] for e in self.ENGS}
        self.cnt = {}
        self.sems = {}
        self.known = {e: {} for e in self.ENGS}
        for e in self.ENGS:
            self.newsem("E_" + e)

    def newsem(self, key):
        self.sems[key] = self.es.enter_context(self.nc.semaphore(key))
        self.cnt[key] = 0
        return key

    def sb(self, name, shape, dt):
        return self.es.enter_context(self.nc.sbuf_tensor(name, list(shape), dt))

    def ps(self, name, shape, dt):
        return self.es.enter_context(self.nc.psum_tensor(name, list(shape), dt))

    def _deps(self, eng, reads, writes):
        need = {}

        def add(tok):
            if tok is None:
                return
            k, v = tok
            if need.get(k, 0) < v:
                need[k] = v
        for r in reads:
            add(r.lw)
        for w in writes:
            add(w.lw)
            for t in w.rd:
                add(t)
        st = self.streams[eng]
        kn = self.known[eng]
        own = "E_" + eng
        for k, v in need.items():
            if k == own and (eng == "pe" or not SAME_ENG_SYNC):
                continue
            if kn.get(k, 0) >= v:
                continue
            kn[k] = v
            st.append(("w", k, v))

    def _mark(self, tok, reads, writes):
        for r in reads:
            r.rd.append(tok)
            if len(r.rd) > 48:
                d = {}
                for k, v in r.rd:
                    if d.get(k, 0) < v:
                        d[k] = v
                r.rd = list(d.items())
        for w in writes:
            w.lw = tok
            w.rd = []

    def op(self, eng, fn, reads=(), writes=()):
        self._deps(eng, reads, writes)
        key = "E_" + eng
        self.cnt[key] += 1
        tok = (key, self.cnt[key])
        self.streams[eng].append(("o", fn, key, 1))
        self._mark(tok, reads, writes)
        return tok

    def dma(self, eng, fn, semkey, reads=(), writes=()):
        self._deps(eng, reads, writes)
        self.cnt[semkey] += 16
        tok = (semkey, self.cnt[semkey])
        self.streams[eng].append(("o", fn, semkey, 16))
        self._mark(tok, reads, writes)
        return tok

    def wait_all(self, eng, toks):
        st = self.streams[eng]
        kn = self.known[eng]
        for k, v in toks:
            if kn.get(k, 0) >= v:
                continue
            kn[k] = v
            st.append(("w", k, v))

    def emit(self):
        nc = self.nc
        with nc.Block() as block:
            def mk(engname):
                def body(e):
                    for it in self.streams[engname]:
                        if it[0] == "w":
                            e.wait_ge(self.sems[it[1]], it[2])
                        else:
                            it[1](e).then_inc(self.sems[it[2]], it[3])
                return body
            block.tensor(mk("pe"))
            block.scalar(mk("act"))
            block.vector(mk("dve"))
            block.gpsimd(mk("pool"))
            block.sync(mk("sp"))

    def close(self):
        self.es.close()

    def stats(self):
        return {e: (sum(1 for i in s if i[0] == "o"), sum(1 for i in s if i[0] == "w"))
                for e, s in self.streams.items()}


class TT:
    def __init__(self, P, name, shape, dt, ncell=1, psum=False):
        self.t = (P.ps if psum else P.sb)(name, shape, dt)
        self.c = [Reg("%s.%d" % (name, i)) for i in range(ncell)]

    def __getitem__(self, k):
        return self.t[k]

    def r(self, i=None):
        if i is None:
            return list(self.c)
        if isinstance(i, int):
            return [self.c[i]]
        return [self.c[j] for j in i]


def build(cfg):
    NL = cfg.get("NL", DEPTH)
    NT = cfg.get("NT", SEQ // 512)
    SAMPLE = cfg.get("SAMPLE", True)
    NTOKP = NT * 512
    DBG = cfg.get("DBG", False)

    nc = bass.Bass("TRN2", target_bir_lowering=False)
    P = Prog(nc)

    def din(name, shape, dt=F32):
        return nc.dram_tensor(name, list(shape), dt, kind="ExternalInput").ap()

    def dout(name, shape):
        return nc.dram_tensor(name, list(shape), F32, kind="ExternalOutput").ap()

    xp = din("xp", [NTOKP, D])
    xs = din("xs", [32, D])
    i_ssd_h = din("i_ssd_h", [DEPTH, 2, 8, 64, 64])
    i_ssd_conv = din("i_ssd_conv", [DEPTH, 2, 3, 768])
    i_mC = din("i_mC", [DEPTH, 2, 4, 128, 128])
    i_mn = din("i_mn", [DEPTH, 2, 4, 128])
    i_mm = din("i_mm", [DEPTH, 2, 4])
    i_rgh = din("i_rgh", [DEPTH, 2, 512])
    i_rgconv = din("i_rgconv", [DEPTH, 2, 3, 512])
    i_gla = din("i_gla", [DEPTH, 2, 4, 64, 128])
    consts = din("consts", [128, C_N])
    ln_in_g = din("ln_in_g", [D]); ln_in_b = din("ln_in_b", [D])
    w_in = din("w_in", [DEPTH, D, D_IN])
    ssd_conv_w = din("ssd_conv_w", [DEPTH, 4, 768]); ssd_conv_b = din("ssd_conv_b", [DEPTH, 768])
    ssd_dt_bias = din("ssd_dt_bias", [DEPTH, 8]); ssd_A_log = din("ssd_A_log", [DEPTH, 8]); ssd_D = din("ssd_D", [DEPTH, 8])
    ssd_norm_w = din("ssd_norm_w", [DEPTH, 512])
    mlstm_if_b = din("mlstm_if_b", [DEPTH, 8]); mlstm_norm_w = din("mlstm_norm_w", [DEPTH, 512])
    rg_conv_w = din("rg_conv_w", [DEPTH, 4, 512]); rg_conv_b = din("rg_conv_b", [DEPTH, 512])
    rg_gate_a_w = din("rg_gate_a_w", [DEPTH, 8, 64, 64]); rg_gate_a_b = din("rg_gate_a_b", [DEPTH, 512])
    rg_gate_x_w = din("rg_gate_x_w", [DEPTH, 8, 64, 64]); rg_gate_x_b = din("rg_gate_x_b", [DEPTH, 512])
    rg_lambda = din("rg_lambda", [DEPTH, 512])
    gla_gate_w2 = din("gla_gate_w2", [DEPTH, 16, 256]); gla_gate_b = din("gla_gate_b", [DEPTH, 256])
    gla_norm_w = din("gla_norm_w", [DEPTH, 512])
    w_out = din("w_out", [DEPTH, 2048, D])
    ln1_g = din("ln1_g", [DEPTH, D]); ln1_b = din("ln1_b", [DEPTH, D])
    mlp_w1 = din("mlp_w1", [DEPTH, D, 4096]); mlp_b1 = din("mlp_b1", [DEPTH, 4096])
    mlp_w2 = din("mlp_w2", [DEPTH, 4096, D]); mlp_b2 = din("mlp_b2", [DEPTH, D])
    ln2_g = din("ln2_g", [DEPTH, D]); ln2_b = din("ln2_b", [DEPTH, D])

    yp = dout("yp", [NTOKP, D])
    ys = dout("ys", [32, D])
    o_p = dict(ssd_h=dout("p_ssd_h", [DEPTH, 8, 64, 64]), ssd_conv=dout("p_ssd_conv", [DEPTH, 3, 768]),
               mC=dout("p_mC", [DEPTH, 4, 128, 128]), mn=dout("p_mn", [DEPTH, 4, 128]), mm=dout("p_mm", [DEPTH, 4]),
               rgh=dout("p_rgh", [DEPTH, 512]), rgconv=dout("p_rgconv", [DEPTH, 3, 512]), gla=dout("p_gla", [DEPTH, 4, 64, 128]))
    o_s = dict(ssd_h=dout("s_ssd_h", [DEPTH, 2, 8, 64, 64]), ssd_conv=dout("s_ssd_conv", [DEPTH, 2, 3, 768]),
               mC=dout("s_mC", [DEPTH, 2, 4, 128, 128]), mn=dout("s_mn", [DEPTH, 2, 4, 128]), mm=dout("s_mm", [DEPTH, 2, 4]),
               rgh=dout("s_rgh", [DEPTH, 2, 512]), rgconv=dout("s_rgconv", [DEPTH, 2, 3, 512]), gla=dout("s_gla", [DEPTH, 2, 4, 64, 128]))
    wbf = nc.dram_tensor("wbf", [NL, NSLAB, 128, 4096], BF16, kind="Internal").ap()
    dbg = dout("dbg_mix", [2048, 512]) if DBG else None

    def semfor(name):
        if name not in P.sems:
            P.newsem(name)
        return name

    class VW:
        def __init__(self, ap, cells_per_chunk):
            self.t = ap
            self.cc = cells_per_chunk

        def __getitem__(self, k):
            return self.t[k]

        def r(self, i=None):
            if i is None:
                out = []
                for c in self.cc:
                    out += c
                return out
            if isinstance(i, int):
                return list(self.cc[i])
            out = []
            for j in i:
                out += self.cc[j]
            return out

    CST = TT(P, "CST", [128, C_N], F32)
    IDB = TT(P, "IDB", [128, 128], BF16)
    ODIVR = TT(P, "ODIVR", [128, 128], F32R)
    KC = TT(P, "KC", [128, 4], F32)
    ONESB = TT(P, "ONESB", [128, 512], BF16)
    PRM = TT(P, "PRM", [128, NL, PN], F32)
    PS8 = TT(P, "PS8", [8, NL, 8], F32)
    DBC = TT(P, "DBC", [128, NL, 8], F32)
    RGW = TT(P, "RGW", [128, NL, 8, 128], BF16)
    GW2 = TT(P, "GW2", [16, NL, 256], BF16)
    LNIN = TT(P, "LNIN", [128, 16], F32)
    WSM = TT(P, "WSM", [128, NL, 8, 32], BF16)

    XT = TT(P, "XT", [128, 8, 512], F32, ncell=8)
    XB = TT(P, "XB", [128, 8, 512], BF16, ncell=8)
    MIX = TT(P, "MIX", [128, 16, 512], BF16, ncell=16)
    HT = TT(P, "HT", [128, 32, 512], BF16, ncell=32)
    TMPR = [TT(P, "TMPR%d" % i, [128, 512], F32R) for i in range(4)]
    LNS = TT(P, "LNS", [128, 3, 512], F32, ncell=3)
    WR = [TT(P, "WR%d" % i, [128, 4096], BF16) for i in range(NSLOT)]
    for i in range(NSLOT):
        P.newsem("W%d" % i)

    HTflat = HT.t[:].rearrange("p a b -> p (a b)")
    MIXflat = MIX.t[:].rearrange("p a b -> p (a b)")

    def aview(base, flat, byte_off, shape, dt, chunked=False):
        esz = 2 if dt == BF16 else 4
        n_el = 1
        for s_ in shape[1:]:
            n_el *= s_
        nb = n_el * esz
        v = flat[:, byte_off // 2:(byte_off + nb) // 2]
        if dt != BF16:
            v = v.bitcast(dt)
        if len(shape) == 3:
            v = v.rearrange("p (a b) -> p a b", a=shape[1])
        cells = base.c[byte_off // 1024:(byte_off + nb + 1023) // 1024]
        if chunked:
            n = shape[1]
            per = len(cells) // n
            cc = [cells[i * per:(i + 1) * per] for i in range(n)]
        else:
            cc = [cells]
        return VW(v, cc)

    FM32 = [aview(HT, HTflat, i * 2048, [128, 512], F32) for i in range(8)]
    TOK32 = [aview(HT, HTflat, 16384 + i * 2048, [128, 512], F32) for i in range(6)]
    TOK32W = aview(HT, HTflat, 16384 + 2 * 2048, [128, 4, 132], F32)
    TK = aview(HT, HTflat, 28672, [128, 8, 128], F32, chunked=False)
    STG = aview(HT, HTflat, 16384, [128, 8, 128], F32)
    STG2 = aview(HT, HTflat, 16384 + 4096, [128, 1024], F32)
    RR1 = aview(HT, HTflat, 0, [128, 8, 512], F32, chunked=True)
    RR2 = aview(MIX, MIXflat, 0, [128, 8, 512], F32, chunked=True)
    XIO = aview(MIX, MIXflat, 0, [128, 4, D], F32, chunked=True)
    HTF = HTflat.bitcast(F32)

    FMB = TT(P, "FMB", [128, 8, 512], BF16, ncell=8)
    BCM = TT(P, "BCM", [128, 4, 512], BF16, ncell=4)
    UX = TT(P, "UX", [128, 2, 520], F32, ncell=2)
    SM8 = TT(P, "SM8", [16, 4, 512], F32, ncell=4)
    TOKB = TT(P, "TOKB", [128, 4, 528], BF16, ncell=4)
    SC = [TT(P, "SC%d" % i, [128, 8, 128], BF16) for i in range(2)]
    COL = [TT(P, "COL%d" % i, [128, 64], F32) for i in range(2)]
    EMT = TT(P, "EMT", [8, 16], F32)
    MB = TT(P, "MB", [128, 8], F32)
    RGHS = TT(P, "RGHS", [128, 2, 4], F32)

    HS = [TT(P, "HS%d" % l, [128, 256], F32) for l in range(NL)]
    HSB = TT(P, "HSB", [128, 256], BF16)
    CSS = [TT(P, "CSS%d" % l, [128, 6, 3], F32) for l in range(NL)]
    CM = [TT(P, "CM%d" % l, [128, 4, 132], F32) for l in range(NL)]
    CMB = TT(P, "CMB", [128, 4, 132], BF16)
    EM = [TT(P, "EM%d" % l, [4, 2], F32) for l in range(NL)]
    RGH = [TT(P, "RGH%d" % l, [128, 4], F32) for l in range(NL)]
    CSR = [TT(P, "CSR%d" % l, [128, 4, 3], F32) for l in range(NL)]
    GS = [TT(P, "GS%d" % l, [128, 2, 128], F32) for l in range(NL)]
    GSB = TT(P, "GSB", [128, 2, 128], BF16)

    PSB = [TT(P, "PSB%d" % i, [128, 512], F32, ncell=1, psum=True) for i in range(8)]
    pstate = {"d": 0, "m": 0}

    def psD():
        b = PSB[pstate["d"] % 3]
        pstate["d"] += 1
        return b

    def psM():
        b = PSB[3 + pstate["m"] % 5]
        pstate["m"] += 1
        return b

    def qr(bank, c0, c1):
        return list(bank.c)

    def pbf(bank):
        return bank.t[:].bitcast(BF16)

    def tt(eng, out, a, b, op, R, W):
        P.op(eng, lambda e: e.tensor_tensor(out=out, in0=a, in1=b, op=op), R, W)

    def ts(eng, out, a, s1, op0, R, W, s2=None, op1=None):
        if s2 is None:
            P.op(eng, lambda e: e.tensor_scalar(out=out, in0=a, scalar1=s1, scalar2=None, op0=op0), R, W)
        else:
            P.op(eng, lambda e: e.tensor_scalar(out=out, in0=a, scalar1=s1, scalar2=s2, op0=op0, op1=op1), R, W)

    def stt(eng, out, a, s, b, op0, op1, R, W):
        P.op(eng, lambda e: e.scalar_tensor_tensor(out=out, in0=a, scalar=s, in1=b, op0=op0, op1=op1), R, W)

    def act(out, in_, func, R, W, bias=None, scale=None, accum=None):
        kw = {}
        if bias is not None:
            kw["bias"] = bias
        if scale is not None:
            kw["scale"] = scale
        if accum is not None:
            kw["accum_out"] = accum
        P.op("act", lambda e: e.activation(out=out, in_=in_, func=func, **kw), R, W)

    def sigm(out, in_, R, W, nbias=None):
        act(out, in_, AF.Exp, R, W, scale=-1.0, bias=nbias)
        act(out, out, AF.Ln, W, W, bias=1.0)
        act(out, out, AF.Exp, W, W, scale=-1.0)

    def rsq(x, R):
        act(x, x, AF.Ln, R, R)
        act(x, x, AF.Exp, R, R, scale=-0.5)

    def cp(eng, out, in_, R, W):
        if eng == "act":
            P.op("act", lambda e: e.activation(out=out, in_=in_, func=AF.Copy), R, W)
        else:
            P.op(eng, lambda e: e.tensor_copy(out=out, in_=in_), R, W)

    def mm(out, lhsT, rhs, st, sp, R, W):
        P.op("pe", lambda e: e.matmul(out, lhsT=lhsT, rhs=rhs, start=st, stop=sp), R, W)

    def tr(out, in_, idn, R, W):
        P.op("pe", lambda e: e.transpose(out, in_, idn), R, W)

    def memset(eng, ap, val, W):
        P.op(eng, lambda e: e.memset(ap, val), [], W)

    def recip(out, in_, R, W):
        P.op("dve", lambda e: e.reciprocal(out=out, in_=in_), R, W)

    def scan(out, d0, d1, init, op0, op1, R, W):
        P.op("dve", lambda e: e.tensor_tensor_scan(out=out, data0=d0, data1=d1, initial=init, op0=op0, op1=op1), R, W)

    def rmax(out, in_, R, W):
        P.op("dve", lambda e: e.reduce_max(out=out, in_=in_, axis=AX.X), R, W)

    def dma(q, out, in_, sem, R, W, slow=False):
        semfor(sem)
        if slow:
            P.dma(q, lambda e: e.dma_start(out=out, in_=in_, allow_slow_non_contiguous=True), sem, R, W)
        else:
            P.dma(q, lambda e: e.dma_start(out=out, in_=in_), sem, R, W)

    ident = CST[:, C_ID:C_ID + 128]
    mneg = CST[:, C_MNEG:C_MNEG + 128]
    m01 = CST[:, C_M01:C_M01 + 128]
    ones_f = CST[:, C_ONES:C_ONES + 128]
    zeros_f = CST[:, C_ZERO:C_ZERO + 128]
    NHALF = KC[:, 0:1]
    SIXT = KC[:, 1:2]
    PHALF = KC[:, 2:3]
    KCr = KC.r()

    dma("sp", CST[:], consts, "LDC", [], CST.r())
    cp("dve", IDB[:], ident, CST.r(), IDB.r())
    cp("act", ODIVR[:], CST[:, C_ODIV:C_ODIV + 128], CST.r(), ODIVR.r())
    memset("pool", KC[:, 0:1], -0.5, KC.r())
    memset("pool", KC[:, 1:2], 1.0 / 16.0, KC.r())
    memset("pool", KC[:, 2:3], 0.5, KC.r())
    memset("pool", ONESB[:], 1.0, ONESB.r())

    def pcol(dst_c0, src, nch, l):
        dma("pool", PRM[:, l, dst_c0:dst_c0 + nch], src.rearrange("(c p) -> p c", p=128), "PL", [], PRM.r(), slow=True)

    dma("pool", LNIN[:, 0:8], ln_in_g.rearrange("(c p) -> p c", p=128), "PL", [], PRM.r(), slow=True)
    dma("pool", LNIN[:, 8:16], ln_in_b.rearrange("(c p) -> p c", p=128), "PL", [], PRM.r(), slow=True)
    for l in range(NL):
        pcol(P_LN1G, ln1_g[l], 8, l); pcol(P_LN1B, ln1_b[l], 8, l)
        pcol(P_LN2G, ln2_g[l], 8, l); pcol(P_LN2B, ln2_b[l], 8, l)
        pcol(P_B1, mlp_b1[l], 32, l)
        pcol(P_B2A, mlp_b2[l], 8, l)
        for j in range(4):
            dma("pool", PRM[:, l, P_SCW:P_SCW + 24].rearrange("p (c j) -> p c j", j=4)[:, :, j],
                ssd_conv_w[l, j].rearrange("(c p) -> p c", p=128), "PL", [], PRM.r(), slow=True)
            dma("pool", PRM[:, l, P_RCW:P_RCW + 16].rearrange("p (c j) -> p c j", j=4)[:, :, j],
                rg_conv_w[l, j].rearrange("(c p) -> p c", p=128), "PL", [], PRM.r(), slow=True)
        pcol(P_SCB, ssd_conv_b[l], 6, l); pcol(P_SNW, ssd_norm_w[l], 4, l); pcol(P_MNW, mlstm_norm_w[l], 4, l)
        pcol(P_RCB, rg_conv_b[l], 4, l); pcol(P_RBA, rg_gate_a_b[l], 4, l); pcol(P_RBX, rg_gate_x_b[l], 4, l)
        pcol(P_RLAM, rg_lambda[l], 4, l); pcol(P_GGB, gla_gate_b[l], 2, l); pcol(P_GNW, gla_norm_w[l], 4, l)
        dma("pool", PS8[0:8, l, 0:1], ssd_dt_bias[l].rearrange("(h o) -> h o", o=1), "PL", [], PRM.r(), slow=True)
        dma("pool", PS8[0:8, l, 1:2], ssd_A_log[l].rearrange("(h o) -> h o", o=1), "PL", [], PRM.r(), slow=True)
        dma("pool", PS8[0:4, l, 2:3], mlstm_if_b[l, 0:4].rearrange("(h o) -> h o", o=1), "PL", [], PRM.r(), slow=True)
        dma("pool", PS8[0:4, l, 3:4], mlstm_if_b[l, 4:8].rearrange("(h o) -> h o", o=1), "PL", [], PRM.r(), slow=True)
        dma("pool", DBC[:, l, :], ssd_D[l].partition_broadcast(128), "PL", [], PRM.r(), slow=True)
    PR = PRM.r()
    for l in range(NL):
        ts("dve", PRM[:, l, P_B2A:P_B2A + 8], PRM[:, l, P_B2A:P_B2A + 8], 1.0 / ALPHA, ALU.mult, PR, PR)
        ts("dve", PRM[:, l, P_RBAH:P_RBAH + 4], PRM[:, l, P_RBA:P_RBA + 4], -1.0, ALU.mult, PR, PR)
        ts("dve", PRM[:, l, P_RBXH:P_RBXH + 4], PRM[:, l, P_RBX:P_RBX + 4], -1.0, ALU.mult, PR, PR)
        ts("dve", PRM[:, l, P_GGBH:P_GGBH + 2], PRM[:, l, P_GGB:P_GGB + 2], -1.0, ALU.mult, PR, PR)
        act(PRM[:, l, P_RC4:P_RC4 + 4], PRM[:, l, P_RLAM:P_RLAM + 4], AF.Exp, PR, PR, scale=-1.0)
        act(PRM[:, l, P_RC4:P_RC4 + 4], PRM[:, l, P_RC4:P_RC4 + 4], AF.Ln, PR, PR, bias=1.0)
        ts("dve", PRM[:, l, P_RC8:P_RC8 + 4], PRM[:, l, P_RC4:P_RC4 + 4], -8.0, ALU.mult, PR, PR)
        ts("dve", PRM[:, l, P_RC4:P_RC4 + 4], PRM[:, l, P_RC4:P_RC4 + 4], -16.0, ALU.mult, PR, PR)
        act(PS8[0:8, l, 1:2], PS8[0:8, l, 1:2], AF.Exp, PR, PR)
        ts("dve", PS8[0:8, l, 1:2], PS8[0:8, l, 1:2], -1.0, ALU.mult, PR, PR)
        ts("dve", PS8[0:4, l, 3:4], PS8[0:4, l, 3:4], -1.0, ALU.mult, PR, PR)
        memset("pool", STG[:], 0.0, STG.r())
        for ax, wsrc in enumerate((rg_gate_a_w, rg_gate_x_w)):
            for n in range(8):
                hh = n % 2
                dma("pool", STG[hh * 64:(hh + 1) * 64, ax * 4 + n // 2, hh * 64:(hh + 1) * 64], wsrc[l, n], "SG", [], STG.r())
        cp("dve", RGW[:, l, :, :], STG[:], STG.r(), RGW.r())
        dma("pool", STG2[0:16, 0:256], gla_gate_w2[l], "SG2", [], STG2.r())
        cp("dve", GW2[0:16, l, :], STG2[0:16, 0:256], STG2.r(), GW2.r())

    WBFR = [[Reg("wbf%d_%d" % (l, j)) for j in range(NSLAB)] for l in range(NL)]

    def slab_srcs(l, j):
        res = []
        if j < 12:
            src = w_in[l].rearrange("(k p) c -> p k c", p=128)
            for (off, c0, n) in WIN_SLABS[j]:
                res.append((8, 512, off, n, src[:, :, c0:c0 + n]))
        elif j < 16:
            jj = j - 12
            res.append((16, 256, 0, 256, w_out[l].rearrange("(k p) c -> p k c", p=128)[:, :, jj * 256:(jj + 1) * 256]))
        elif j < 24:
            jj = j - 16
            res.append((8, 512, 0, 512, mlp_w1[l].rearrange("(k p) c -> p k c", p=128)[:, :, jj * 512:(jj + 1) * 512]))
        else:
            jj = j - 24
            res.append((32, 128, 0, 128, mlp_w2[l].rearrange("(k p) c -> p k c", p=128)[:, :, jj * 128:(jj + 1) * 128]))
        return res

    n_pl = 0
    for l in range(NL):
        for j in range(NSLAB):
            half = n_pl % 2
            stg = HTF[:, half * 4096:(half + 1) * 4096]
            sreg = HT.r(list(range(half * 16, half * 16 + 16)))
            slot = WR[n_pl % NSLOT]
            if j == 1:
                memset("pool", stg, 0.0, sreg)
            for (kk, cw, off, n, src) in slab_srcs(l, j):
                dstv = stg.rearrange("p (k c) -> p k c", k=kk)[:, :, off:off + n]
                dma("sp", dstv, src, "LDS%d" % half, [], sreg)
            cp("act", slot[:, 0:2048], stg[:, 0:2048], sreg, slot.r())
            cp("dve", slot[:, 2048:4096], stg[:, 2048:4096], sreg, slot.r())
            if j == 1:
                cp("pool", WSM[:, l, :, :], slot[:, :].rearrange("p (k c) -> p k c", k=8)[:, :, 256:288], slot.r(), WSM.r())
            dma("sp", wbf[l, j], slot[:], "W%d" % (n_pl % NSLOT), slot.r(), [WBFR[l][j]])
            n_pl += 1

    slab_seq = []
    wstate = {"issued": 0, "released": 0}

    def pump():
        while wstate["issued"] < len(slab_seq) and wstate["issued"] < wstate["released"] + NSLOT:
            i = wstate["issued"]
            l, j = slab_seq[i]
            slot = WR[i % NSLOT]
            dma("sp", slot[:], wbf[l, j], "W%d" % (i % NSLOT), [WBFR[l][j]], slot.r())
            wstate["issued"] += 1

    def slab(i):
        pump()
        assert i < wstate["issued"], (i, wstate)
        assert i >= wstate["released"], (i, wstate)
        return WR[i % NSLOT]

    def release_upto(i):
        if i + 1 > wstate["released"]:
            wstate["released"] = i + 1
        pump()

    def proj_fm(wslot, c0, M, T):
        b = psD()
        wv = wslot[:, :].rearrange("p (k c) -> p k c", k=8)
        for k in range(8):
            mm(b[0:M, 0:T], wv[:, k, c0:c0 + M], XB[:, k, 0:T], k == 0, k == 7, wslot.r() + XB.r(k), qr(b, 0, T))
        return b

    def proj_wsm(l, c0, M, T):
        b = psD()
        for k in range(8):
            mm(b[0:M, 0:T], WSM[:, l, k, c0:c0 + M], XB[:, k, 0:T], k == 0, k == 7, WSM.r() + XB.r(k), qr(b, 0, T))
        return b

    def proj_tm(wslot, c0, ncols, tc0, L):
        b = psD()
        wv = wslot[:, :].rearrange("p (k c) -> p k c", k=8)
        for k in range(8):
            mm(b[0:L, 0:ncols], XB[:, k, tc0:tc0 + L], wv[:, k, c0:c0 + ncols], k == 0, k == 7,
               wslot.r() + XB.r(k), qr(b, 0, ncols))
        return b

    def layer_norm_fm(l, T, RR, gcol, bcol, last):
        bm = psM(); bq = psM()
        for k in range(8):
            t1 = TMPR[(2 * k) % 4]; t2 = TMPR[(2 * k + 1) % 4]
            cp("act", t1[:, 0:T], RR[:, k, 0:T], RR.r(k), t1.r())
            act(t2[:, 0:T], RR[:, k, 0:T], AF.Square, RR.r(k), t2.r())
            mm(bm[:, 0:T], ODIVR[:, :], t1[:, 0:T], k == 0, k == 7, ODIVR.r() + t1.r(), qr(bm, 0, T))
            mm(bq[:, 0:T], ODIVR[:, :], t2[:, 0:T], k == 0, k == 7, ODIVR.r() + t2.r(), qr(bq, 0, T))
        cp("act", LNS[:, 0, 0:T], bm[:, 0:T], qr(bm, 0, T), LNS.r(0))
        tt("pool", LNS[:, 2, 0:T], LNS[:, 0, 0:T], LNS[:, 0, 0:T], ALU.mult, LNS.r(0), LNS.r(2))
        stt("dve", LNS[:, 1, 0:T], bq[:, 0:T], EPS / (ALPHA * ALPHA), LNS[:, 2, 0:T], ALU.add, ALU.subtract,
            qr(bq, 0, T) + LNS.r(2), LNS.r(1))
        rsq(LNS[:, 1, 0:T], LNS.r(1))
        stt("dve", LNS[:, 2, 0:T], LNS[:, 0, 0:T], -1.0, LNS[:, 1, 0:T], ALU.mult, ALU.mult, LNS.r([0, 1]), LNS.r(2))
        for k in range(8):
            tt("dve", RR[:, k, 0:T], RR[:, k, 0:T], LNS[:, 1, 0:T], ALU.mult, RR.r(k) + LNS.r(1), RR.r(k))
            tt("pool" if k % 2 else "dve", RR[:, k, 0:T], RR[:, k, 0:T], LNS[:, 2, 0:T], ALU.add, RR.r(k) + LNS.r(2), RR.r(k))
            act(XT[:, k, 0:T], RR[:, k, 0:T], AF.Identity, RR.r(k) + PR, XT.r(k),
                bias=PRM[:, l, bcol + k:bcol + k + 1], scale=PRM[:, l, gcol + k:gcol + k + 1])
            if not last:
                cp("pool" if k % 2 else "dve", XB[:, k, 0:T], XT[:, k, 0:T], XT.r(k), XB.r(k))

    def load_tile_and_ln_in(src, chunks, T):
        for ci, (c0, L) in enumerate(chunks):
            dma("sp", XIO[0:L, ci, :], src[c0:c0 + L, :], "XI%d" % ci, [], XIO.r(ci))
        for ci, (c0, L) in enumerate(chunks):
            col = COL[ci % 2]
            memset("pool", col[0:L, 0:16], 0.0, col.r())
            for hh in range(2):
                act(LNS[0:L, 0, :], XIO[0:L, ci, hh * 512:(hh + 1) * 512], AF.Copy, XIO.r(ci), LNS.r(0) + col.r(), accum=col[0:L, 8 + hh:9 + hh])
                act(LNS[0:L, 1, :], XIO[0:L, ci, hh * 512:(hh + 1) * 512], AF.Square, XIO.r(ci), LNS.r(1) + col.r(), accum=col[0:L, 10 + hh:11 + hh])
            tt("dve", col[0:L, 0:1], col[0:L, 8:9], col[0:L, 9:10], ALU.add, col.r(), col.r())
            tt("dve", col[0:L, 1:2], col[0:L, 10:11], col[0:L, 11:12], ALU.add, col.r(), col.r())
            ts("dve", col[0:L, 2:3], col[0:L, 0:1], 1.0 / 1024, ALU.mult, col.r(), col.r())
            tt("dve", col[0:L, 3:4], col[0:L, 2:3], col[0:L, 2:3], ALU.mult, col.r(), col.r())
            stt("dve", col[0:L, 4:5], col[0:L, 1:2], 1.0 / 1024, col[0:L, 3:4], ALU.mult, ALU.subtract, col.r(), col.r())
            ts("dve", col[0:L, 4:5], col[0:L, 4:5], EPS, ALU.add, col.r(), col.r())
            cp("dve", col[0:L, 5:6], col[0:L, 4:5], col.r(), col.r())
            rsq(col[0:L, 5:6], col.r())
            stt("dve", col[0:L, 6:7], col[0:L, 2:3], -1.0, col[0:L, 5:6], ALU.mult, ALU.mult, col.r(), col.r())
            ts("dve", XIO[0:L, ci, :], XIO[0:L, ci, :], col[0:L, 5:6], ALU.mult, XIO.r(ci) + col.r(), XIO.r(ci),
               s2=col[0:L, 6:7], op1=ALU.add)
            for half in range(2):
                b = psM()
                for kk in range(4):
                    k = half * 4 + kk
                    tr(b[:, kk * 128:kk * 128 + L], XIO[0:L, ci, k * 128:(k + 1) * 128], ident[0:L, 0:L],
                       XIO.r(ci) + CST.r(), qr(b, kk * 128, kk * 128 + L))
                for kk in range(4):
                    k = half * 4 + kk
                    act(XT[:, k, c0:c0 + L], b[:, kk * 128:kk * 128 + L], AF.Identity, qr(b, kk * 128, kk * 128 + L) + PR,
                        XT.r(k), bias=LNIN[:, 8 + k:9 + k], scale=LNIN[:, k:k + 1])
        for k in range(8):
            cp("dve" if k % 2 else "pool", XB[:, k, 0:T], XT[:, k, 0:T], XT.r(k), XB.r(k))

    def store_tile(dst, chunks, T):
        for ci, (c0, L) in enumerate(chunks):
            for half in range(2):
                b = psM()
                for kk in range(4):
                    k = half * 4 + kk
                    tr(b[0:L, kk * 128:(kk + 1) * 128], XT[:, k, c0:c0 + L], ident, XT.r(k) + CST.r(), qr(b, kk * 128, (kk + 1) * 128))
                cp("act" if half else "dve", XIO[0:L, ci, half * 512:(half + 1) * 512], b[0:L, :], b.r(), XIO.r(ci))
            dma("sp", dst[c0:c0 + L, :], XIO[0:L, ci, :], "XO%d" % ci, XIO.r(ci), [])

    def run_tile(src, dst, chunks, segs, T, is_sample, gi, fin):
        load_tile_and_ln_in(src, chunks, T)
        for l in range(NL):
            gi = run_layer(l, chunks, segs, T, is_sample, gi, fin)
        store_tile(dst, chunks, T)
        return gi

    def run_layer(l, chunks, segs, T, is_sample, gi, fin):
        if cfg.get("STOP", 0) == 2:
            return gi + NSLAB
        def SL(j):
            return slab(gi + j)
        SL.rel = lambda j: release_upto(gi + j)
        MX = cfg.get("MIXERS", "smrg")
        for ch, fn, lastslab, j0 in (("s", mix_ssd, 2, 0), ("m", mix_mlstm, 6, 4), ("r", mix_rglru, 8, 8), ("g", mix_gla, 11, 12)):
            if ch in MX:
                fn(l, chunks, segs, T, is_sample, SL, fin)
            else:
                SL.rel(lastslab)
                for j in range(j0, j0 + 4):
                    memset("pool", MIX[:, j, 0:T], 0.0, MIX.r(j))
        if DBG and l == 0 and not is_sample:
            for j in range(16):
                jv = LNS[:, j % 3, :]
                cp("dve", jv[:, 0:T], MIX[:, j, 0:T], MIX.r(j), LNS.r(j % 3))
                dma("sp", dbg[j * 128:(j + 1) * 128, 0:T], jv[:, 0:T], "DBG%d" % (j % 3), LNS.r(j % 3), [])
        for jj in range(4):
            ws = SL(12 + jj)
            wv = ws[:, :].rearrange("p (k c) -> p k c", k=16)
            for ee in range(2):
                e = jj * 2 + ee
                b = psD()
                for k in range(16):
                    mm(b[:, 0:T], wv[:, k, ee * 128:(ee + 1) * 128], MIX[:, k, 0:T], k == 0, k == 15,
                       ws.r() + MIX.r(k), qr(b, 0, T))
                stt("dve", RR1[:, e, 0:T], b[:, 0:T], 1.0 / ALPHA, XT[:, e, 0:T], ALU.mult, ALU.add,
                    qr(b, 0, T) + XT.r(e), RR1.r(e))
            SL.rel(12 + jj)
        layer_norm_fm(l, T, RR1, P_LN1G, P_LN1B, False)
        for jj in range(8):
            ws = SL(16 + jj)
            wv = ws[:, :].rearrange("p (k c) -> p k c", k=8)
            for ff in range(4):
                f = jj * 4 + ff
                b = psD()
                for k in range(8):
                    mm(b[:, 0:T], wv[:, k, ff * 128:(ff + 1) * 128], XB[:, k, 0:T], k == 0, k == 7,
                       ws.r() + XB.r(k), qr(b, 0, T))
                tv = LNS[:, f % 3, :]
                act(tv[:, 0:T], b[:, 0:T], AF.Relu, qr(b, 0, T) + PR, LNS.r(f % 3), bias=PRM[:, l, P_B1 + f:P_B1 + f + 1])
                tt("pool" if f % 2 else "dve", HT[:, f, 0:T], tv[:, 0:T], tv[:, 0:T], ALU.mult, LNS.r(f % 3), HT.r(f))
            SL.rel(16 + jj)
        for e in range(8):
            ws = SL(24 + e)
            wv = ws[:, :].rearrange("p (k c) -> p k c", k=32)
            b = psD()
            for f in range(32):
                mm(b[:, 0:T], wv[:, f, :], HT[:, f, 0:T], f == 0, f == 31, ws.r() + HT.r(f), qr(b, 0, T))
            act(RR2[:, e, 0:T], b[:, 0:T], AF.Identity, qr(b, 0, T) + PR, RR2.r(e), bias=PRM[:, l, P_B2A + e:P_B2A + e + 1],
                scale=1.0 / ALPHA)
            tt("pool", RR2[:, e, 0:T], RR2[:, e, 0:T], XT[:, e, 0:T], ALU.add, RR2.r(e) + XT.r(e), RR2.r(e))
            SL.rel(24 + e)
        layer_norm_fm(l, T, RR2, P_LN2G, P_LN2B, l == NL - 1)
        return gi + NSLAB

    def to_mix(src_bf, L, c0, jbase, pcol0, l):
        ap, cells = src_bf
        bo = psM(); bob = pbf(bo)
        for j in range(4):
            tr(bob[:, j * 128:j * 128 + L], ap[0:L, j * 128:(j + 1) * 128], IDB[0:L, 0:L], cells + IDB.r(),
               qr(bo, j * 64, j * 64 + 64))
        for j in range(4):
            act(MIX[:, jbase + j, c0:c0 + L], bob[:, j * 128:j * 128 + L], AF.Identity, qr(bo, j * 64, j * 64 + 64) + PR,
                MIX.r(jbase + j), scale=PRM[:, l, pcol0 + j:pcol0 + j + 1])

    def conv_hist_in(kind, l, c, u, segs, is_sample):
        for si, (s0, Ls, seq) in enumerate(segs):
            base = si * (Ls + 3)
            if is_sample:
                src = (i_ssd_conv if kind == "ssd" else i_rgconv)[l, seq].rearrange("j (c p) -> p c j", p=128)[:, c, :]
                dma("pool", UX[:, u, base:base + 3], src, "UXH%d" % u, [], UX.r(u), slow=True)
            else:
                st = (CSS if kind == "ssd" else CSR)[l]
                cp("pool", UX[:, u, base:base + 3], st[:, c, :], st.r(), UX.r(u))

    def conv_hist_out(kind, l, c, u, segs, is_sample):
        for si, (s0, Ls, seq) in enumerate(segs):
            base = si * (Ls + 3)
            if is_sample:
                dst = o_s["ssd_conv" if kind == "ssd" else "rgconv"][l, seq].rearrange("j (c p) -> p c j", p=128)[:, c, :]
                dma("pool", dst, UX[:, u, base + Ls:base + Ls + 3], "UXO%d" % u, UX.r(u), [], slow=True)
            else:
                st = (CSS if kind == "ssd" else CSR)[l]
                cp("pool", st[:, c, :], UX[:, u, base + Ls:base + Ls + 3], UX.r(u), st.r())

    def ssd_load_state(l, seq):
        for h in range(8):
            g, hh = h // 4, h % 4
            dma("pool", HS[l][g * 64:(g + 1) * 64, hh * 64:(hh + 1) * 64], i_ssd_h[l, seq, h].rearrange("p n -> n p"),
                "SSI", [], HS[l].r(), slow=True)
        cp("dve", HSB[:, :], HS[l][:, :], HS[l].r(), HSB.r())

    def ssd_store_state(l, dst):
        stg = TOK32[5]
        for g in range(2):
            b = psM()
            for hh in range(4):
                tr(b[0:64, hh * 64:(hh + 1) * 64], HS[l][g * 64:(g + 1) * 64, hh * 64:(hh + 1) * 64],
                   ident[g * 64:(g + 1) * 64, g * 64:(g + 1) * 64], HS[l].r() + CST.r(), qr(b, 0, 256))
            cp("dve", stg[0:64, g * 256:(g + 1) * 256], b[0:64, 0:256], b.r(), stg.r())
        dma("sp", dst.rearrange("h p n -> p h n"), stg[0:64, :].rearrange("p (h n) -> p h n", h=8), "SSO", stg.r(), [])

    def mix_ssd(l, chunks, segs, T, is_sample, SL, fin):
        XC = FMB
        if cfg.get("SST", 99) < 99:
            for j in range(4):
                memset("pool", MIX[:, j, 0:T], 0.0, MIX.r(j))
        for c in range(6):
            u = c % 2
            ws, c0w = (SL(0), c * 128) if c < 4 else (SL(1), (c - 4) * 128)
            conv_hist_in("ssd", l, c, u, segs, is_sample)
            b = proj_fm(ws, c0w, 128, T)
            for si, (s0, Ls, seq) in enumerate(segs):
                base = si * (Ls + 3)
                cp("act", UX[:, u, base + 3:base + 3 + Ls], b[:, s0:s0 + Ls], qr(b, 0, T), UX.r(u))
            acc = FM32[c % 2]; th = FM32[2 + c % 2]
            for si, (s0, Ls, seq) in enumerate(segs):
                base = si * (Ls + 3)
                wc = P_SCW + c * 4
                ts("dve", acc[:, s0:s0 + Ls], UX[:, u, base:base + Ls], PRM[:, l, wc:wc + 1], ALU.mult, UX.r(u) + PR, acc.r(),
                   s2=PRM[:, l, P_SCB + c:P_SCB + c + 1], op1=ALU.add)
                for j in range(1, 4):
                    stt("dve", acc[:, s0:s0 + Ls], UX[:, u, base + j:base + j + Ls], PRM[:, l, wc + j:wc + j + 1],
                        acc[:, s0:s0 + Ls], ALU.mult, ALU.add, UX.r(u) + PR + acc.r(), acc.r())
            sigm(th[:, 0:T], acc[:, 0:T], acc.r(), th.r())
            tt("dve", XC[:, c, 0:T], th[:, 0:T], acc[:, 0:T], ALU.mult, th.r() + acc.r(), XC.r(c))
            conv_hist_out("ssd", l, c, u, segs, is_sample)
            if c == 3:
                SL.rel(0)
        SL.rel(1)
        for g in range(2):
            ts("dve", BCM[:, g, 0:T], XC[:, 4, 0:T], CST[:, C_HM + g:C_HM + g + 1], ALU.mult, XC.r(4) + CST.r(), BCM.r(g))
            ts("dve", BCM[:, 2 + g, 0:T], XC[:, 5, 0:T], CST[:, C_HM + g:C_HM + g + 1], ALU.mult, XC.r(5) + CST.r(), BCM.r(2 + g))
        b = proj_wsm(l, 0, 8, T)
        act(SM8[0:8, 0, 0:T], b[0:8, 0:T], AF.Exp, qr(b, 0, T) + PR, SM8.r(0), bias=PS8[0:8, l, 0:1])
        act(SM8[0:8, 0, 0:T], SM8[0:8, 0, 0:T], AF.Ln, SM8.r(0), SM8.r(0), bias=1.0)
        ts("dve", SM8[0:8, 1, 0:T], SM8[0:8, 0, 0:T], PS8[0:8, l, 1:2], ALU.mult, SM8.r(0) + PR, SM8.r(1))
        for (c0, L) in chunks:
            scan(SM8[0:8, 2, c0:c0 + L], ones_f[0:8, 0:L], SM8[0:8, 1, c0:c0 + L], 0.0, ALU.mult, ALU.add,
                 SM8.r(1) + CST.r(), SM8.r(2))
        if not is_sample:
            cp("pool", HSB[:, :], HS[l][:, :], HS[l].r(), HSB.r())
        for ci, (c0, L) in enumerate(chunks):
            p = ci % 2
            col = COL[p]; cr = col.r()
            seq = segs[ci][2] if is_sample else None
            if cfg.get("SST", 99) <= 0:
                continue
            if is_sample and cfg.get("SSL", 1):
                ssd_load_state(l, seq)
            bt = psM()
            tr(bt[0:L, 0:8], SM8[0:8, 2, c0:c0 + L], ident[0:8, 0:8], SM8.r(2) + CST.r(), qr(bt, 0, 16))
            tr(bt[0:L, 8:16], SM8[0:8, 0, c0:c0 + L], ident[0:8, 0:8], SM8.r(0) + CST.r(), qr(bt, 0, 16))
            cp("dve", col[0:L, 0:16], bt[0:L, 0:16], qr(bt, 0, 16), cr)
            if cfg.get("SST", 99) <= 1:
                continue
            bx = psM(); bxb = pbf(bx)
            for j in range(5):
                tr(bxb[0:L, j * 128:(j + 1) * 128], XC[:, j, c0:c0 + L], IDB[:, :], XC.r(j) + IDB.r(), qr(bx, j * 64, j * 64 + 64))
            xbf = TOKB[:, 0, :]; xbr = TOKB.r(0)
            btok = TOKB[:, 1, :]; btr = TOKB.r(1)
            cp("act", xbf[0:L, 0:512], bxb[0:L, 0:512], qr(bx, 0, 256), xbr)
            cp("act", btok[0:L, 0:128], bxb[0:L, 512:640], qr(bx, 256, 320), btr)
            if cfg.get("SST", 99) <= 2:
                continue
            bcb = psM()
            for g in range(2):
                mm(bcb[0:L, g * 128:g * 128 + L], BCM[:, g, c0:c0 + L], XC[:, 5, c0:c0 + L],
                   True, True, BCM.r(g) + XC.r(5), qr(bcb, g * 128, g * 128 + L))
            if cfg.get("SST", 99) <= 3:
                continue
            bb = [psM(), psM()]
            for h in range(8):
                bk = bb[h // 4]; q = h % 4
                mm(bk[:, q * 128:q * 128 + L], CST[0:8, C_SEL + h * 128:C_SEL + (h + 1) * 128], SM8[0:8, 2, c0:c0 + L],
                   True, True, CST.r() + SM8.r(2), qr(bk, q * 128, q * 128 + L))
            if cfg.get("SST", 99) <= 4:
                continue
            sc = SC[p]
            for h in range(8):
                bk = bb[h // 4]; q = h % 4
                stt("dve", TK[0:L, h, 0:L], bk[0:L, q * 128:q * 128 + L], col[0:L, h:h + 1], mneg[0:L, 0:L],
                    ALU.subtract, ALU.add, qr(bk, q * 128, q * 128 + L) + cr + CST.r(), TK.r())
            act(TK[0:L, :, 0:L], TK[0:L, :, 0:L], AF.Exp, TK.r(), TK.r())
            for h in range(8):
                g = h // 4
                stt("dve", sc[0:L, h, 0:L], bcb[0:L, g * 128:g * 128 + L], col[0:L, 8 + h:9 + h], TK[0:L, h, 0:L],
                    ALU.mult, ALU.mult, qr(bcb, g * 128, g * 128 + L) + cr + TK.r(), sc.r())
            if cfg.get("SST", 99) <= 5:
                continue
            by = psM(); byi = psM()
            for h in range(8):
                mm(by[0:L, h * 64:(h + 1) * 64], sc[0:L, h, 0:L], xbf[0:L, h * 64:(h + 1) * 64], True, True,
                   sc.r() + xbr, qr(by, h * 64, (h + 1) * 64))
            for g in range(2):
                mm(byi[0:L, g * 256:(g + 1) * 256], BCM[:, 2 + g, c0:c0 + L], HSB[:, 0:256],
                   True, True, BCM.r(2 + g) + HSB.r(), qr(byi, g * 256, (g + 1) * 256))
            if cfg.get("SST", 99) <= 6:
                continue
            act(col[0:L, 16:24], col[0:L, 0:8], AF.Exp, cr, cr)
            t0 = TOK32[0]; t1 = TOK32[1]; t2 = TOK32[2]
            tt("dve", t0[0:L, :].rearrange("p (h j) -> p h j", h=8), byi[0:L, :].rearrange("p (h j) -> p h j", h=8),
               col[0:L, 16:24].unsqueeze(2).to_broadcast([L, 8, 64]), ALU.mult, byi.r() + cr, t0.r())
            tt("dve", t0[0:L, :], t0[0:L, :], by[0:L, :], ALU.add, t0.r() + by.r(), t0.r())
            tt("pool", t1[0:L, :].rearrange("p (h j) -> p h j", h=8), xbf[0:L, 0:512].rearrange("p (h j) -> p h j", h=8),
               DBC[0:L, l, :].unsqueeze(2).to_broadcast([L, 8, 64]), ALU.mult, xbr + PR, t1.r())
            tt("pool", t0[0:L, :], t0[0:L, :], t1[0:L, :], ALU.add, t0.r() + t1.r(), t0.r())
            bz = proj_tm(SL(2), 0, 512, c0, L)
            sigm(t2[0:L, :], bz[0:L, :], bz.r(), t2.r())
            tt("dve", t2[0:L, :], t2[0:L, :], bz[0:L, :], ALU.mult, t2.r() + bz.r(), t2.r())
            tt("dve", t0[0:L, :], t0[0:L, :], t2[0:L, :], ALU.mult, t0.r() + t2.r(), t0.r())
            memset("pool", col[0:L, 40:42], 0.0, cr)
            act(t1[0:L, :], t0[0:L, :], AF.Square, t0.r(), t1.r() + cr, accum=col[0:L, 40:41])
            ts("dve", col[0:L, 41:42], col[0:L, 40:41], 1.0 / 512, ALU.mult, cr, cr, s2=EPS, op1=ALU.add)
            rsq(col[0:L, 41:42], cr)
            gn = TOKB[:, 3, :]; gnr = TOKB.r(3)
            ts("dve", gn[0:L, 0:512], t0[0:L, :], col[0:L, 41:42], ALU.mult, t0.r() + cr, gnr)
            if cfg.get("SST", 99) <= 7:
                continue
            to_mix((gn, gnr), L, c0, 0, P_SNW, l)
            if cfg.get("SST", 99) <= 8:
                continue
            for half in range(2):
                bk = bb[half]
                tt("dve", col[0:L, 24 + half * 4:28 + half * 4], bk[0:L, :].rearrange("p (h t) -> p h t", h=4)[:, :, L - 1],
                   col[0:L, half * 4:half * 4 + 4], ALU.subtract, bk.r() + cr, cr)
                act(col[:, 32 + half * 4:36 + half * 4], bk[:, :].rearrange("p (h t) -> p h t", h=4)[:, :, L - 1], AF.Exp, bk.r(), cr)
            act(col[0:L, 24:32], col[0:L, 24:32], AF.Exp, cr, cr)
            tt("dve", col[0:L, 24:32], col[0:L, 24:32], col[0:L, 8:16], ALU.mult, cr, cr)
            xw = TOKB[:, 2, :]; xwr = TOKB.r(2)
            tt("dve", xw[0:L, 0:512].rearrange("p (h j) -> p h j", h=8), xbf[0:L, 0:512].rearrange("p (h j) -> p h j", h=8),
               col[0:L, 24:32].unsqueeze(2).to_broadcast([L, 8, 64]), ALU.mult, xbr + cr, xwr)
            bd = psM()
            for g in range(2):
                mm(bd[g * 64:(g + 1) * 64, 0:256], btok[0:L, g * 64:(g + 1) * 64], xw[0:L, g * 256:(g + 1) * 256], True, True,
                   btr + xwr, qr(bd, 0, 256))
            for g in range(2):
                rows = slice(g * 64, (g + 1) * 64)
                tt("dve", HS[l][rows, :].rearrange("p (h j) -> p h j", h=4), HS[l][rows, :].rearrange("p (h j) -> p h j", h=4),
                   col[rows, 32 + g * 4:36 + g * 4].unsqueeze(2).to_broadcast([64, 4, 64]), ALU.mult, HS[l].r() + cr, HS[l].r())
                tt("dve", HS[l][rows, :], HS[l][rows, :], bd[rows, 0:256], ALU.add, HS[l].r() + qr(bd, 0, 256), HS[l].r())
            cp("pool", HSB[:, :], HS[l][:, :], HS[l].r(), HSB.r())
            if is_sample:
                ssd_store_state(l, o_s["ssd_h"][l, seq])
        SL.rel(2)
        if fin and cfg.get("SST", 99) > 9:
            ssd_store_state(l, o_p["ssd_h"][l])
            for c in range(6):
                dma("pool", o_p["ssd_conv"][l].rearrange("j (c p) -> p c j", p=128)[:, c, :], CSS[l][:, c, :], "FSO", CSS[l].r(), [], slow=True)

    def em_bcast(l, dstcol):
        ts("dve", EMT[0:4, 4:8], ident[0:4, 0:4], EM[l][0:4, 0:1], ALU.mult, CST.r() + EM[l].r(), EMT.r())
        b = psM()
        mm(b[:, 0:4], ones_f[0:4, 0:128], EMT[0:4, 4:8], True, True, CST.r() + EMT.r(), qr(b, 0, 4))
        return b

    def mlstm_load_state(l, seq):
        stg = TOK32W
        dma("sp", stg[:, :, 0:128], i_mC[l, seq].rearrange("h d v -> d h v"), "MSI", [], stg.r())
        dma("pool", stg[:, :, 128], i_mn[l, seq].rearrange("h d -> d h"), "MSIp", [], stg.r(), slow=True)
        dma("pool", MB[:, 0:4], i_mm[l, seq].partition_broadcast(128), "MSI2", [], MB.r(), slow=True)
        dma("pool", EM[l][0:4, 0:1], i_mm[l, seq].rearrange("(h o) -> h o", o=1), "MSI3", [], EM[l].r(), slow=True)
        act(MB[:, 0:4], MB[:, 0:4], AF.Exp, MB.r(), MB.r())
        act(EM[l][0:4, 0:1], EM[l][0:4, 0:1], AF.Exp, EM[l].r(), EM[l].r())
        tt("dve", CM[l][:, :, 0:129], stg[:, :, 0:129], MB[:, 0:4].unsqueeze(2).to_broadcast([128, 4, 129]), ALU.mult,
           stg.r() + MB.r(), CM[l].r())
        cp("pool", CMB[:, :, 0:129], CM[l][:, :, 0:129], CM[l].r(), CMB.r())

    def mlstm_store_state(l, dC, dn, dm):
        b = em_bcast(l, None)
        recip(MB[:, 4:8], b[:, 0:4], qr(b, 0, 4), MB.r())
        stg = TOK32W
        tt("dve", stg[:, :, 0:129], CM[l][:, :, 0:129], MB[:, 4:8].unsqueeze(2).to_broadcast([128, 4, 129]), ALU.mult,
           CM[l].r() + MB.r(), stg.r())
        dma("sp", dC.rearrange("h d v -> d h v"), stg[:, :, 0:128], "MSO", stg.r(), [])
        dma("pool", dn.rearrange("h d -> d h"), stg[:, :, 128], "MSOp", stg.r(), [], slow=True)
        act(EMT[0:4, 8:9], EM[l][0:4, 0:1], AF.Ln, EM[l].r(), EMT.r())
        dma("pool", dm.rearrange("(h o) -> h o", o=1), EMT[0:4, 8:9], "MSO2", EMT.r(), [], slow=True)

    def mix_mlstm(l, chunks, segs, T, is_sample, SL, fin):
        for h in range(4):
            b = proj_fm(SL(3), h * 128, 128, T)
            cp("act", FMB[:, h, 0:T], b[:, 0:T], qr(b, 0, T), FMB.r(h))
            b = proj_fm(SL(4), h * 128, 128, T)
            act(FMB[:, 4 + h, 0:T], b[:, 0:T], AF.Identity, qr(b, 0, T), FMB.r(4 + h), scale=float(128 ** -0.5))
        SL.rel(4)
        bi = proj_wsm(l, 8, 4, T)
        act(SM8[0:4, 0, 0:T], bi[0:4, 0:T], AF.Exp, qr(bi, 0, T) + PR, SM8.r(0), bias=PS8[0:4, l, 2:3])
        bf_ = proj_wsm(l, 12, 4, T)
        sigm(SM8[0:4, 1, 0:T], bf_[0:4, 0:T], qr(bf_, 0, T) + PR, SM8.r(1), nbias=PS8[0:4, l, 3:4])
        for (c0, L) in chunks:
            scan(SM8[0:4, 3, c0:c0 + L], SM8[0:4, 1, c0:c0 + L], zeros_f[0:4, 0:L], 1.0, ALU.mult, ALU.add,
                 SM8.r(1) + CST.r(), SM8.r(3))
        recip(SM8[0:4, 2, 0:T], SM8[0:4, 3, 0:T], SM8.r(3), SM8.r(2))
        tt("dve", SM8[0:4, 2, 0:T], SM8[0:4, 2, 0:T], SM8[0:4, 0, 0:T], ALU.mult, SM8.r([0, 2]), SM8.r(2))
        if not is_sample:
            cp("pool", CMB[:, :, 0:129], CM[l][:, :, 0:129], CM[l].r(), CMB.r())
        for ci, (c0, L) in enumerate(chunks):
            p = ci % 2
            col = COL[p]; cr = col.r()
            seq = segs[ci][2] if is_sample else None
            if is_sample:
                mlstm_load_state(l, seq)
            bt = psM()
            tr(bt[0:L, 0:4], SM8[0:4, 2, c0:c0 + L], ident[0:4, 0:4], SM8.r(2) + CST.r(), qr(bt, 0, 8))
            tr(bt[0:L, 4:8], SM8[0:4, 3, c0:c0 + L], ident[0:4, 0:4], SM8.r(3) + CST.r(), qr(bt, 0, 8))
            cp("dve", col[0:L, 0:8], bt[0:L, 0:8], qr(bt, 0, 8), cr)
            rmax(EMT[0:4, 0:1], SM8[0:4, 2, c0:c0 + L], SM8.r(2), EMT.r())
            tt("dve", EM[l][0:4, 0:1], EM[l][0:4, 0:1], EMT[0:4, 0:1], ALU.max, EM[l].r() + EMT.r(), EM[l].r())
            tt("dve", EM[l][0:4, 0:1], EM[l][0:4, 0:1], SM8[0:4, 3, c0 + L - 1:c0 + L], ALU.mult, EM[l].r() + SM8.r(3), EM[l].r())
            ts("dve", EMT[0:4, 4:8], ident[0:4, 0:4], SM8[0:4, 3, c0 + L - 1:c0 + L], ALU.mult, CST.r() + SM8.r(3), EMT.r())
            bfl = psM()
            mm(bfl[:, 0:4], ones_f[0:4, 0:128], EMT[0:4, 4:8], True, True, CST.r() + EMT.r(), qr(bfl, 0, 4))
            cp("dve", col[:, 8:12], bfl[:, 0:4], qr(bfl, 0, 4), cr)
            bv = proj_tm(SL(5), 0, 512, c0, L)
            va = TOKB[:, 1, :].rearrange("p (h v) -> p h v", h=4); var_ = TOKB.r(1)
            tt("dve", va[0:L, :, 0:128], bv[0:L, :].rearrange("p (h v) -> p h v", h=4),
               col[0:L, 0:4].unsqueeze(2).to_broadcast([L, 4, 128]), ALU.mult, bv.r() + cr, var_)
            cp("dve", va[0:L, :, 128:129], col[0:L, 0:4].unsqueeze(2), cr, var_)
            bo_ = proj_tm(SL(6), 0, 512, c0, L)
            tho = TOK32[0]
            sigm(tho[0:L, :], bo_[0:L, :], bo_.r(), tho.r())
            bk_ = psM(); bkb = pbf(bk_)
            for h in range(4):
                tr(bkb[0:L, h * 128:(h + 1) * 128], FMB[:, 4 + h, c0:c0 + L], IDB[:, :], FMB.r(4 + h) + IDB.r(), qr(bk_, h * 64, h * 64 + 64))
            ktok = TOKB[:, 0, :]; ktr = TOKB.r(0)
            cp("act", ktok[0:L, 0:512], bkb[0:L, 0:512], qr(bk_, 0, 256), ktr)
            bs = psM()
            for h in range(4):
                mm(bs[0:L, h * 128:h * 128 + L], FMB[:, 4 + h, c0:c0 + L], FMB[:, h, c0:c0 + L], True, True,
                   FMB.r([h, 4 + h]), qr(bs, h * 128, h * 128 + L))
            sc = SC[p]
            tt("dve", sc[0:L, 0:4, 0:L], bs[0:L, :].rearrange("p (h t) -> p h t", h=4)[:, :, 0:L],
               m01[0:L, 0:L].unsqueeze(1).to_broadcast([L, 4, L]), ALU.mult, bs.r() + CST.r(), sc.r())
            by = [psM(), psM()]
            for h in range(4):
                bk2 = by[h // 2]; o = (h % 2) * 132
                mm(bk2[0:L, o:o + 129], sc[0:L, h, 0:L], va[0:L, h, 0:129], True, False, sc.r() + var_, qr(bk2, o, o + 129))
                mm(bk2[0:L, o:o + 129], FMB[:, h, c0:c0 + L], CMB[:, h, 0:129], False, True, FMB.r(h) + CMB.r(), qr(bk2, o, o + 129))
            for h in range(4):
                bk2 = by[h // 2]; o = (h % 2) * 132
                cp("dve", col[0:L, 12 + h:13 + h], bk2[0:L, o + 128:o + 129], qr(bk2, o, o + 129), cr)
            tt("dve", col[0:L, 16:20], col[0:L, 12:16], col[0:L, 4:8], ALU.mult, cr, cr)
            stt("dve", col[0:L, 16:20], col[0:L, 16:20], -1.0, col[0:L, 16:20], ALU.mult, ALU.max, cr, cr)
            ts("dve", col[0:L, 16:20], col[0:L, 16:20], 1.0, ALU.max, cr, cr)
            recip(col[0:L, 16:20], col[0:L, 16:20], cr, cr)
            tt("dve", col[0:L, 16:20], col[0:L, 16:20], col[0:L, 4:8], ALU.mult, cr, cr)
            memset("pool", col[0:L, 20:24], 0.0, cr)
            junk = TOK32[1]
            for h in range(4):
                bk2 = by[h // 2]; o = (h % 2) * 132
                act(junk[0:L, h * 128:(h + 1) * 128], bk2[0:L, o:o + 128], AF.Square, qr(bk2, o, o + 128), junk.r() + cr,
                    accum=col[0:L, 20 + h:21 + h])
            tt("dve", col[0:L, 24:28], col[0:L, 16:20], col[0:L, 16:20], ALU.mult, cr, cr)
            tt("dve", col[0:L, 24:28], col[0:L, 24:28], col[0:L, 20:24], ALU.mult, cr, cr)
            ts("dve", col[0:L, 24:28], col[0:L, 24:28], 1.0 / 128, ALU.mult, cr, cr, s2=EPS, op1=ALU.add)
            rsq(col[0:L, 24:28], cr)
            tt("dve", col[0:L, 24:28], col[0:L, 24:28], col[0:L, 16:20], ALU.mult, cr, cr)
            t2 = TOK32[4]
            for h in range(4):
                bk2 = by[h // 2]; o = (h % 2) * 132
                ts("dve", t2[0:L, h * 128:(h + 1) * 128], bk2[0:L, o:o + 128], col[0:L, 24 + h:25 + h], ALU.mult,
                   qr(bk2, o, o + 128) + cr, t2.r())
            mo = TOKB[:, 2, :]; mor = TOKB.r(2)
            tt("dve", mo[0:L, 0:512], tho[0:L, :], t2[0:L, :], ALU.mult, tho.r() + t2.r(), mor)
            to_mix((mo, mor), L, c0, 4, P_MNW, l)
            bd = [psM(), psM()]
            for h in range(4):
                bk2 = bd[h // 2]; o = (h % 2) * 132
                mm(bk2[:, o:o + 129], ktok[0:L, h * 128:(h + 1) * 128], va[0:L, h, 0:129], True, True, ktr + var_, qr(bk2, o, o + 129))
            for h in range(4):
                bk2 = bd[h // 2]; o = (h % 2) * 132
                tt("dve", CM[l][:, h, 0:129], CM[l][:, h, 0:129], bk2[:, o:o + 129], ALU.add, CM[l].r() + qr(bk2, o, o + 129), CM[l].r())
                ts("dve", CM[l][:, h, 0:129], CM[l][:, h, 0:129], col[:, 8 + h:9 + h], ALU.mult, CM[l].r() + cr, CM[l].r())
            cp("pool", CMB[:, :, 0:129], CM[l][:, :, 0:129], CM[l].r(), CMB.r())
            if is_sample:
                mlstm_store_state(l, o_s["mC"][l, seq], o_s["mn"][l, seq], o_s["mm"][l, seq])
        SL.rel(6)
        if fin:
            mlstm_store_state(l, o_p["mC"][l], o_p["mn"][l], o_p["mm"][l])

    def mix_rglru(l, chunks, segs, T, is_sample, SL, fin):
        if is_sample:
            for si, (s0, Ls, seq) in enumerate(segs):
                dma("pool", RGHS[:, si, :], i_rgh[l, seq].rearrange("(c p) -> p c", p=128), "RGI", [], RGHS.r(), slow=True)
        for c in range(4):
            u = c % 2
            conv_hist_in("rg", l, c, u, segs, is_sample)
            b = proj_fm(SL(7), c * 128, 128, T)
            for si, (s0, Ls, seq) in enumerate(segs):
                base = si * (Ls + 3)
                cp("act", UX[:, u, base + 3:base + 3 + Ls], b[:, s0:s0 + Ls], qr(b, 0, T), UX.r(u))
            xr = FM32[c % 2]
            for si, (s0, Ls, seq) in enumerate(segs):
                base = si * (Ls + 3)
                wc = P_RCW + c * 4
                ts("dve", xr[:, s0:s0 + Ls], UX[:, u, base:base + Ls], PRM[:, l, wc:wc + 1], ALU.mult, UX.r(u) + PR, xr.r(),
                   s2=PRM[:, l, P_RCB + c:P_RCB + c + 1], op1=ALU.add)
                for j in range(1, 4):
                    stt("dve", xr[:, s0:s0 + Ls], UX[:, u, base + j:base + j + Ls], PRM[:, l, wc + j:wc + j + 1],
                        xr[:, s0:s0 + Ls], ALU.mult, ALU.add, UX.r(u) + PR + xr.r(), xr.r())
            conv_hist_out("rg", l, c, u, segs, is_sample)
            xrb = FMB[:, c % 2, :]; xrbr = FMB.r(c % 2)
            cp("pool", xrb[:, 0:T], xr[:, 0:T], xr.r(), xrbr)
            ba = psD()
            mm(ba[:, 0:T], RGW[:, l, c, :], xrb[:, 0:T], True, True, RGW.r() + xrbr, qr(ba, 0, T))
            tha = FM32[2 + c % 2]
            sigm(tha[:, 0:T], ba[:, 0:T], qr(ba, 0, T) + PR, tha.r(), nbias=PRM[:, l, P_RBAH + c:P_RBAH + c + 1])
            a = FM32[4 + c % 2]
            act(a[:, 0:T], tha[:, 0:T], AF.Exp, tha.r() + PR, a.r(), scale=PRM[:, l, P_RC8 + c:P_RC8 + c + 1])
            sq = LNS[:, 0, :]; sqr = LNS.r(0)
            act(sq[:, 0:T], tha[:, 0:T], AF.Exp, tha.r() + PR, sqr, scale=PRM[:, l, P_RC4 + c:P_RC4 + c + 1])
            ts("dve", sq[:, 0:T], sq[:, 0:T], -1.0, ALU.mult, sqr, sqr, s2=1.0, op1=ALU.add)
            act(sq[:, 0:T], sq[:, 0:T], AF.Ln, sqr, sqr)
            act(sq[:, 0:T], sq[:, 0:T], AF.Exp, sqr, sqr, scale=0.5)
            bx = psD()
            mm(bx[:, 0:T], RGW[:, l, 4 + c, :], xrb[:, 0:T], True, True, RGW.r() + xrbr, qr(bx, 0, T))
            thx = LNS[:, 1, :]; thxr = LNS.r(1)
            sigm(thx[:, 0:T], bx[:, 0:T], qr(bx, 0, T) + PR, thxr, nbias=PRM[:, l, P_RBXH + c:P_RBXH + c + 1])
            tt("dve", thx[:, 0:T], thx[:, 0:T], xr[:, 0:T], ALU.mult, thxr + xr.r(), thxr)
            tt("pool", thx[:, 0:T], thx[:, 0:T], sq[:, 0:T], ALU.mult, thxr + sqr, thxr)
            hr = FM32[6 + c % 2]
            for si, (s0, Ls, seq) in enumerate(segs):
                if is_sample:
                    init = RGHS[:, si, c:c + 1]; ir = RGHS.r()
                else:
                    init = RGH[l][:, c:c + 1]; ir = RGH[l].r()
                scan(hr[:, s0:s0 + Ls], a[:, s0:s0 + Ls], thx[:, s0:s0 + Ls], init, ALU.mult, ALU.add, a.r() + thxr + ir, hr.r())
                if is_sample:
                    dma("pool", o_s["rgh"][l, seq].rearrange("(c p) -> p c", p=128)[:, c:c + 1], hr[:, s0 + Ls - 1:s0 + Ls],
                        "RGO%d" % (c % 2), hr.r(), [], slow=True)
                else:
                    cp("pool", RGH[l][:, c:c + 1], hr[:, s0 + Ls - 1:s0 + Ls], hr.r(), RGH[l].r())
            by = proj_fm(SL(8), c * 128, 128, T)
            gy = LNS[:, 2, :]; gyr = LNS.r(2)
            yv = FM32[2 + c % 2]
            cp("act", yv[:, 0:T], by[:, 0:T], qr(by, 0, T), yv.r())
            act(gy[:, 0:T], by[:, 0:T], AF.Square, qr(by, 0, T), gyr)
            ts("dve", gy[:, 0:T], gy[:, 0:T], 0.044715, ALU.mult, gyr, gyr, s2=1.0, op1=ALU.add)
            tt("dve", gy[:, 0:T], gy[:, 0:T], yv[:, 0:T], ALU.mult, gyr + yv.r(), gyr)
            act(gy[:, 0:T], gy[:, 0:T], AF.Exp, gyr, gyr, scale=-1.5957691216057308)
            act(gy[:, 0:T], gy[:, 0:T], AF.Ln, gyr, gyr, bias=1.0)
            act(gy[:, 0:T], gy[:, 0:T], AF.Exp, gyr, gyr, scale=-1.0)
            tt("dve", gy[:, 0:T], gy[:, 0:T], yv[:, 0:T], ALU.mult, gyr + yv.r(), gyr)
            tt("pool", MIX[:, 8 + c, 0:T], hr[:, 0:T], gy[:, 0:T], ALU.mult, hr.r() + gyr, MIX.r(8 + c))
        SL.rel(8)
        if fin:
            dma("pool", o_p["rgh"][l].rearrange("(c p) -> p c", p=128), RGH[l][:, :], "FSO", RGH[l].r(), [], slow=True)
            for c in range(4):
                dma("pool", o_p["rgconv"][l].rearrange("j (c p) -> p c j", p=128)[:, c, :], CSR[l][:, c, :], "FSO", CSR[l].r(), [], slow=True)

    def gla_load_state(l, seq):
        stg = TOK32[4]
        dma("sp", stg[:, 0:256].rearrange("p (a v) -> p a v", a=2), i_gla[l, seq].rearrange("(a hh) d v -> (hh d) a v", hh=2),
            "GSI", [], stg.r())
        cp("dve", GS[l][:, :, :], stg[:, 0:256].rearrange("p (a v) -> p a v", a=2), stg.r(), GS[l].r())
        cp("pool", GSB[:, :, :], stg[:, 0:256].rearrange("p (a v) -> p a v", a=2), stg.r(), GSB.r())

    def gla_store_state(l, dst):
        dma("sp", dst.rearrange("(a hh) d v -> (hh d) a v", hh=2), GS[l][:, :, :], "GSO%d" % l, GS[l].r(), [])

    def mix_gla(l, chunks, segs, T, is_sample, SL, fin):
        bag = proj_wsm(l, 16, 16, T)
        agb = TOKB[:, 3, :]; agr = TOKB.r(3)
        cp("act", agb[0:16, 0:T], bag[0:16, 0:T], qr(bag, 0, T), agr)
        Ac = [FM32[0], FM32[1]]; rAc = [FM32[2], FM32[3]]
        for pc in range(2):
            bl = psD()
            mm(bl[:, 0:T], GW2[0:16, l, pc * 128:(pc + 1) * 128], agb[0:16, 0:T], True, True, GW2.r() + agr, qr(bl, 0, T))
            th = FM32[4 + pc]
            act(th[:, 0:T], bl[:, 0:T], AF.Exp, qr(bl, 0, T) + PR, th.r(), bias=PRM[:, l, P_GGBH + pc:P_GGBH + pc + 1], scale=-1.0)
            act(th[:, 0:T], th[:, 0:T], AF.Ln, th.r(), th.r(), bias=1.0)
            act(th[:, 0:T], th[:, 0:T], AF.Exp, th.r(), th.r(), scale=-1.0 / 16.0)
            for (c0, L) in chunks:
                scan(Ac[pc][:, c0:c0 + L], th[:, c0:c0 + L], zeros_f[:, 0:L], 1.0, ALU.mult, ALU.add, th.r() + CST.r(), Ac[pc].r())
            recip(rAc[pc][:, 0:T], Ac[pc][:, 0:T], Ac[pc].r(), rAc[pc].r())
            bq = proj_fm(SL(9), pc * 128, 128, T)
            stt("dve", FMB[:, pc, 0:T], bq[:, 0:T], 0.125, Ac[pc][:, 0:T], ALU.mult, ALU.mult, qr(bq, 0, T) + Ac[pc].r(), FMB.r(pc))
            bk = proj_fm(SL(9), 256 + pc * 128, 128, T)
            tt("dve", FMB[:, 2 + pc, 0:T], bk[:, 0:T], rAc[pc][:, 0:T], ALU.mult, qr(bk, 0, T) + rAc[pc].r(), FMB.r(2 + pc))
            for (c0, L) in chunks:
                stt("dve", FMB[:, 4 + pc, c0:c0 + L], bk[:, c0:c0 + L], Ac[pc][:, c0 + L - 1:c0 + L], rAc[pc][:, c0:c0 + L],
                    ALU.mult, ALU.mult, qr(bk, 0, T) + Ac[pc].r() + rAc[pc].r(), FMB.r(4 + pc))
        SL.rel(9)
        for h in range(4):
            ts("dve", BCM[:, h, 0:T], FMB[:, h // 2, 0:T], CST[:, C_HM + h % 2:C_HM + h % 2 + 1], ALU.mult,
               FMB.r(h // 2) + CST.r(), BCM.r(h))
        if not is_sample:
            cp("pool", GSB[:, :, :], GS[l][:, :, :], GS[l].r(), GSB.r())
        for ci, (c0, L) in enumerate(chunks):
            p = ci % 2
            col = COL[p]; cr = col.r()
            seq = segs[ci][2] if is_sample else None
            if is_sample:
                gla_load_state(l, seq)
            bs = psM()
            for h in range(4):
                pc = h // 2; r0 = (h % 2) * 64
                mm(bs[0:L, h * 128:h * 128 + L], FMB[:, 2 + pc, c0:c0 + L], BCM[:, h, c0:c0 + L], True, True,
                   FMB.r(2 + pc) + BCM.r(h), qr(bs, h * 128, h * 128 + L))
            sc = SC[p]
            tt("dve", sc[0:L, 0:4, 0:L], bs[0:L, :].rearrange("p (h t) -> p h t", h=4)[:, :, 0:L],
               m01[0:L, 0:L].unsqueeze(1).to_broadcast([L, 4, L]), ALU.mult, bs.r() + CST.r(), sc.r())
            bv = proj_tm(SL(10), 0, 512, c0, L)
            vbf = TOKB[:, 0, :]; vbr = TOKB.r(0)
            cp("act", vbf[0:L, 0:512], bv[0:L, :], bv.r(), vbr)
            bo = psM()
            for h in range(4):
                pc = h // 2; r0 = (h % 2) * 64
                mm(bo[0:L, h * 128:(h + 1) * 128], sc[0:L, h, 0:L], vbf[0:L, h * 128:(h + 1) * 128], True, False,
                   sc.r() + vbr, qr(bo, h * 128, (h + 1) * 128))
                mm(bo[0:L, h * 128:(h + 1) * 128], BCM[:, h, c0:c0 + L], GSB[:, pc, :], False, True,
                   BCM.r(h) + GSB.r(), qr(bo, h * 128, (h + 1) * 128))
            memset("pool", col[0:L, 0:4], 0.0, cr)
            junk = TOK32[1]
            for h in range(4):
                act(junk[0:L, h * 128:(h + 1) * 128], bo[0:L, h * 128:(h + 1) * 128], AF.Square, qr(bo, h * 128, (h + 1) * 128),
                    junk.r() + cr, accum=col[0:L, h:h + 1])
            ts("dve", col[0:L, 4:8], col[0:L, 0:4], 1.0 / 128, ALU.mult, cr, cr, s2=EPS, op1=ALU.add)
            rsq(col[0:L, 4:8], cr)
            bg = proj_tm(SL(11), 0, 512, c0, L)
            thg = TOK32[0]
            sigm(thg[0:L, :], bg[0:L, :], bg.r(), thg.r())
            tt("dve", thg[0:L, :], thg[0:L, :], bg[0:L, :], ALU.mult, thg.r() + bg.r(), thg.r())
            t2 = TOK32[5]
            tt("dve", t2[0:L, :].rearrange("p (h v) -> p h v", h=4), bo[0:L, :].rearrange("p (h v) -> p h v", h=4),
               col[0:L, 4:8].unsqueeze(2).to_broadcast([L, 4, 128]), ALU.mult, bo.r() + cr, t2.r())
            go = TOKB[:, 2, :]; gor = TOKB.r(2)
            tt("pool", go[0:L, 0:512], t2[0:L, :], thg[0:L, :], ALU.mult, t2.r() + thg.r(), gor)
            to_mix((go, gor), L, c0, 12, P_GNW, l)
            bkd = psM(); bkdb = pbf(bkd)
            for pc in range(2):
                tr(bkdb[0:L, pc * 128:(pc + 1) * 128], FMB[:, 4 + pc, c0:c0 + L], IDB[:, :], FMB.r(4 + pc) + IDB.r(), qr(bkd, pc * 64, pc * 64 + 64))
            kdt = TOKB[:, 1, :]; kdr = TOKB.r(1)
            cp("act", kdt[0:L, 0:256], bkdb[0:L, 0:256], qr(bkd, 0, 128), kdr)
            bd = psM()
            for h in range(4):
                pc = h // 2; r0 = (h % 2) * 64
                mm(bd[r0:r0 + 64, pc * 128:(pc + 1) * 128], kdt[0:L, pc * 128 + r0:pc * 128 + r0 + 64], vbf[0:L, h * 128:(h + 1) * 128],
                   True, True, kdr + vbr, qr(bd, pc * 128, (pc + 1) * 128))
            for pc in range(2):
                stt("dve", GS[l][:, pc, :], GS[l][:, pc, :], Ac[pc][:, c0 + L - 1:c0 + L], bd[:, pc * 128:(pc + 1) * 128],
                    ALU.mult, ALU.add, GS[l].r() + Ac[pc].r() + qr(bd, pc * 128, (pc + 1) * 128), GS[l].r())
            cp("pool", GSB[:, :, :], GS[l][:, :, :], GS[l].r(), GSB.r())
            if is_sample:
                gla_store_state(l, o_s["gla"][l, seq])
        SL.rel(11)
        if fin:
            gla_store_state(l, o_p["gla"][l])

    tiles = []
    if SAMPLE:
        tiles.append(("s", None))
    for t in range(NT):
        tiles.append(("p", t))
    for kind, t in tiles:
        for l in range(NL):
            for j in range(NSLAB):
                slab_seq.append((l, j))

    def zero_states():
        for l in range(NL):
            memset("pool", HS[l][:], 0.0, HS[l].r())
            memset("pool", CSS[l][:], 0.0, CSS[l].r())
            memset("pool", CM[l][:], 0.0, CM[l].r())
            memset("pool", EM[l][:], 1.0, EM[l].r())
            memset("pool", RGH[l][:], 0.0, RGH[l].r())
            memset("pool", CSR[l][:], 0.0, CSR[l].r())
            memset("pool", GS[l][:], 0.0, GS[l].r())
        memset("pool", HSB[:], 0.0, HSB.r())
        memset("pool", CMB[:], 0.0, CMB.r())
        memset("pool", GSB[:], 0.0, GSB.r())

    zero_states()
    gi = 0
    STOP = cfg.get("STOP", 0)
    if STOP == 1:
        tiles = []
    for kind, t in tiles:
        if kind == "s":
            gi = run_tile(xs, ys, [(0, 16), (16, 16)], [(0, 16, 0), (16, 16, 1)], 32, True, gi, False)
            zero_states()
        else:
            gi = run_tile(xp[t * 512:(t + 1) * 512, :], yp[t * 512:(t + 1) * 512, :],
                          [(0, 128), (128, 128), (256, 128), (384, 128)], [(0, 512, None)], 512, False, gi, t == NT - 1)

    fin_toks = [(k, v) for k, v in P.cnt.items() if not k.startswith("E_") and v > 0]
    P.wait_all("sp", fin_toks)
    P.emit()
    P.close()
    return nc, P


_CACHE = {}


def _get_program(cfg_key, cfg):
    if cfg_key not in _CACHE:
        _CACHE[cfg_key] = build(cfg)
    return _CACHE[cfg_key]


WEIGHT_NAMES = ["ln_in_g", "ln_in_b", "w_in", "ssd_conv_w", "ssd_conv_b", "ssd_dt_bias", "ssd_A_log", "ssd_D", "ssd_norm_w",
                "mlstm_if_b", "mlstm_norm_w", "rg_conv_w", "rg_conv_b", "rg_gate_a_w", "rg_gate_a_b", "rg_gate_x_w",
                "rg_gate_x_b", "rg_lambda", "gla_gate_w2", "gla_gate_b", "gla_norm_w", "w_out", "ln1_g", "ln1_b",
                "mlp_w1", "mlp_b1", "mlp_w2", "mlp_b2", "ln2_g", "ln2_b"]
STATE_IN = [("state_ssd_h", "i_ssd_h"), ("state_ssd_conv", "i_ssd_conv"), ("state_mlstm_C", "i_mC"), ("state_mlstm_n", "i_mn"),
            ("state_mlstm_m", "i_mm"), ("state_rglru_h", "i_rgh"), ("state_rglru_conv", "i_rgconv"), ("state_gla_S", "i_gla")]
OUT_KEYS = ["ssd_h", "ssd_conv", "mC", "mn", "mm", "rgh", "rgconv", "gla"]


def run(inputs, cfg=None, ncores=NCORE):
    cfg = dict(cfg or {})
    NCORE_ = ncores
    NT = cfg.get("NT", SEQ // 512)
    nc, P = _get_program(tuple(sorted(cfg.items())), cfg)
    cst = make_consts()
    in_maps = []
    for c in range(NCORE_):
        m = {"xp": np.ascontiguousarray(inputs["x_prompt"][c, :NT * 512]),
             "xs": np.ascontiguousarray(inputs["x_sample"][2 * c:2 * c + 2].reshape(32, D)),
             "consts": cst}
        for src, dst in STATE_IN:
            m[dst] = np.ascontiguousarray(inputs[src][:, 2 * c:2 * c + 2])
        for w in WEIGHT_NAMES:
            m[w] = np.ascontiguousarray(inputs[w])
        in_maps.append(m)
    res = run_bass_kernel_spmd(nc, in_maps, core_ids=list(range(NCORE_)))
    R = res.results
    if NCORE_ < NCORE:
        R = list(R) + [R[0]] * (NCORE - NCORE_)
    y_prompt = np.stack([R[c]["yp"] for c in range(NCORE)], 0)
    y_sample = np.concatenate([R[c]["ys"].reshape(2, 16, D) for c in range(NCORE)], 0)
    outs = [y_prompt, y_sample]
    for k in OUT_KEYS:
        outs.append(np.stack([R[c]["p_" + k] for c in range(NCORE)], 1))
    for k in OUT_KEYS:
        outs.append(np.concatenate([R[c]["s_" + k] for c in range(NCORE)], 1))
    return tuple(np.asarray(o, np.float32) for o in outs), res


def kernel(**inputs):
    inputs = {k: np.asarray(v) for k, v in inputs.items()}
    outs, _ = run(inputs, {})
    return outs
```

```python
import numpy as np
import concourse.bass as bass
import concourse.mybir as mybir
from concourse.bass_utils import run_bass_kernel_spmd
from contextlib import ExitStack

F32 = mybir.dt.float32
F32R = mybir.dt.float32r
BF16 = mybir.dt.bfloat16
ALU = mybir.AluOpType
AF = mybir.ActivationFunctionType
AX = mybir.AxisListType

SAME_ENG_SYNC = True

D = 1024
DEPTH = 4
SEQ = 8192
NCORE = 8
EPS = 1e-5
ALPHA = (2 * DEPTH) ** 0.25
D_IN = 5920
NSLAB = 32
NSLOT = 5

WIN_SLABS = [
    [(0, 512, 512)],
    [(0, 1024, 256), (256, 1280, 8), (264, 2824, 8), (272, 5392, 16)],
    [(0, 0, 512)],
    [(0, 3344, 512)],
    [(0, 3856, 512)],
    [(0, 1288, 512)],
    [(0, 1800, 512)],
    [(0, 4368, 512)],
    [(0, 2312, 512)],
    [(0, 2832, 512)],
    [(0, 4880, 512)],
    [(0, 5408, 512)],
]
S_XBC, S_BC, S_Z, S_XR, S_YR, S_MQ, S_MK, S_GQK, S_MV, S_MO, S_GV, S_GG = range(12)

C_ID, C_MNEG, C_M01, C_ODIV, C_ONES, C_SEL = 0, 128, 256, 384, 512, 640
C_ZERO = 640 + 1024
C_HM = C_ZERO + 128
C_N = C_HM + 4


def make_consts():
    c = np.zeros((128, C_N), np.float32)
    c[:, C_ID:C_ID + 128] = np.eye(128, dtype=np.float32)
    s = np.arange(128)[:, None]
    t = np.arange(128)[None, :]
    c[:, C_MNEG:C_MNEG + 128] = np.where(t >= s, 0.0, -30000.0)
    c[:, C_M01:C_M01 + 128] = np.where(t >= s, 1.0, 0.0)
    c[:, C_ODIV:C_ODIV + 128] = 1.0 / 1024.0
    c[:, C_ONES:C_ONES + 128] = 1.0
    for h in range(8):
        c[h, C_SEL + h * 128:C_SEL + (h + 1) * 128] = 1.0
    c[0:64, C_HM] = 1.0
    c[64:128, C_HM + 1] = 1.0
    return c


P_LN1G, P_LN1B, P_LN2G, P_LN2B, P_B1 = 0, 8, 16, 24, 32
P_SCW, P_SCB, P_SNW, P_MNW = 64, 88, 94, 98
P_RCW, P_RCB, P_RBA, P_RBX, P_RLAM, P_GGB, P_GNW = 102, 118, 122, 126, 130, 134, 136
P_SCWH, P_SCBH, P_RC4, P_RC8, P_RBAH, P_RBXH, P_GGBH = 140, 164, 170, 174, 178, 182, 186
P_B2A = 188
PN = 200


class Reg:
    __slots__ = ("name", "lw", "rd")

    def __init__(self, name):
        self.name = name
        self.lw = None
        self.rd = []


class Prog:
    ENGS = ("pe", "act", "dve", "pool", "sp")

    def __init__(self, nc):
        self.nc = nc
        self.es = ExitStack()
        self.streams = {e: [] for e in self.ENGS}
        self.cnt = {}
        self.sems = {}
        self.known = {e: {} for e in self.ENGS}
        for e in self.ENGS:
            self.newsem("E_" + e)

    def newsem(self, key):
        self.sems[key] = self.es.enter_context(self.nc.semaphore(key))
        self.cnt[key] = 0
        return key

    def sb(self, name, shape, dt):
        return self.es.enter_context(self.nc.sbuf_tensor(name, list(shape), dt))

    def ps(self, name, shape, dt):
        return self.es.enter_context(self.nc.psum_tensor(name, list(shape), dt))

    def _deps(self, eng, reads, writes):
        need = {}

        def add(tok):
            if tok is None:
                return
            k, v = tok
            if need.get(k, 0) < v:
                need[k] = v
        for r in reads:
            add(r.lw)
        for w in writes:
            add(w.lw)
            for t in w.rd:
                add(t)
        st = self.streams[eng]
        kn = self.known[eng]
        own = "E_" + eng
        for k, v in need.items():
            if k == own and (eng == "pe" or not SAME_ENG_SYNC):
                continue
            if kn.get(k, 0) >= v:
                continue
            kn[k] = v
            st.append(("w", k, v))

    def _mark(self, tok, reads, writes):
        for r in reads:
            r.rd.append(tok)
            if len(r.rd) > 48:
                d = {}
                for k, v in r.rd:
                    if d.get(k, 0) < v:
                        d[k] = v
                r.rd = list(d.items())
        for w in writes:
            w.lw = tok
            w.rd = []

    def op(self, eng, fn, reads=(), writes=()):
        self._deps(eng, reads, writes)
        key = "E_" + eng
        self.cnt[key] += 1
        tok = (key, self.cnt[key])
        self.streams[eng].append(("o", fn, key, 1))
        self._mark(tok, reads, writes)
        return tok

    def dma(self, eng, fn, semkey, reads=(), writes=()):
        self._deps(eng, reads, writes)
        self.cnt[semkey] += 16
        tok = (semkey, self.cnt[semkey])
        self.streams[eng].append(("o", fn, semkey, 16))
        self._mark(tok, reads, writes)
        return tok

    def wait_all(self, eng, toks):
        st = self.streams[eng]
        kn = self.known[eng]
        for k, v in toks:
            if kn.get(k, 0) >= v:
                continue
            kn[k] = v
            st.append(("w", k, v))

    def emit(self):
        nc = self.nc
        with nc.Block() as block:
            def mk(engname):
                def body(e):
                    for it in self.streams[engname]:
                        if it[0] == "w":
                            e.wait_ge(self.sems[it[1]], it[2])
                        else:
                            it[1](e).then_inc(self.sems[it[2]], it[3])
                return body
            block.tensor(mk("pe"))
            block.scalar(mk("act"))
            block.vector(mk("dve"))
            block.gpsimd(mk("pool"))
            block.sync(mk("sp"))

    def close(self):
        self.es.close()

    def stats(self):
        return {e: (sum(1 for i in s if i[0] == "o"), sum(1 for i in s if i[0] == "w"))
                for e, s in self.streams.items()}


class TT:
    def __init__(self, P, name, shape, dt, ncell=1, psum=False):
        self.t = (P.ps if psum else P.sb)(name, shape, dt)
        self.c = [Reg("%s.%d" % (name, i)) for i in range(ncell)]

    def __getitem__(self, k):
        return self.t[k]

    def r(self, i=None):
        if i is None:
            return list(self.c)
        if isinstance(i, int):
            return [self.c[i]]
        return [self.c[j] for j in i]


def build(cfg):
    NL = cfg.get("NL", DEPTH)
    NT = cfg.get("NT", SEQ // 512)
    SAMPLE = cfg.get("SAMPLE", True)
    NTOKP = NT * 512
    DBG = cfg.get("DBG", False)

    nc = bass.Bass("TRN2", target_bir_lowering=False)
    P = Prog(nc)

    def din(name, shape, dt=F32):
        return nc.dram_tensor(name, list(shape), dt, kind="ExternalInput").ap()

    def dout(name, shape):
        return nc.dram_tensor(name, list(shape), F32, kind="ExternalOutput").ap()

    xp = din("xp", [NTOKP, D])
    xs = din("xs", [32, D])
    i_ssd_h = din("i_ssd_h", [DEPTH, 2, 8, 64, 64])
    i_ssd_conv = din("i_ssd_conv", [DEPTH, 2, 3, 768])
    i_mC = din("i_mC", [DEPTH, 2, 4, 128, 128])
    i_mn = din("i_mn", [DEPTH, 2, 4, 128])
    i_mm = din("i_mm", [DEPTH, 2, 4])
    i_rgh = din("i_rgh", [DEPTH, 2, 512])
    i_rgconv = din("i_rgconv", [DEPTH, 2, 3, 512])
    i_gla = din("i_gla", [DEPTH, 2, 4, 64, 128])
    consts = din("consts", [128, C_N])
    ln_in_g = din("ln_in_g", [D]); ln_in_b = din("ln_in_b", [D])
    w_in = din("w_in", [DEPTH, D, D_IN])
    ssd_conv_w = din("ssd_conv_w", [DEPTH, 4, 768]); ssd_conv_b = din("ssd_conv_b", [DEPTH, 768])
    ssd_dt_bias = din("ssd_dt_bias", [DEPTH, 8]); ssd_A_log = din("ssd_A_log", [DEPTH, 8]); ssd_D = din("ssd_D", [DEPTH, 8])
    ssd_norm_w = din("ssd_norm_w", [DEPTH, 512])
    mlstm_if_b = din("mlstm_if_b", [DEPTH, 8]); mlstm_norm_w = din("mlstm_norm_w", [DEPTH, 512])
    rg_conv_w = din("rg_conv_w", [DEPTH, 4, 512]); rg_conv_b = din("rg_conv_b", [DEPTH, 512])
    rg_gate_a_w = din("rg_gate_a_w", [DEPTH, 8, 64, 64]); rg_gate_a_b = din("rg_gate_a_b", [DEPTH, 512])
    rg_gate_x_w = din("rg_gate_x_w", [DEPTH, 8, 64, 64]); rg_gate_x_b = din("rg_gate_x_b", [DEPTH, 512])
    rg_lambda = din("rg_lambda", [DEPTH, 512])
    gla_gate_w2 = din("gla_gate_w2", [DEPTH, 16, 256]); gla_gate_b = din("gla_gate_b", [DEPTH, 256])
    gla_norm_w = din("gla_norm_w", [DEPTH, 512])
    w_out = din("w_out", [DEPTH, 2048, D])
    ln1_g = din("ln1_g", [DEPTH, D]); ln1_b = din("ln1_b", [DEPTH, D])
    mlp_w1 = din("mlp_w1", [DEPTH, D, 4096]); mlp_b1 = din("mlp_b1", [DEPTH, 4096])
    mlp_w2 = din("mlp_w2", [DEPTH, 4096, D]); mlp_b2 = din("mlp_b2", [DEPTH, D])
    ln2_g = din("ln2_g", [DEPTH, D]); ln2_b = din("ln2_b", [DEPTH, D])

    yp = dout("yp", [NTOKP, D])
    ys = dout("ys", [32, D])
    o_p = dict(ssd_h=dout("p_ssd_h", [DEPTH, 8, 64, 64]), ssd_conv=dout("p_ssd_conv", [DEPTH, 3, 768]),
               mC=dout("p_mC", [DEPTH, 4, 128, 128]), mn=dout("p_mn", [DEPTH, 4, 128]), mm=dout("p_mm", [DEPTH, 4]),
               rgh=dout("p_rgh", [DEPTH, 512]), rgconv=dout("p_rgconv", [DEPTH, 3, 512]), gla=dout("p_gla", [DEPTH, 4, 64, 128]))
    o_s = dict(ssd_h=dout("s_ssd_h", [DEPTH, 2, 8, 64, 64]), ssd_conv=dout("s_ssd_conv", [DEPTH, 2, 3, 768]),
               mC=dout("s_mC", [DEPTH, 2, 4, 128, 128]), mn=dout("s_mn", [DEPTH, 2, 4, 128]), mm=dout("s_mm", [DEPTH, 2, 4]),
               rgh=dout("s_rgh", [DEPTH, 2, 512]), rgconv=dout("s_rgconv", [DEPTH, 2, 3, 512]), gla=dout("s_gla", [DEPTH, 2, 4, 64, 128]))
    wbf = nc.dram_tensor("wbf", [NL, NSLAB, 128, 4096], BF16, kind="Internal").ap()
    dbg = dout("dbg_mix", [2048, 512]) if DBG else None

    def semfor(name):
        if name not in P.sems:
            P.newsem(name)
        return name

    class VW:
        def __init__(self, ap, cells_per_chunk):
            self.t = ap
            self.cc = cells_per_chunk

        def __getitem__(self, k):
            return self.t[k]

        def r(self, i=None):
            if i is None:
                out = []
                for c in self.cc:
                    out += c
                return out
            if isinstance(i, int):
                return list(self.cc[i])
            out = []
            for j in i:
                out += self.cc[j]
            return out

    CST = TT(P, "CST", [128, C_N], F32)
    IDB = TT(P, "IDB", [128, 128], BF16)
    ODIVR = TT(P, "ODIVR", [128, 128], F32R)
    KC = TT(P, "KC", [128, 4], F32)
    ONESB = TT(P, "ONESB", [128, 512], BF16)
    PRM = TT(P, "PRM", [128, NL, PN], F32)
    PS8 = TT(P, "PS8", [8, NL, 8], F32)
    DBC = TT(P, "DBC", [128, NL, 8], F32)
    RGW = TT(P, "RGW", [128, NL, 8, 128], BF16)
    GW2 = TT(P, "GW2", [16, NL, 256], BF16)
    LNIN = TT(P, "LNIN", [128, 16], F32)
    WSM = TT(P, "WSM", [128, NL, 8, 32], BF16)

    XT = TT(P, "XT", [128, 8, 512], F32, ncell=8)
    XB = TT(P, "XB", [128, 8, 512], BF16, ncell=8)
    MIX = TT(P, "MIX", [128, 16, 512], BF16, ncell=16)
    HT = TT(P, "HT", [128, 32, 512], BF16, ncell=32)
    TMPR = [TT(P, "TMPR%d" % i, [128, 512], F32R) for i in range(4)]
    LNS = TT(P, "LNS", [128, 3, 512], F32, ncell=3)
    WR = [TT(P, "WR%d" % i, [128, 4096], BF16) for i in range(NSLOT)]
    for i in range(NSLOT):
        P.newsem("W%d" % i)

    HTflat = HT.t[:].rearrange("p a b -> p (a b)")
    MIXflat = MIX.t[:].rearrange("p a b -> p (a b)")

    def aview(base, flat, byte_off, shape, dt, chunked=False):
        esz = 2 if dt == BF16 else 4
        n_el = 1
        for s_ in shape[1:]:
            n_el *= s_
        nb = n_el * esz
        v = flat[:, byte_off // 2:(byte_off + nb) // 2]
        if dt != BF16:
            v = v.bitcast(dt)
        if len(shape) == 3:
            v = v.rearrange("p (a b) -> p a b", a=shape[1])
        cells = base.c[byte_off // 1024:(byte_off + nb + 1023) // 1024]
        if chunked:
            n = shape[1]
            per = len(cells) // n
            cc = [cells[i * per:(i + 1) * per] for i in range(n)]
        else:
            cc = [cells]
        return VW(v, cc)

    FM32 = [aview(HT, HTflat, i * 2048, [128, 512], F32) for i in range(8)]
    TOK32 = [aview(HT, HTflat, 16384 + i * 2048, [128, 512], F32) for i in range(6)]
    TOK32W = aview(HT, HTflat, 16384 + 2 * 2048, [128, 4, 132], F32)
    TK = aview(HT, HTflat, 28672, [128, 8, 128], F32, chunked=False)
    STG = aview(HT, HTflat, 16384, [128, 8, 128], F32)
    STG2 = aview(HT, HTflat, 16384 + 4096, [128, 1024], F32)
    RR1 = aview(HT, HTflat, 0, [128, 8, 512], F32, chunked=True)
    RR2 = aview(MIX, MIXflat, 0, [128, 8, 512], F32, chunked=True)
    XIO = aview(MIX, MIXflat, 0, [128, 4, D], F32, chunked=True)
    HTF = HTflat.bitcast(F32)
    FM16 = [aview(HT, HTflat, i * 2048, [128, 1024], BF16) for i in range(8)]
    TMPB = [VW(LNS.t[:, i, :].bitcast(BF16), [LNS.r(i)]) for i in range(3)]

    FMB = TT(P, "FMB", [128, 8, 512], BF16, ncell=8)
    BCM = TT(P, "BCM", [128, 4, 512], BF16, ncell=4)
    UX = TT(P, "UX", [128, 2, 520], F32, ncell=2)
    SM8 = TT(P, "SM8", [16, 4, 512], F32, ncell=4)
    TOKB = TT(P, "TOKB", [128, 4, 528], BF16, ncell=4)
    SC = [TT(P, "SC%d" % i, [128, 8, 128], BF16) for i in range(2)]
    COL = [TT(P, "COL%d" % i, [128, 64], F32) for i in range(2)]
    SCG = [TT(P, "SCG%d" % i, [128, 4, 128], BF16) for i in range(2)]
    COLG = [TT(P, "COLG%d" % i, [128, 16], F32) for i in range(2)]
    GAL = TT(P, "GAL", [128, 2, 4], F32)
    EMT = TT(P, "EMT", [8, 16], F32)
    MB = TT(P, "MB", [128, 8], F32)
    RGHS = TT(P, "RGHS", [128, 2, 4], F32)

    HS = [TT(P, "HS%d" % l, [128, 256], F32) for l in range(NL)]
    HSB = TT(P, "HSB", [128, 256], BF16)
    CSS = [TT(P, "CSS%d" % l, [128, 6, 3], F32) for l in range(NL)]
    CM = [TT(P, "CM%d" % l, [128, 4, 132], F32) for l in range(NL)]
    CMB = TT(P, "CMB", [128, 4, 132], BF16)
    EM = [TT(P, "EM%d" % l, [4, 2], F32) for l in range(NL)]
    RGH = [TT(P, "RGH%d" % l, [128, 4], F32) for l in range(NL)]
    CSR = [TT(P, "CSR%d" % l, [128, 4, 3], F32) for l in range(NL)]
    GS = [TT(P, "GS%d" % l, [128, 2, 128], F32) for l in range(NL)]
    GSB = TT(P, "GSB", [128, 2, 128], BF16)

    PSB = [TT(P, "PSB%d" % i, [128, 512], F32, ncell=1, psum=True) for i in range(8)]
    pstate = {"d": 0, "m": 0}

    def psD():
        b = PSB[pstate["d"] % 3]
        pstate["d"] += 1
        return b

    def psM():
        b = PSB[3 + pstate["m"] % 5]
        pstate["m"] += 1
        return b

    def mkrot(idx):
        st = {"i": 0}

        def f():
            bk = PSB[idx[st["i"] % len(idx)]]
            st["i"] += 1
            return bk
        return f

    def qr(bank, c0, c1):
        return list(bank.c)

    def pbf(bank):
        return bank.t[:].bitcast(BF16)

    def tt(eng, out, a, b, op, R, W):
        P.op(eng, lambda e: e.tensor_tensor(out=out, in0=a, in1=b, op=op), R, W)

    def ts(eng, out, a, s1, op0, R, W, s2=None, op1=None):
        if s2 is None:
            P.op(eng, lambda e: e.tensor_scalar(out=out, in0=a, scalar1=s1, scalar2=None, op0=op0), R, W)
        else:
            P.op(eng, lambda e: e.tensor_scalar(out=out, in0=a, scalar1=s1, scalar2=s2, op0=op0, op1=op1), R, W)

    def stt(eng, out, a, s, b, op0, op1, R, W):
        P.op(eng, lambda e: e.scalar_tensor_tensor(out=out, in0=a, scalar=s, in1=b, op0=op0, op1=op1), R, W)

    def act(out, in_, func, R, W, bias=None, scale=None, accum=None):
        kw = {}
        if bias is not None:
            kw["bias"] = bias
        if scale is not None:
            kw["scale"] = scale
        if accum is not None:
            kw["accum_out"] = accum
        P.op("act", lambda e: e.activation(out=out, in_=in_, func=func, **kw), R, W)

    def sigm(out, in_, R, W, nbias=None):
        act(out, in_, AF.Exp, R, W, scale=-1.0, bias=nbias)
        act(out, out, AF.Ln, W, W, bias=1.0)
        act(out, out, AF.Exp, W, W, scale=-1.0)

    def rsq(x, R):
        act(x, x, AF.Ln, R, R)
        act(x, x, AF.Exp, R, R, scale=-0.5)

    def cp(eng, out, in_, R, W):
        if eng == "act":
            P.op("act", lambda e: e.activation(out=out, in_=in_, func=AF.Copy), R, W)
        else:
            P.op(eng, lambda e: e.tensor_copy(out=out, in_=in_), R, W)

    def mm(out, lhsT, rhs, st, sp, R, W):
        P.op("pe", lambda e: e.matmul(out, lhsT=lhsT, rhs=rhs, start=st, stop=sp), R, W)

    def tr(out, in_, idn, R, W):
        P.op("pe", lambda e: e.transpose(out, in_, idn), R, W)

    def memset(eng, ap, val, W):
        P.op(eng, lambda e: e.memset(ap, val), [], W)

    def recip(out, in_, R, W):
        P.op("dve", lambda e: e.reciprocal(out=out, in_=in_), R, W)

    def scan(out, d0, d1, init, op0, op1, R, W):
        P.op("dve", lambda e: e.tensor_tensor_scan(out=out, data0=d0, data1=d1, initial=init, op0=op0, op1=op1), R, W)

    def rmax(out, in_, R, W):
        P.op("dve", lambda e: e.reduce_max(out=out, in_=in_, axis=AX.X), R, W)

    def dma(q, out, in_, sem, R, W, slow=False):
        semfor(sem)
        if slow:
            P.dma(q, lambda e: e.dma_start(out=out, in_=in_, allow_slow_non_contiguous=True), sem, R, W)
        else:
            P.dma(q, lambda e: e.dma_start(out=out, in_=in_), sem, R, W)

    ident = CST[:, C_ID:C_ID + 128]
    mneg = CST[:, C_MNEG:C_MNEG + 128]
    m01 = CST[:, C_M01:C_M01 + 128]
    ones_f = CST[:, C_ONES:C_ONES + 128]
    zeros_f = CST[:, C_ZERO:C_ZERO + 128]
    NHALF = KC[:, 0:1]
    SIXT = KC[:, 1:2]
    PHALF = KC[:, 2:3]
    KCr = KC.r()

    dma("sp", CST[:], consts, "LDC", [], CST.r())
    cp("dve", IDB[:], ident, CST.r(), IDB.r())
    cp("act", ODIVR[:], CST[:, C_ODIV:C_ODIV + 128], CST.r(), ODIVR.r())
    memset("pool", KC[:, 0:1], -0.5, KC.r())
    memset("pool", KC[:, 1:2], 1.0 / 16.0, KC.r())
    memset("pool", KC[:, 2:3], 0.5, KC.r())
    memset("pool", ONESB[:], 1.0, ONESB.r())

    def pcol(dst_c0, src, nch, l):
        dma("pool", PRM[:, l, dst_c0:dst_c0 + nch], src.rearrange("(c p) -> p c", p=128), "PL", [], PRM.r(), slow=True)

    dma("pool", LNIN[:, 0:8], ln_in_g.rearrange("(c p) -> p c", p=128), "PL", [], PRM.r(), slow=True)
    dma("pool", LNIN[:, 8:16], ln_in_b.rearrange("(c p) -> p c", p=128), "PL", [], PRM.r(), slow=True)
    for l in range(NL):
        pcol(P_LN1G, ln1_g[l], 8, l); pcol(P_LN1B, ln1_b[l], 8, l)
        pcol(P_LN2G, ln2_g[l], 8, l); pcol(P_LN2B, ln2_b[l], 8, l)
        pcol(P_B1, mlp_b1[l], 32, l)
        pcol(P_B2A, mlp_b2[l], 8, l)
        for j in range(4):
            dma("pool", PRM[:, l, P_SCW:P_SCW + 24].rearrange("p (c j) -> p c j", j=4)[:, :, j],
                ssd_conv_w[l, j].rearrange("(c p) -> p c", p=128), "PL", [], PRM.r(), slow=True)
            dma("pool", PRM[:, l, P_RCW:P_RCW + 16].rearrange("p (c j) -> p c j", j=4)[:, :, j],
                rg_conv_w[l, j].rearrange("(c p) -> p c", p=128), "PL", [], PRM.r(), slow=True)
        pcol(P_SCB, ssd_conv_b[l], 6, l); pcol(P_SNW, ssd_norm_w[l], 4, l); pcol(P_MNW, mlstm_norm_w[l], 4, l)
        pcol(P_RCB, rg_conv_b[l], 4, l); pcol(P_RBA, rg_gate_a_b[l], 4, l); pcol(P_RBX, rg_gate_x_b[l], 4, l)
        pcol(P_RLAM, rg_lambda[l], 4, l); pcol(P_GGB, gla_gate_b[l], 2, l); pcol(P_GNW, gla_norm_w[l], 4, l)
        dma("pool", PS8[0:8, l, 0:1], ssd_dt_bias[l].rearrange("(h o) -> h o", o=1), "PL", [], PRM.r(), slow=True)
        dma("pool", PS8[0:8, l, 1:2], ssd_A_log[l].rearrange("(h o) -> h o", o=1), "PL", [], PRM.r(), slow=True)
        dma("pool", PS8[0:4, l, 2:3], mlstm_if_b[l, 0:4].rearrange("(h o) -> h o", o=1), "PL", [], PRM.r(), slow=True)
        dma("pool", PS8[0:4, l, 3:4], mlstm_if_b[l, 4:8].rearrange("(h o) -> h o", o=1), "PL", [], PRM.r(), slow=True)
        dma("pool", DBC[:, l, :], ssd_D[l].partition_broadcast(128), "PL", [], PRM.r(), slow=True)
    PR = PRM.r()
    for l in range(NL):
        ts("dve", PRM[:, l, P_B2A:P_B2A + 8], PRM[:, l, P_B2A:P_B2A + 8], 1.0 / ALPHA, ALU.mult, PR, PR)
        ts("dve", PRM[:, l, P_RBAH:P_RBAH + 4], PRM[:, l, P_RBA:P_RBA + 4], -1.0, ALU.mult, PR, PR)
        ts("dve", PRM[:, l, P_RBXH:P_RBXH + 4], PRM[:, l, P_RBX:P_RBX + 4], -1.0, ALU.mult, PR, PR)
        ts("dve", PRM[:, l, P_GGBH:P_GGBH + 2], PRM[:, l, P_GGB:P_GGB + 2], -1.0, ALU.mult, PR, PR)
        act(PRM[:, l, P_RC4:P_RC4 + 4], PRM[:, l, P_RLAM:P_RLAM + 4], AF.Exp, PR, PR, scale=-1.0)
        act(PRM[:, l, P_RC4:P_RC4 + 4], PRM[:, l, P_RC4:P_RC4 + 4], AF.Ln, PR, PR, bias=1.0)
        ts("dve", PRM[:, l, P_RC8:P_RC8 + 4], PRM[:, l, P_RC4:P_RC4 + 4], -8.0, ALU.mult, PR, PR)
        ts("dve", PRM[:, l, P_RC4:P_RC4 + 4], PRM[:, l, P_RC4:P_RC4 + 4], -16.0, ALU.mult, PR, PR)
        act(PS8[0:8, l, 1:2], PS8[0:8, l, 1:2], AF.Exp, PR, PR)
        ts("dve", PS8[0:8, l, 1:2], PS8[0:8, l, 1:2], -1.0, ALU.mult, PR, PR)
        ts("dve", PS8[0:4, l, 3:4], PS8[0:4, l, 3:4], -1.0, ALU.mult, PR, PR)
        memset("pool", STG[:], 0.0, STG.r())
        for ax, wsrc in enumerate((rg_gate_a_w, rg_gate_x_w)):
            for n in range(8):
                hh = n % 2
                dma("pool", STG[hh * 64:(hh + 1) * 64, ax * 4 + n // 2, hh * 64:(hh + 1) * 64], wsrc[l, n], "SG", [], STG.r())
        cp("dve", RGW[:, l, :, :], STG[:], STG.r(), RGW.r())
        dma("pool", STG2[0:16, 0:256], gla_gate_w2[l], "SG2", [], STG2.r())
        cp("dve", GW2[0:16, l, :], STG2[0:16, 0:256], STG2.r(), GW2.r())

    WBFR = [[Reg("wbf%d_%d" % (l, j)) for j in range(NSLAB)] for l in range(NL)]

    def slab_srcs(l, j):
        res = []
        if j < 12:
            src = w_in[l].rearrange("(k p) c -> p k c", p=128)
            for (off, c0, n) in WIN_SLABS[j]:
                res.append((8, 512, off, n, src[:, :, c0:c0 + n]))
        elif j < 16:
            jj = j - 12
            res.append((16, 256, 0, 256, w_out[l].rearrange("(k p) c -> p k c", p=128)[:, :, jj * 256:(jj + 1) * 256]))
        elif j < 24:
            jj = j - 16
            res.append((8, 512, 0, 512, mlp_w1[l].rearrange("(k p) c -> p k c", p=128)[:, :, jj * 512:(jj + 1) * 512]))
        else:
            jj = j - 24
            res.append((32, 128, 0, 128, mlp_w2[l].rearrange("(k p) c -> p k c", p=128)[:, :, jj * 128:(jj + 1) * 128]))
        return res

    n_pl = 0
    for l in range(NL):
        for j in range(NSLAB):
            half = n_pl % 2
            stg = HTF[:, half * 4096:(half + 1) * 4096]
            sreg = HT.r(list(range(half * 16, half * 16 + 16)))
            slot = WR[n_pl % NSLOT]
            if j == 1:
                memset("pool", stg, 0.0, sreg)
            for (kk, cw, off, n, src) in slab_srcs(l, j):
                dstv = stg.rearrange("p (k c) -> p k c", k=kk)[:, :, off:off + n]
                dma("sp", dstv, src, "LDS%d" % half, [], sreg)
            cp("act", slot[:, 0:2048], stg[:, 0:2048], sreg, slot.r())
            cp("dve", slot[:, 2048:4096], stg[:, 2048:4096], sreg, slot.r())
            if j == 1:
                cp("pool", WSM[:, l, :, :], slot[:, :].rearrange("p (k c) -> p k c", k=8)[:, :, 256:288], slot.r(), WSM.r())
            dma("sp", wbf[l, j], slot[:], "W%d" % (n_pl % NSLOT), slot.r(), [WBFR[l][j]])
            n_pl += 1

    slab_seq = []
    wstate = {"issued": 0, "released": 0}

    def pump():
        while wstate["issued"] < len(slab_seq) and wstate["issued"] < wstate["released"] + NSLOT:
            i = wstate["issued"]
            l, j = slab_seq[i]
            slot = WR[i % NSLOT]
            dma("sp", slot[:], wbf[l, j], "W%d" % (i % NSLOT), [WBFR[l][j]], slot.r())
            wstate["issued"] += 1

    def slab(i):
        pump()
        assert i < wstate["issued"], (i, wstate)
        assert i >= wstate["released"], (i, wstate)
        return WR[i % NSLOT]

    def release_upto(i):
        if i + 1 > wstate["released"]:
            wstate["released"] = i + 1
        pump()

    def proj_fm(wslot, c0, M, T, bank=None):
        b = bank or psD()
        wv = wslot[:, :].rearrange("p (k c) -> p k c", k=8)
        for k in range(8):
            mm(b[0:M, 0:T], wv[:, k, c0:c0 + M], XB[:, k, 0:T], k == 0, k == 7, wslot.r() + XB.r(k), qr(b, 0, T))
        return b

    def proj_wsm(l, c0, M, T):
        b = psD()
        for k in range(8):
            mm(b[0:M, 0:T], WSM[:, l, k, c0:c0 + M], XB[:, k, 0:T], k == 0, k == 7, WSM.r() + XB.r(k), qr(b, 0, T))
        return b

    def proj_tm(wslot, c0, ncols, tc0, L, bank=None):
        b = bank or psD()
        wv = wslot[:, :].rearrange("p (k c) -> p k c", k=8)
        for k in range(8):
            mm(b[0:L, 0:ncols], XB[:, k, tc0:tc0 + L], wv[:, k, c0:c0 + ncols], k == 0, k == 7,
               wslot.r() + XB.r(k), qr(b, 0, ncols))
        return b

    def layer_norm_fm(l, T, RR, gcol, bcol, last):
        bm = psM(); bq = psM()
        for k in range(8):
            t1 = TMPR[(2 * k) % 4]; t2 = TMPR[(2 * k + 1) % 4]
            cp("act", t1[:, 0:T], RR[:, k, 0:T], RR.r(k), t1.r())
            act(t2[:, 0:T], RR[:, k, 0:T], AF.Square, RR.r(k), t2.r())
            mm(bm[:, 0:T], ODIVR[:, :], t1[:, 0:T], k == 0, k == 7, ODIVR.r() + t1.r(), qr(bm, 0, T))
            mm(bq[:, 0:T], ODIVR[:, :], t2[:, 0:T], k == 0, k == 7, ODIVR.r() + t2.r(), qr(bq, 0, T))
        cp("act", LNS[:, 0, 0:T], bm[:, 0:T], qr(bm, 0, T), LNS.r(0))
        tt("pool", LNS[:, 2, 0:T], LNS[:, 0, 0:T], LNS[:, 0, 0:T], ALU.mult, LNS.r(0), LNS.r(2))
        stt("dve", LNS[:, 1, 0:T], bq[:, 0:T], EPS / (ALPHA * ALPHA), LNS[:, 2, 0:T], ALU.add, ALU.subtract,
            qr(bq, 0, T) + LNS.r(2), LNS.r(1))
        rsq(LNS[:, 1, 0:T], LNS.r(1))
        stt("dve", LNS[:, 2, 0:T], LNS[:, 0, 0:T], -1.0, LNS[:, 1, 0:T], ALU.mult, ALU.mult, LNS.r([0, 1]), LNS.r(2))
        for k in range(8):
            tt("dve", RR[:, k, 0:T], RR[:, k, 0:T], LNS[:, 1, 0:T], ALU.mult, RR.r(k) + LNS.r(1), RR.r(k))
            tt("pool" if k % 2 else "dve", RR[:, k, 0:T], RR[:, k, 0:T], LNS[:, 2, 0:T], ALU.add, RR.r(k) + LNS.r(2), RR.r(k))
            act(XT[:, k, 0:T], RR[:, k, 0:T], AF.Identity, RR.r(k) + PR, XT.r(k),
                bias=PRM[:, l, bcol + k:bcol + k + 1], scale=PRM[:, l, gcol + k:gcol + k + 1])
            if not last:
                cp("pool" if k % 2 else "dve", XB[:, k, 0:T], XT[:, k, 0:T], XT.r(k), XB.r(k))

    def load_tile_and_ln_in(src, chunks, T):
        for ci, (c0, L) in enumerate(chunks):
            dma("sp", XIO[0:L, ci, :], src[c0:c0 + L, :], "XI%d" % ci, [], XIO.r(ci))
        for ci, (c0, L) in enumerate(chunks):
            col = COL[ci % 2]
            memset("pool", col[0:L, 0:16], 0.0, col.r())
            for hh in range(2):
                act(LNS[0:L, 0, :], XIO[0:L, ci, hh * 512:(hh + 1) * 512], AF.Copy, XIO.r(ci), LNS.r(0) + col.r(), accum=col[0:L, 8 + hh:9 + hh])
                act(LNS[0:L, 1, :], XIO[0:L, ci, hh * 512:(hh + 1) * 512], AF.Square, XIO.r(ci), LNS.r(1) + col.r(), accum=col[0:L, 10 + hh:11 + hh])
            tt("dve", col[0:L, 0:1], col[0:L, 8:9], col[0:L, 9:10], ALU.add, col.r(), col.r())
            tt("dve", col[0:L, 1:2], col[0:L, 10:11], col[0:L, 11:12], ALU.add, col.r(), col.r())
            ts("dve", col[0:L, 2:3], col[0:L, 0:1], 1.0 / 1024, ALU.mult, col.r(), col.r())
            tt("dve", col[0:L, 3:4], col[0:L, 2:3], col[0:L, 2:3], ALU.mult, col.r(), col.r())
            stt("dve", col[0:L, 4:5], col[0:L, 1:2], 1.0 / 1024, col[0:L, 3:4], ALU.mult, ALU.subtract, col.r(), col.r())
            ts("dve", col[0:L, 4:5], col[0:L, 4:5], EPS, ALU.add, col.r(), col.r())
            cp("dve", col[0:L, 5:6], col[0:L, 4:5], col.r(), col.r())
            rsq(col[0:L, 5:6], col.r())
            stt("dve", col[0:L, 6:7], col[0:L, 2:3], -1.0, col[0:L, 5:6], ALU.mult, ALU.mult, col.r(), col.r())
            ts("dve", XIO[0:L, ci, :], XIO[0:L, ci, :], col[0:L, 5:6], ALU.mult, XIO.r(ci) + col.r(), XIO.r(ci),
               s2=col[0:L, 6:7], op1=ALU.add)
            for half in range(2):
                b = psM()
                for kk in range(4):
                    k = half * 4 + kk
                    tr(b[:, kk * 128:kk * 128 + L], XIO[0:L, ci, k * 128:(k + 1) * 128], ident[0:L, 0:L],
                       XIO.r(ci) + CST.r(), qr(b, kk * 128, kk * 128 + L))
                for kk in range(4):
                    k = half * 4 + kk
                    act(XT[:, k, c0:c0 + L], b[:, kk * 128:kk * 128 + L], AF.Identity, qr(b, kk * 128, kk * 128 + L) + PR,
                        XT.r(k), bias=LNIN[:, 8 + k:9 + k], scale=LNIN[:, k:k + 1])
        for k in range(8):
            cp("dve" if k % 2 else "pool", XB[:, k, 0:T], XT[:, k, 0:T], XT.r(k), XB.r(k))

    def store_tile(dst, chunks, T):
        for ci, (c0, L) in enumerate(chunks):
            for half in range(2):
                b = psM()
                for kk in range(4):
                    k = half * 4 + kk
                    tr(b[0:L, kk * 128:(kk + 1) * 128], XT[:, k, c0:c0 + L], ident, XT.r(k) + CST.r(), qr(b, kk * 128, (kk + 1) * 128))
                cp("act" if half else "dve", XIO[0:L, ci, half * 512:(half + 1) * 512], b[0:L, :], b.r(), XIO.r(ci))
            dma("sp", dst[c0:c0 + L, :], XIO[0:L, ci, :], "XO%d" % ci, XIO.r(ci), [])

    def run_tile(src, dst, chunks, segs, T, is_sample, gi, fin):
        load_tile_and_ln_in(src, chunks, T)
        for l in range(NL):
            gi = run_layer(l, chunks, segs, T, is_sample, gi, fin)
        store_tile(dst, chunks, T)
        return gi

    def run_layer(l, chunks, segs, T, is_sample, gi, fin):
        if cfg.get("STOP", 0) == 2:
            return gi + NSLAB
        def SL(j):
            return slab(gi + j)
        SL.rel = lambda j: release_upto(gi + j)
        def interleave(gens):
            gens = list(gens)
            while gens:
                for g in list(gens):
                    try:
                        next(g)
                    except StopIteration:
                        gens.remove(g)
        ssd_fm(l, chunks, segs, T, is_sample, SL)
        interleave([ssd_loop(l, chunks, segs, T, is_sample, SL, fin), rg_gen(l, chunks, segs, T, is_sample, SL, fin)])
        SL.rel(S_YR)
        mlstm_fm(l, chunks, segs, T, is_sample, SL)
        gla_fm(l, chunks, segs, T, is_sample, SL)
        interleave([mlstm_loop(l, chunks, segs, T, is_sample, SL, fin), gla_loop(l, chunks, segs, T, is_sample, SL, fin)])
        SL.rel(S_GG)
        if DBG and l == 0 and not is_sample:
            for j in range(16):
                jv = LNS[:, j % 3, :]
                cp("dve", jv[:, 0:T], MIX[:, j, 0:T], MIX.r(j), LNS.r(j % 3))
                dma("sp", dbg[j * 128:(j + 1) * 128, 0:T], jv[:, 0:T], "DBG%d" % (j % 3), LNS.r(j % 3), [])
        for jj in range(4):
            ws = SL(12 + jj)
            wv = ws[:, :].rearrange("p (k c) -> p k c", k=16)
            for ee in range(2):
                e = jj * 2 + ee
                b = psD()
                for k in range(16):
                    mm(b[:, 0:T], wv[:, k, ee * 128:(ee + 1) * 128], MIX[:, k, 0:T], k == 0, k == 15,
                       ws.r() + MIX.r(k), qr(b, 0, T))
                stt("dve", RR1[:, e, 0:T], b[:, 0:T], 1.0 / ALPHA, XT[:, e, 0:T], ALU.mult, ALU.add,
                    qr(b, 0, T) + XT.r(e), RR1.r(e))
            SL.rel(12 + jj)
        layer_norm_fm(l, T, RR1, P_LN1G, P_LN1B, False)
        for jj in range(8):
            ws = SL(16 + jj)
            wv = ws[:, :].rearrange("p (k c) -> p k c", k=8)
            for ff in range(4):
                f = jj * 4 + ff
                b = psD()
                for k in range(8):
                    mm(b[:, 0:T], wv[:, k, ff * 128:(ff + 1) * 128], XB[:, k, 0:T], k == 0, k == 7,
                       ws.r() + XB.r(k), qr(b, 0, T))
                tv = LNS[:, f % 3, :]
                act(tv[:, 0:T], b[:, 0:T], AF.Relu, qr(b, 0, T) + PR, LNS.r(f % 3), bias=PRM[:, l, P_B1 + f:P_B1 + f + 1])
                tt("pool" if f % 2 else "dve", HT[:, f, 0:T], tv[:, 0:T], tv[:, 0:T], ALU.mult, LNS.r(f % 3), HT.r(f))
            SL.rel(16 + jj)
        for e in range(8):
            ws = SL(24 + e)
            wv = ws[:, :].rearrange("p (k c) -> p k c", k=32)
            b = psD()
            for f in range(32):
                mm(b[:, 0:T], wv[:, f, :], HT[:, f, 0:T], f == 0, f == 31, ws.r() + HT.r(f), qr(b, 0, T))
            act(RR2[:, e, 0:T], b[:, 0:T], AF.Identity, qr(b, 0, T) + PR, RR2.r(e), bias=PRM[:, l, P_B2A + e:P_B2A + e + 1],
                scale=1.0 / ALPHA)
            tt("pool", RR2[:, e, 0:T], RR2[:, e, 0:T], XT[:, e, 0:T], ALU.add, RR2.r(e) + XT.r(e), RR2.r(e))
            SL.rel(24 + e)
        layer_norm_fm(l, T, RR2, P_LN2G, P_LN2B, l == NL - 1)
        return gi + NSLAB

    def to_mix(src_bf, L, c0, jbase, pcol0, l, bank=None):
        ap, cells = src_bf
        bo = bank or psM(); bob = pbf(bo)
        for j in range(4):
            tr(bob[:, j * 128:j * 128 + L], ap[0:L, j * 128:(j + 1) * 128], IDB[0:L, 0:L], cells + IDB.r(),
               qr(bo, j * 64, j * 64 + 64))
        for j in range(4):
            act(MIX[:, jbase + j, c0:c0 + L], bob[:, j * 128:j * 128 + L], AF.Identity, qr(bo, j * 64, j * 64 + 64) + PR,
                MIX.r(jbase + j), scale=PRM[:, l, pcol0 + j:pcol0 + j + 1])

    def conv_hist_in(kind, l, c, u, segs, is_sample):
        for si, (s0, Ls, seq) in enumerate(segs):
            base = si * (Ls + 3)
            if is_sample:
                src = (i_ssd_conv if kind == "ssd" else i_rgconv)[l, seq].rearrange("j (c p) -> p c j", p=128)[:, c, :]
                dma("pool", UX[:, u, base:base + 3], src, "UXH%d" % u, [], UX.r(u), slow=True)
            else:
                st = (CSS if kind == "ssd" else CSR)[l]
                cp("pool", UX[:, u, base:base + 3], st[:, c, :], st.r(), UX.r(u))

    def conv_hist_out(kind, l, c, u, segs, is_sample):
        for si, (s0, Ls, seq) in enumerate(segs):
            base = si * (Ls + 3)
            if is_sample:
                dst = o_s["ssd_conv" if kind == "ssd" else "rgconv"][l, seq].rearrange("j (c p) -> p c j", p=128)[:, c, :]
                dma("pool", dst, UX[:, u, base + Ls:base + Ls + 3], "UXO%d" % u, UX.r(u), [], slow=True)
            else:
                st = (CSS if kind == "ssd" else CSR)[l]
                cp("pool", st[:, c, :], UX[:, u, base + Ls:base + Ls + 3], UX.r(u), st.r())

    def ssd_load_state(l, seq):
        for h in range(8):
            g, hh = h // 4, h % 4
            dma("pool", HS[l][g * 64:(g + 1) * 64, hh * 64:(hh + 1) * 64], i_ssd_h[l, seq, h].rearrange("p n -> n p"),
                "SSI", [], HS[l].r(), slow=True)
        cp("dve", HSB[:, :], HS[l][:, :], HS[l].r(), HSB.r())

    def ssd_store_state(l, dst, pm=None):
        pm = pm or psM
        stg = TOK32[5]
        for g in range(2):
            b = pm()
            for hh in range(4):
                tr(b[0:64, hh * 64:(hh + 1) * 64], HS[l][g * 64:(g + 1) * 64, hh * 64:(hh + 1) * 64],
                   ident[g * 64:(g + 1) * 64, g * 64:(g + 1) * 64], HS[l].r() + CST.r(), qr(b, 0, 256))
            cp("dve", stg[0:64, g * 256:(g + 1) * 256], b[0:64, 0:256], b.r(), stg.r())
        dma("sp", dst.rearrange("h p n -> p h n"), stg[0:64, :].rearrange("p (h n) -> p h n", h=8), "SSO", stg.r(), [])

    def ssd_fm(l, chunks, segs, T, is_sample, SL):
        XC = FMB
        for c in range(6):
            u = c % 2
            ws, c0w = (SL(S_XBC), c * 128) if c < 4 else (SL(S_BC), (c - 4) * 128)
            conv_hist_in("ssd", l, c, u, segs, is_sample)
            b = proj_fm(ws, c0w, 128, T)
            for si, (s0, Ls, seq) in enumerate(segs):
                base = si * (Ls + 3)
                cp("act", UX[:, u, base + 3:base + 3 + Ls], b[:, s0:s0 + Ls], qr(b, 0, T), UX.r(u))
            acc = FM32[c % 2]; th = FM32[2 + c % 2]
            for si, (s0, Ls, seq) in enumerate(segs):
                base = si * (Ls + 3)
                wc = P_SCW + c * 4
                ts("dve", acc[:, s0:s0 + Ls], UX[:, u, base:base + Ls], PRM[:, l, wc:wc + 1], ALU.mult, UX.r(u) + PR, acc.r(),
                   s2=PRM[:, l, P_SCB + c:P_SCB + c + 1], op1=ALU.add)
                for j in range(1, 4):
                    stt("dve", acc[:, s0:s0 + Ls], UX[:, u, base + j:base + j + Ls], PRM[:, l, wc + j:wc + j + 1],
                        acc[:, s0:s0 + Ls], ALU.mult, ALU.add, UX.r(u) + PR + acc.r(), acc.r())
            sigm(th[:, 0:T], acc[:, 0:T], acc.r(), th.r())
            tt("dve", XC[:, c, 0:T], th[:, 0:T], acc[:, 0:T], ALU.mult, th.r() + acc.r(), XC.r(c))
            conv_hist_out("ssd", l, c, u, segs, is_sample)
            if c == 3:
                SL.rel(S_XBC)
        SL.rel(S_BC)
        for g in range(2):
            ts("dve", BCM[:, g, 0:T], XC[:, 4, 0:T], CST[:, C_HM + g:C_HM + g + 1], ALU.mult, XC.r(4) + CST.r(), BCM.r(g))
            ts("dve", BCM[:, 2 + g, 0:T], XC[:, 5, 0:T], CST[:, C_HM + g:C_HM + g + 1], ALU.mult, XC.r(5) + CST.r(), BCM.r(2 + g))
        b = proj_wsm(l, 0, 8, T)
        act(SM8[0:8, 0, 0:T], b[0:8, 0:T], AF.Exp, qr(b, 0, T) + PR, SM8.r(0), bias=PS8[0:8, l, 0:1])
        act(SM8[0:8, 0, 0:T], SM8[0:8, 0, 0:T], AF.Ln, SM8.r(0), SM8.r(0), bias=1.0)
        ts("dve", SM8[0:8, 1, 0:T], SM8[0:8, 0, 0:T], PS8[0:8, l, 1:2], ALU.mult, SM8.r(0) + PR, SM8.r(1))
        for (c0, L) in chunks:
            scan(SM8[0:8, 2, c0:c0 + L], ones_f[0:8, 0:L], SM8[0:8, 1, c0:c0 + L], 0.0, ALU.mult, ALU.add,
                 SM8.r(1) + CST.r(), SM8.r(2))

    def ssd_loop(l, chunks, segs, T, is_sample, SL, fin):
        XC = FMB
        pm = mkrot([3, 4, 5, 6, 7]); pd = mkrot([0])
        if not is_sample:
            cp("dve", HSB[:, :], HS[l][:, :], HS[l].r(), HSB.r())
        for ci, (c0, L) in enumerate(chunks):
            p = ci % 2
            col = COL[p]; cr = col.r()
            seq = segs[ci][2] if is_sample else None
            if is_sample:
                ssd_load_state(l, seq)
            bz = proj_tm(SL(S_Z), 0, 512, c0, L, bank=pd())
            yield
            bt = pm()
            tr(bt[0:L, 0:8], SM8[0:8, 2, c0:c0 + L], ident[0:8, 0:8], SM8.r(2) + CST.r(), qr(bt, 0, 16))
            tr(bt[0:L, 8:16], SM8[0:8, 0, c0:c0 + L], ident[0:8, 0:8], SM8.r(0) + CST.r(), qr(bt, 0, 16))
            cp("dve", col[0:L, 0:16], bt[0:L, 0:16], qr(bt, 0, 16), cr)
            bx = pm(); bxb = pbf(bx)
            for j in range(5):
                tr(bxb[0:L, j * 128:(j + 1) * 128], XC[:, j, c0:c0 + L], IDB[:, :], XC.r(j) + IDB.r(), qr(bx, j * 64, j * 64 + 64))
            xbf = TOKB[:, 0, :]; xbr = TOKB.r(0)
            btok = TOKB[:, 1, :]; btr = TOKB.r(1)
            cp("act", xbf[0:L, 0:512], bxb[0:L, 0:512], qr(bx, 0, 256), xbr)
            cp("act", btok[0:L, 0:128], bxb[0:L, 512:640], qr(bx, 256, 320), btr)
            yield
            bcb = pm()
            for g in range(2):
                mm(bcb[0:L, g * 128:g * 128 + L], BCM[:, g, c0:c0 + L], XC[:, 5, c0:c0 + L],
                   True, True, BCM.r(g) + XC.r(5), qr(bcb, g * 128, g * 128 + L))
            bb = [pm(), pm()]
            for h in range(8):
                bk = bb[h // 4]; q = h % 4
                mm(bk[:, q * 128:q * 128 + L], CST[0:8, C_SEL + h * 128:C_SEL + (h + 1) * 128], SM8[0:8, 2, c0:c0 + L],
                   True, True, CST.r() + SM8.r(2), qr(bk, q * 128, q * 128 + L))
            yield
            sc = SC[p]
            for h in range(8):
                bk = bb[h // 4]; q = h % 4
                stt("dve", TK[0:L, h, 0:L], bk[0:L, q * 128:q * 128 + L], col[0:L, h:h + 1], mneg[0:L, 0:L],
                    ALU.subtract, ALU.add, qr(bk, q * 128, q * 128 + L) + cr + CST.r(), TK.r())
            yield
            act(TK[0:L, :, 0:L], TK[0:L, :, 0:L], AF.Exp, TK.r(), TK.r())
            yield
            for h in range(8):
                g = h // 4
                stt("dve", sc[0:L, h, 0:L], bcb[0:L, g * 128:g * 128 + L], col[0:L, 8 + h:9 + h], TK[0:L, h, 0:L],
                    ALU.mult, ALU.mult, qr(bcb, g * 128, g * 128 + L) + cr + TK.r(), sc.r())
            yield
            by = pm(); byi = pm()
            for h in range(8):
                mm(by[0:L, h * 64:(h + 1) * 64], sc[0:L, h, 0:L], xbf[0:L, h * 64:(h + 1) * 64], True, True,
                   sc.r() + xbr, qr(by, h * 64, (h + 1) * 64))
            for g in range(2):
                mm(byi[0:L, g * 256:(g + 1) * 256], BCM[:, 2 + g, c0:c0 + L], HSB[:, 0:256],
                   True, True, BCM.r(2 + g) + HSB.r(), qr(byi, g * 256, (g + 1) * 256))
            yield
            act(col[0:L, 16:24], col[0:L, 0:8], AF.Exp, cr, cr)
            t0 = TOK32[0]; t1 = TOK32[1]; t2 = TOK32[2]
            tt("dve", t0[0:L, :].rearrange("p (h j) -> p h j", h=8), byi[0:L, :].rearrange("p (h j) -> p h j", h=8),
               col[0:L, 16:24].unsqueeze(2).to_broadcast([L, 8, 64]), ALU.mult, byi.r() + cr, t0.r())
            tt("dve", t0[0:L, :], t0[0:L, :], by[0:L, :], ALU.add, t0.r() + by.r(), t0.r())
            tt("pool", t1[0:L, :].rearrange("p (h j) -> p h j", h=8), xbf[0:L, 0:512].rearrange("p (h j) -> p h j", h=8),
               DBC[0:L, l, :].unsqueeze(2).to_broadcast([L, 8, 64]), ALU.mult, xbr + PR, t1.r())
            tt("pool", t0[0:L, :], t0[0:L, :], t1[0:L, :], ALU.add, t0.r() + t1.r(), t0.r())
            yield
            sigm(t2[0:L, :], bz[0:L, :], bz.r(), t2.r())
            tt("dve", t2[0:L, :], t2[0:L, :], bz[0:L, :], ALU.mult, t2.r() + bz.r(), t2.r())
            tt("dve", t0[0:L, :], t0[0:L, :], t2[0:L, :], ALU.mult, t0.r() + t2.r(), t0.r())
            yield
            memset("pool", col[0:L, 40:42], 0.0, cr)
            act(t1[0:L, :], t0[0:L, :], AF.Square, t0.r(), t1.r() + cr, accum=col[0:L, 40:41])
            ts("dve", col[0:L, 41:42], col[0:L, 40:41], 1.0 / 512, ALU.mult, cr, cr, s2=EPS, op1=ALU.add)
            rsq(col[0:L, 41:42], cr)
            gn = TOKB[:, 3, :]; gnr = TOKB.r(3)
            ts("dve", gn[0:L, 0:512], t0[0:L, :], col[0:L, 41:42], ALU.mult, t0.r() + cr, gnr)
            yield
            to_mix((gn, gnr), L, c0, 0, P_SNW, l, bank=pm())
            yield
            for half in range(2):
                bk = bb[half]
                tt("dve", col[0:L, 24 + half * 4:28 + half * 4], bk[0:L, :].rearrange("p (h t) -> p h t", h=4)[:, :, L - 1],
                   col[0:L, half * 4:half * 4 + 4], ALU.subtract, bk.r() + cr, cr)
                act(col[:, 32 + half * 4:36 + half * 4], bk[:, :].rearrange("p (h t) -> p h t", h=4)[:, :, L - 1], AF.Exp, bk.r(), cr)
            act(col[0:L, 24:32], col[0:L, 24:32], AF.Exp, cr, cr)
            tt("dve", col[0:L, 24:32], col[0:L, 24:32], col[0:L, 8:16], ALU.mult, cr, cr)
            xw = TOKB[:, 2, :]; xwr = TOKB.r(2)
            tt("dve", xw[0:L, 0:512].rearrange("p (h j) -> p h j", h=8), xbf[0:L, 0:512].rearrange("p (h j) -> p h j", h=8),
               col[0:L, 24:32].unsqueeze(2).to_broadcast([L, 8, 64]), ALU.mult, xbr + cr, xwr)
            yield
            bd = pm()
            for g in range(2):
                mm(bd[g * 64:(g + 1) * 64, 0:256], btok[0:L, g * 64:(g + 1) * 64], xw[0:L, g * 256:(g + 1) * 256], True, True,
                   btr + xwr, qr(bd, 0, 256))
            for g in range(2):
                rows = slice(g * 64, (g + 1) * 64)
                tt("dve", HS[l][rows, :].rearrange("p (h j) -> p h j", h=4), HS[l][rows, :].rearrange("p (h j) -> p h j", h=4),
                   col[rows, 32 + g * 4:36 + g * 4].unsqueeze(2).to_broadcast([64, 4, 64]), ALU.mult, HS[l].r() + cr, HS[l].r())
                tt("dve", HS[l][rows, :], HS[l][rows, :], bd[rows, 0:256], ALU.add, HS[l].r() + qr(bd, 0, 256), HS[l].r())
            yield
            cp("dve", HSB[:, :], HS[l][:, :], HS[l].r(), HSB.r())
            if is_sample:
                ssd_store_state(l, o_s["ssd_h"][l, seq], pm)
            yield
        if fin:
            ssd_store_state(l, o_p["ssd_h"][l], pm)
            for c in range(6):
                dma("pool", o_p["ssd_conv"][l].rearrange("j (c p) -> p c j", p=128)[:, c, :], CSS[l][:, c, :], "FSO", CSS[l].r(), [], slow=True)

    def em_bcast(l, dstcol, pm=None):
        ts("dve", EMT[0:4, 4:8], ident[0:4, 0:4], EM[l][0:4, 0:1], ALU.mult, CST.r() + EM[l].r(), EMT.r())
        b = (pm or psM)()
        mm(b[:, 0:4], ones_f[0:4, 0:128], EMT[0:4, 4:8], True, True, CST.r() + EMT.r(), qr(b, 0, 4))
        return b

    def mlstm_load_state(l, seq):
        stg = TOK32W
        dma("sp", stg[:, :, 0:128], i_mC[l, seq].rearrange("h d v -> d h v"), "MSI", [], stg.r())
        dma("pool", stg[:, :, 128], i_mn[l, seq].rearrange("h d -> d h"), "MSIp", [], stg.r(), slow=True)
        dma("pool", MB[:, 0:4], i_mm[l, seq].partition_broadcast(128), "MSI2", [], MB.r(), slow=True)
        dma("pool", EM[l][0:4, 0:1], i_mm[l, seq].rearrange("(h o) -> h o", o=1), "MSI3", [], EM[l].r(), slow=True)
        act(MB[:, 0:4], MB[:, 0:4], AF.Exp, MB.r(), MB.r())
        act(EM[l][0:4, 0:1], EM[l][0:4, 0:1], AF.Exp, EM[l].r(), EM[l].r())
        tt("dve", CM[l][:, :, 0:129], stg[:, :, 0:129], MB[:, 0:4].unsqueeze(2).to_broadcast([128, 4, 129]), ALU.mult,
           stg.r() + MB.r(), CM[l].r())
        cp("pool", CMB[:, :, 0:129], CM[l][:, :, 0:129], CM[l].r(), CMB.r())

    def mlstm_store_state(l, dC, dn, dm, pm=None):
        b = em_bcast(l, None, pm)
        recip(MB[:, 4:8], b[:, 0:4], qr(b, 0, 4), MB.r())
        stg = TOK32W
        tt("dve", stg[:, :, 0:129], CM[l][:, :, 0:129], MB[:, 4:8].unsqueeze(2).to_broadcast([128, 4, 129]), ALU.mult,
           CM[l].r() + MB.r(), stg.r())
        dma("sp", dC.rearrange("h d v -> d h v"), stg[:, :, 0:128], "MSO", stg.r(), [])
        dma("pool", dn.rearrange("h d -> d h"), stg[:, :, 128], "MSOp", stg.r(), [], slow=True)
        act(EMT[0:4, 8:9], EM[l][0:4, 0:1], AF.Ln, EM[l].r(), EMT.r())
        dma("pool", dm.rearrange("(h o) -> h o", o=1), EMT[0:4, 8:9], "MSO2", EMT.r(), [], slow=True)

    def mlstm_fm(l, chunks, segs, T, is_sample, SL):
        for h in range(4):
            b = proj_fm(SL(S_MQ), h * 128, 128, T)
            cp("act", FMB[:, h, 0:T], b[:, 0:T], qr(b, 0, T), FMB.r(h))
        SL.rel(S_MQ)
        for h in range(4):
            b = proj_fm(SL(S_MK), h * 128, 128, T)
            act(FMB[:, 4 + h, 0:T], b[:, 0:T], AF.Identity, qr(b, 0, T), FMB.r(4 + h), scale=float(128 ** -0.5))
        SL.rel(S_MK)
        bi = proj_wsm(l, 8, 4, T)
        act(SM8[0:4, 0, 0:T], bi[0:4, 0:T], AF.Exp, qr(bi, 0, T) + PR, SM8.r(0), bias=PS8[0:4, l, 2:3])
        bf_ = proj_wsm(l, 12, 4, T)
        sigm(SM8[0:4, 1, 0:T], bf_[0:4, 0:T], qr(bf_, 0, T) + PR, SM8.r(1), nbias=PS8[0:4, l, 3:4])
        for (c0, L) in chunks:
            scan(SM8[0:4, 3, c0:c0 + L], SM8[0:4, 1, c0:c0 + L], zeros_f[0:4, 0:L], 1.0, ALU.mult, ALU.add,
                 SM8.r(1) + CST.r(), SM8.r(3))
        recip(SM8[0:4, 2, 0:T], SM8[0:4, 3, 0:T], SM8.r(3), SM8.r(2))
        tt("dve", SM8[0:4, 2, 0:T], SM8[0:4, 2, 0:T], SM8[0:4, 0, 0:T], ALU.mult, SM8.r([0, 2]), SM8.r(2))

    def mlstm_loop(l, chunks, segs, T, is_sample, SL, fin):
        pm = mkrot([3, 4, 5, 6]); pd = mkrot([0])
        if not is_sample:
            cp("dve", CMB[:, :, 0:129], CM[l][:, :, 0:129], CM[l].r(), CMB.r())
        for ci, (c0, L) in enumerate(chunks):
            p = ci % 2
            col = COL[p]; cr = col.r()
            seq = segs[ci][2] if is_sample else None
            if is_sample:
                mlstm_load_state(l, seq)
            yield
            bt = pm()
            tr(bt[0:L, 0:4], SM8[0:4, 2, c0:c0 + L], ident[0:4, 0:4], SM8.r(2) + CST.r(), qr(bt, 0, 8))
            tr(bt[0:L, 4:8], SM8[0:4, 3, c0:c0 + L], ident[0:4, 0:4], SM8.r(3) + CST.r(), qr(bt, 0, 8))
            cp("dve", col[0:L, 0:8], bt[0:L, 0:8], qr(bt, 0, 8), cr)
            rmax(EMT[0:4, 0:1], SM8[0:4, 2, c0:c0 + L], SM8.r(2), EMT.r())
            tt("dve", EM[l][0:4, 0:1], EM[l][0:4, 0:1], EMT[0:4, 0:1], ALU.max, EM[l].r() + EMT.r(), EM[l].r())
            tt("dve", EM[l][0:4, 0:1], EM[l][0:4, 0:1], SM8[0:4, 3, c0 + L - 1:c0 + L], ALU.mult, EM[l].r() + SM8.r(3), EM[l].r())
            ts("dve", EMT[0:4, 4:8], ident[0:4, 0:4], SM8[0:4, 3, c0 + L - 1:c0 + L], ALU.mult, CST.r() + SM8.r(3), EMT.r())
            bfl = pm()
            mm(bfl[:, 0:4], ones_f[0:4, 0:128], EMT[0:4, 4:8], True, True, CST.r() + EMT.r(), qr(bfl, 0, 4))
            cp("dve", col[:, 8:12], bfl[:, 0:4], qr(bfl, 0, 4), cr)
            yield
            bv = proj_tm(SL(S_MV), 0, 512, c0, L, bank=pd())
            va = TOKB[:, 1, :].rearrange("p (h v) -> p h v", h=4); var_ = TOKB.r(1)
            tt("dve", va[0:L, :, 0:128], bv[0:L, :].rearrange("p (h v) -> p h v", h=4),
               col[0:L, 0:4].unsqueeze(2).to_broadcast([L, 4, 128]), ALU.mult, bv.r() + cr, var_)
            cp("dve", va[0:L, :, 128:129], col[0:L, 0:4].unsqueeze(2), cr, var_)
            yield
            bo_ = proj_tm(SL(S_MO), 0, 512, c0, L, bank=pd())
            tho = TOK32[0]
            sigm(tho[0:L, :], bo_[0:L, :], bo_.r(), tho.r())
            yield
            bk_ = pm(); bkb = pbf(bk_)
            for h in range(4):
                tr(bkb[0:L, h * 128:(h + 1) * 128], FMB[:, 4 + h, c0:c0 + L], IDB[:, :], FMB.r(4 + h) + IDB.r(), qr(bk_, h * 64, h * 64 + 64))
            ktok = TOKB[:, 0, :]; ktr = TOKB.r(0)
            cp("act", ktok[0:L, 0:512], bkb[0:L, 0:512], qr(bk_, 0, 256), ktr)
            yield
            bs = pm()
            for h in range(4):
                mm(bs[0:L, h * 128:h * 128 + L], FMB[:, 4 + h, c0:c0 + L], FMB[:, h, c0:c0 + L], True, True,
                   FMB.r([h, 4 + h]), qr(bs, h * 128, h * 128 + L))
            sc = SC[p]
            tt("dve", sc[0:L, 0:4, 0:L], bs[0:L, :].rearrange("p (h t) -> p h t", h=4)[:, :, 0:L],
               m01[0:L, 0:L].unsqueeze(1).to_broadcast([L, 4, L]), ALU.mult, bs.r() + CST.r(), sc.r())
            yield
            by = [pm(), pm()]
            for h in range(4):
                bk2 = by[h // 2]; o = (h % 2) * 132
                mm(bk2[0:L, o:o + 129], sc[0:L, h, 0:L], va[0:L, h, 0:129], True, False, sc.r() + var_, qr(bk2, o, o + 129))
                mm(bk2[0:L, o:o + 129], FMB[:, h, c0:c0 + L], CMB[:, h, 0:129], False, True, FMB.r(h) + CMB.r(), qr(bk2, o, o + 129))
            yield
            for h in range(4):
                bk2 = by[h // 2]; o = (h % 2) * 132
                cp("dve", col[0:L, 12 + h:13 + h], bk2[0:L, o + 128:o + 129], qr(bk2, o, o + 129), cr)
            tt("dve", col[0:L, 16:20], col[0:L, 12:16], col[0:L, 4:8], ALU.mult, cr, cr)
            stt("dve", col[0:L, 16:20], col[0:L, 16:20], -1.0, col[0:L, 16:20], ALU.mult, ALU.max, cr, cr)
            ts("dve", col[0:L, 16:20], col[0:L, 16:20], 1.0, ALU.max, cr, cr)
            recip(col[0:L, 16:20], col[0:L, 16:20], cr, cr)
            tt("dve", col[0:L, 16:20], col[0:L, 16:20], col[0:L, 4:8], ALU.mult, cr, cr)
            yield
            memset("pool", col[0:L, 20:24], 0.0, cr)
            junk = TOK32[1]
            for h in range(4):
                bk2 = by[h // 2]; o = (h % 2) * 132
                act(junk[0:L, h * 128:(h + 1) * 128], bk2[0:L, o:o + 128], AF.Square, qr(bk2, o, o + 128), junk.r() + cr,
                    accum=col[0:L, 20 + h:21 + h])
            tt("dve", col[0:L, 24:28], col[0:L, 16:20], col[0:L, 16:20], ALU.mult, cr, cr)
            tt("dve", col[0:L, 24:28], col[0:L, 24:28], col[0:L, 20:24], ALU.mult, cr, cr)
            ts("dve", col[0:L, 24:28], col[0:L, 24:28], 1.0 / 128, ALU.mult, cr, cr, s2=EPS, op1=ALU.add)
            rsq(col[0:L, 24:28], cr)
            tt("dve", col[0:L, 24:28], col[0:L, 24:28], col[0:L, 16:20], ALU.mult, cr, cr)
            yield
            t2 = TOK32[4]
            for h in range(4):
                bk2 = by[h // 2]; o = (h % 2) * 132
                ts("dve", t2[0:L, h * 128:(h + 1) * 128], bk2[0:L, o:o + 128], col[0:L, 24 + h:25 + h], ALU.mult,
                   qr(bk2, o, o + 128) + cr, t2.r())
            mo = TOKB[:, 2, :]; mor = TOKB.r(2)
            tt("dve", mo[0:L, 0:512], tho[0:L, :], t2[0:L, :], ALU.mult, tho.r() + t2.r(), mor)
            yield
            to_mix((mo, mor), L, c0, 4, P_MNW, l, bank=pm())
            yield
            bd = [pm(), pm()]
            for h in range(4):
                bk2 = bd[h // 2]; o = (h % 2) * 132
                mm(bk2[:, o:o + 129], ktok[0:L, h * 128:(h + 1) * 128], va[0:L, h, 0:129], True, True, ktr + var_, qr(bk2, o, o + 129))
            for h in range(4):
                bk2 = bd[h // 2]; o = (h % 2) * 132
                tt("dve", CM[l][:, h, 0:129], CM[l][:, h, 0:129], bk2[:, o:o + 129], ALU.add, CM[l].r() + qr(bk2, o, o + 129), CM[l].r())
                ts("dve", CM[l][:, h, 0:129], CM[l][:, h, 0:129], col[:, 8 + h:9 + h], ALU.mult, CM[l].r() + cr, CM[l].r())
            yield
            cp("dve", CMB[:, :, 0:129], CM[l][:, :, 0:129], CM[l].r(), CMB.r())
            if is_sample:
                mlstm_store_state(l, o_s["mC"][l, seq], o_s["mn"][l, seq], o_s["mm"][l, seq], pm)
            yield
        if fin:
            mlstm_store_state(l, o_p["mC"][l], o_p["mn"][l], o_p["mm"][l], pm)

    def rg_gen(l, chunks, segs, T, is_sample, SL, fin):
        pd = mkrot([1, 2])
        if is_sample:
            for si, (s0, Ls, seq) in enumerate(segs):
                dma("pool", RGHS[:, si, :], i_rgh[l, seq].rearrange("(c p) -> p c", p=128), "RGI", [], RGHS.r(), slow=True)
        for c in range(4):
            u = c % 2
            conv_hist_in("rg", l, c, u, segs, is_sample)
            b = proj_fm(SL(S_XR), c * 128, 128, T, bank=pd())
            yield
            for si, (s0, Ls, seq) in enumerate(segs):
                base = si * (Ls + 3)
                cp("act", UX[:, u, base + 3:base + 3 + Ls], b[:, s0:s0 + Ls], qr(b, 0, T), UX.r(u))
            xr = FM32[c % 2]
            for si, (s0, Ls, seq) in enumerate(segs):
                base = si * (Ls + 3)
                wc = P_RCW + c * 4
                ts("dve", xr[:, s0:s0 + Ls], UX[:, u, base:base + Ls], PRM[:, l, wc:wc + 1], ALU.mult, UX.r(u) + PR, xr.r(),
                   s2=PRM[:, l, P_RCB + c:P_RCB + c + 1], op1=ALU.add)
                for j in range(1, 4):
                    stt("dve", xr[:, s0:s0 + Ls], UX[:, u, base + j:base + j + Ls], PRM[:, l, wc + j:wc + j + 1],
                        xr[:, s0:s0 + Ls], ALU.mult, ALU.add, UX.r(u) + PR + xr.r(), xr.r())
            conv_hist_out("rg", l, c, u, segs, is_sample)
            yield
            xrb = FMB[:, 6 + c % 2, :]; xrbr = FMB.r(6 + c % 2)
            cp("pool", xrb[:, 0:T], xr[:, 0:T], xr.r(), xrbr)
            ba = pd()
            mm(ba[:, 0:T], RGW[:, l, c, :], xrb[:, 0:T], True, True, RGW.r() + xrbr, qr(ba, 0, T))
            tha = FM32[2 + c % 2]
            yield
            sigm(tha[:, 0:T], ba[:, 0:T], qr(ba, 0, T) + PR, tha.r(), nbias=PRM[:, l, P_RBAH + c:P_RBAH + c + 1])
            yield
            a = FM32[4 + c % 2]
            act(a[:, 0:T], tha[:, 0:T], AF.Exp, tha.r() + PR, a.r(), scale=PRM[:, l, P_RC8 + c:P_RC8 + c + 1])
            sq = LNS[:, 0, :]; sqr = LNS.r(0)
            act(sq[:, 0:T], tha[:, 0:T], AF.Exp, tha.r() + PR, sqr, scale=PRM[:, l, P_RC4 + c:P_RC4 + c + 1])
            ts("dve", sq[:, 0:T], sq[:, 0:T], -1.0, ALU.mult, sqr, sqr, s2=1.0, op1=ALU.add)
            act(sq[:, 0:T], sq[:, 0:T], AF.Ln, sqr, sqr)
            act(sq[:, 0:T], sq[:, 0:T], AF.Exp, sqr, sqr, scale=0.5)
            yield
            bx = pd()
            mm(bx[:, 0:T], RGW[:, l, 4 + c, :], xrb[:, 0:T], True, True, RGW.r() + xrbr, qr(bx, 0, T))
            thx = LNS[:, 1, :]; thxr = LNS.r(1)
            sigm(thx[:, 0:T], bx[:, 0:T], qr(bx, 0, T) + PR, thxr, nbias=PRM[:, l, P_RBXH + c:P_RBXH + c + 1])
            tt("dve", thx[:, 0:T], thx[:, 0:T], xr[:, 0:T], ALU.mult, thxr + xr.r(), thxr)
            tt("pool", thx[:, 0:T], thx[:, 0:T], sq[:, 0:T], ALU.mult, thxr + sqr, thxr)
            yield
            hr = FM32[6 + c % 2]
            for si, (s0, Ls, seq) in enumerate(segs):
                if is_sample:
                    init = RGHS[:, si, c:c + 1]; ir = RGHS.r()
                else:
                    init = RGH[l][:, c:c + 1]; ir = RGH[l].r()
                scan(hr[:, s0:s0 + Ls], a[:, s0:s0 + Ls], thx[:, s0:s0 + Ls], init, ALU.mult, ALU.add, a.r() + thxr + ir, hr.r())
                if is_sample:
                    dma("pool", o_s["rgh"][l, seq].rearrange("(c p) -> p c", p=128)[:, c:c + 1], hr[:, s0 + Ls - 1:s0 + Ls],
                        "RGO%d" % (c % 2), hr.r(), [], slow=True)
                else:
                    cp("pool", RGH[l][:, c:c + 1], hr[:, s0 + Ls - 1:s0 + Ls], hr.r(), RGH[l].r())
            yield
            by = proj_fm(SL(S_YR), c * 128, 128, T, bank=pd())
            yield
            gy = LNS[:, 2, :]; gyr = LNS.r(2)
            yv = FM32[2 + c % 2]
            cp("act", yv[:, 0:T], by[:, 0:T], qr(by, 0, T), yv.r())
            act(gy[:, 0:T], by[:, 0:T], AF.Square, qr(by, 0, T), gyr)
            ts("dve", gy[:, 0:T], gy[:, 0:T], 0.044715, ALU.mult, gyr, gyr, s2=1.0, op1=ALU.add)
            tt("dve", gy[:, 0:T], gy[:, 0:T], yv[:, 0:T], ALU.mult, gyr + yv.r(), gyr)
            yield
            act(gy[:, 0:T], gy[:, 0:T], AF.Exp, gyr, gyr, scale=-1.5957691216057308)
            act(gy[:, 0:T], gy[:, 0:T], AF.Ln, gyr, gyr, bias=1.0)
            act(gy[:, 0:T], gy[:, 0:T], AF.Exp, gyr, gyr, scale=-1.0)
            tt("dve", gy[:, 0:T], gy[:, 0:T], yv[:, 0:T], ALU.mult, gyr + yv.r(), gyr)
            tt("pool", MIX[:, 8 + c, 0:T], hr[:, 0:T], gy[:, 0:T], ALU.mult, hr.r() + gyr, MIX.r(8 + c))
            yield
        if fin:
            dma("pool", o_p["rgh"][l].rearrange("(c p) -> p c", p=128), RGH[l][:, :], "FSO", RGH[l].r(), [], slow=True)
            for c in range(4):
                dma("pool", o_p["rgconv"][l].rearrange("j (c p) -> p c j", p=128)[:, c, :], CSR[l][:, c, :], "FSO", CSR[l].r(), [], slow=True)

    def gla_load_state(l, seq):
        stg = FM32[6]
        dma("sp", stg[:, 0:256].rearrange("p (a v) -> p a v", a=2), i_gla[l, seq].rearrange("(a hh) d v -> (hh d) a v", hh=2),
            "GSI", [], stg.r())
        cp("dve", GS[l][:, :, :], stg[:, 0:256].rearrange("p (a v) -> p a v", a=2), stg.r(), GS[l].r())
        cp("pool", GSB[:, :, :], stg[:, 0:256].rearrange("p (a v) -> p a v", a=2), stg.r(), GSB.r())

    def gla_store_state(l, dst):
        dma("sp", dst.rearrange("(a hh) d v -> (hh d) a v", hh=2), GS[l][:, :, :], "GSO%d" % l, GS[l].r(), [])

    GQ = [(TMPB[0][:, pc * 512:(pc + 1) * 512], TMPB[0].r()) for pc in range(2)]
    GK = [(TMPB[1][:, pc * 512:(pc + 1) * 512], TMPB[1].r()) for pc in range(2)]
    GKD = [(TMPB[2][:, pc * 512:(pc + 1) * 512], TMPB[2].r()) for pc in range(2)]

    def gla_fm(l, chunks, segs, T, is_sample, SL):
        bag = proj_wsm(l, 16, 16, T)
        agb = TOKB[:, 3, :]; agr = TOKB.r(3)
        cp("act", agb[0:16, 0:T], bag[0:16, 0:T], qr(bag, 0, T), agr)
        Ac = [FM32[0], FM32[1]]; rAc = [FM32[2], FM32[3]]
        for pc in range(2):
            bl = psD()
            mm(bl[:, 0:T], GW2[0:16, l, pc * 128:(pc + 1) * 128], agb[0:16, 0:T], True, True, GW2.r() + agr, qr(bl, 0, T))
            th = FM32[4 + pc]
            act(th[:, 0:T], bl[:, 0:T], AF.Exp, qr(bl, 0, T) + PR, th.r(), bias=PRM[:, l, P_GGBH + pc:P_GGBH + pc + 1], scale=-1.0)
            act(th[:, 0:T], th[:, 0:T], AF.Ln, th.r(), th.r(), bias=1.0)
            act(th[:, 0:T], th[:, 0:T], AF.Exp, th.r(), th.r(), scale=-1.0 / 16.0)
            for (c0, L) in chunks:
                scan(Ac[pc][:, c0:c0 + L], th[:, c0:c0 + L], zeros_f[:, 0:L], 1.0, ALU.mult, ALU.add, th.r() + CST.r(), Ac[pc].r())
            recip(rAc[pc][:, 0:T], Ac[pc][:, 0:T], Ac[pc].r(), rAc[pc].r())
            bq = proj_fm(SL(S_GQK), pc * 128, 128, T)
            stt("dve", GQ[pc][0][:, 0:T], bq[:, 0:T], 0.125, Ac[pc][:, 0:T], ALU.mult, ALU.mult, qr(bq, 0, T) + Ac[pc].r(), GQ[pc][1])
            bk = proj_fm(SL(S_GQK), 256 + pc * 128, 128, T)
            tt("dve", GK[pc][0][:, 0:T], bk[:, 0:T], rAc[pc][:, 0:T], ALU.mult, qr(bk, 0, T) + rAc[pc].r(), GK[pc][1])
            for ci, (c0, L) in enumerate(chunks):
                stt("dve", GKD[pc][0][:, c0:c0 + L], bk[:, c0:c0 + L], Ac[pc][:, c0 + L - 1:c0 + L], rAc[pc][:, c0:c0 + L],
                    ALU.mult, ALU.mult, qr(bk, 0, T) + Ac[pc].r() + rAc[pc].r(), GKD[pc][1])
                cp("dve", GAL[:, pc, ci:ci + 1], Ac[pc][:, c0 + L - 1:c0 + L], Ac[pc].r(), GAL.r())
        SL.rel(S_GQK)
        for h in range(4):
            ts("dve", BCM[:, h, 0:T], GQ[h // 2][0][:, 0:T], CST[:, C_HM + h % 2:C_HM + h % 2 + 1], ALU.mult,
               GQ[h // 2][1] + CST.r(), BCM.r(h))

    def gla_loop(l, chunks, segs, T, is_sample, SL, fin):
        pm = mkrot([7, 2]); pd = mkrot([1])
        if not is_sample:
            cp("dve", GSB[:, :, :], GS[l][:, :, :], GS[l].r(), GSB.r())
        for ci, (c0, L) in enumerate(chunks):
            p = ci % 2
            col = COLG[p]; cr = col.r()
            seq = segs[ci][2] if is_sample else None
            if is_sample:
                gla_load_state(l, seq)
            yield
            bs = pm()
            for h in range(4):
                pc = h // 2; r0 = (h % 2) * 64
                mm(bs[0:L, h * 128:h * 128 + L], GK[pc][0][:, c0:c0 + L], BCM[:, h, c0:c0 + L], True, True,
                   GK[pc][1] + BCM.r(h), qr(bs, h * 128, h * 128 + L))
            yield
            sc = SCG[p]
            tt("dve", sc[0:L, 0:4, 0:L], bs[0:L, :].rearrange("p (h t) -> p h t", h=4)[:, :, 0:L],
               m01[0:L, 0:L].unsqueeze(1).to_broadcast([L, 4, L]), ALU.mult, bs.r() + CST.r(), sc.r())
            yield
            bv = proj_tm(SL(S_GV), 0, 512, c0, L, bank=pd())
            vbf = FM16[0]; vbr = FM16[0].r()
            cp("act", vbf[0:L, 0:512], bv[0:L, :], bv.r(), vbr)
            yield
            bo = pm()
            for h in range(4):
                pc = h // 2; r0 = (h % 2) * 64
                mm(bo[0:L, h * 128:(h + 1) * 128], sc[0:L, h, 0:L], vbf[0:L, h * 128:(h + 1) * 128], True, False,
                   sc.r() + vbr, qr(bo, h * 128, (h + 1) * 128))
                mm(bo[0:L, h * 128:(h + 1) * 128], BCM[:, h, c0:c0 + L], GSB[:, pc, :], False, True,
                   BCM.r(h) + GSB.r(), qr(bo, h * 128, (h + 1) * 128))
            yield
            memset("pool", col[0:L, 0:4], 0.0, cr)
            junk = FM32[4]
            for h in range(4):
                act(junk[0:L, h * 128:(h + 1) * 128], bo[0:L, h * 128:(h + 1) * 128], AF.Square, qr(bo, h * 128, (h + 1) * 128),
                    junk.r() + cr, accum=col[0:L, h:h + 1])
            ts("dve", col[0:L, 4:8], col[0:L, 0:4], 1.0 / 128, ALU.mult, cr, cr, s2=EPS, op1=ALU.add)
            rsq(col[0:L, 4:8], cr)
            yield
            bg = proj_tm(SL(S_GG), 0, 512, c0, L, bank=pd())
            yield
            thg = FM32[3]
            sigm(thg[0:L, :], bg[0:L, :], bg.r(), thg.r())
            tt("dve", thg[0:L, :], thg[0:L, :], bg[0:L, :], ALU.mult, thg.r() + bg.r(), thg.r())
            yield
            t2 = FM32[5]
            tt("dve", t2[0:L, :].rearrange("p (h v) -> p h v", h=4), bo[0:L, :].rearrange("p (h v) -> p h v", h=4),
               col[0:L, 4:8].unsqueeze(2).to_broadcast([L, 4, 128]), ALU.mult, bo.r() + cr, t2.r())
            go = FM16[2]; gor = FM16[2].r()
            tt("dve", go[0:L, 0:512], t2[0:L, :], thg[0:L, :], ALU.mult, t2.r() + thg.r(), gor)
            yield
            to_mix((go, gor), L, c0, 12, P_GNW, l, bank=pm())
            yield
            bkd = pm(); bkdb = pbf(bkd)
            for pc in range(2):
                tr(bkdb[0:L, pc * 128:(pc + 1) * 128], GKD[pc][0][:, c0:c0 + L], IDB[:, :], GKD[pc][1] + IDB.r(), qr(bkd, pc * 64, pc * 64 + 64))
            kdt = FM16[1]; kdr = FM16[1].r()
            cp("act", kdt[0:L, 0:256], bkdb[0:L, 0:256], qr(bkd, 0, 128), kdr)
            yield
            bd = pm()
            for h in range(4):
                pc = h // 2; r0 = (h % 2) * 64
                mm(bd[r0:r0 + 64, pc * 128:(pc + 1) * 128], kdt[0:L, pc * 128 + r0:pc * 128 + r0 + 64], vbf[0:L, h * 128:(h + 1) * 128],
                   True, True, kdr + vbr, qr(bd, pc * 128, (pc + 1) * 128))
            for pc in range(2):
                stt("dve", GS[l][:, pc, :], GS[l][:, pc, :], GAL[:, pc, ci:ci + 1], bd[:, pc * 128:(pc + 1) * 128],
                    ALU.mult, ALU.add, GS[l].r() + GAL.r() + qr(bd, pc * 128, (pc + 1) * 128), GS[l].r())
            yield
            cp("dve", GSB[:, :, :], GS[l][:, :, :], GS[l].r(), GSB.r())
            if is_sample:
                gla_store_state(l, o_s["gla"][l, seq])
            yield
        if fin:
            gla_store_state(l, o_p["gla"][l])

    tiles = []
    if SAMPLE:
        tiles.append(("s", None))
    for t in range(NT):
        tiles.append(("p", t))
    for kind, t in tiles:
        for l in range(NL):
            for j in range(NSLAB):
                slab_seq.append((l, j))

    def zero_states():
        for l in range(NL):
            memset("pool", HS[l][:], 0.0, HS[l].r())
            memset("pool", CSS[l][:], 0.0, CSS[l].r())
            memset("pool", CM[l][:], 0.0, CM[l].r())
            memset("pool", EM[l][:], 1.0, EM[l].r())
            memset("pool", RGH[l][:], 0.0, RGH[l].r())
            memset("pool", CSR[l][:], 0.0, CSR[l].r())
            memset("pool", GS[l][:], 0.0, GS[l].r())
        memset("pool", HSB[:], 0.0, HSB.r())
        memset("pool", CMB[:], 0.0, CMB.r())
        memset("pool", GSB[:], 0.0, GSB.r())

    zero_states()
    gi = 0
    STOP = cfg.get("STOP", 0)
    if STOP == 1:
        tiles = []
    for kind, t in tiles:
        if kind == "s":
            gi = run_tile(xs, ys, [(0, 16), (16, 16)], [(0, 16, 0), (16, 16, 1)], 32, True, gi, False)
            zero_states()
        else:
            gi = run_tile(xp[t * 512:(t + 1) * 512, :], yp[t * 512:(t + 1) * 512, :],
                          [(0, 128), (128, 128), (256, 128), (384, 128)], [(0, 512, None)], 512, False, gi, t == NT - 1)

    fin_toks = [(k, v) for k, v in P.cnt.items() if not k.startswith("E_") and v > 0]
    P.wait_all("sp", fin_toks)
    P.emit()
    P.close()
    return nc, P


_CACHE = {}


def _get_program(cfg_key, cfg):
    if cfg_key not in _CACHE:
        _CACHE[cfg_key] = build(cfg)
    return _CACHE[cfg_key]


WEIGHT_NAMES = ["ln_in_g", "ln_in_b", "w_in", "ssd_conv_w", "ssd_conv_b", "ssd_dt_bias", "ssd_A_log", "ssd_D", "ssd_norm_w",
                "mlstm_if_b", "mlstm_norm_w", "rg_conv_w", "rg_conv_b", "rg_gate_a_w", "rg_gate_a_b", "rg_gate_x_w",
                "rg_gate_x_b", "rg_lambda", "gla_gate_w2", "gla_gate_b", "gla_norm_w", "w_out", "ln1_g", "ln1_b",
                "mlp_w1", "mlp_b1", "mlp_w2", "mlp_b2", "ln2_g", "ln2_b"]
STATE_IN = [("state_ssd_h", "i_ssd_h"), ("state_ssd_conv", "i_ssd_conv"), ("state_mlstm_C", "i_mC"), ("state_mlstm_n", "i_mn"),
            ("state_mlstm_m", "i_mm"), ("state_rglru_h", "i_rgh"), ("state_rglru_conv", "i_rgconv"), ("state_gla_S", "i_gla")]
OUT_KEYS = ["ssd_h", "ssd_conv", "mC", "mn", "mm", "rgh", "rgconv", "gla"]


def run(inputs, cfg=None, ncores=NCORE):
    cfg = dict(cfg or {})
    NCORE_ = ncores
    NT = cfg.get("NT", SEQ // 512)
    nc, P = _get_program(tuple(sorted(cfg.items())), cfg)
    cst = make_consts()
    in_maps = []
    for c in range(NCORE_):
        m = {"xp": np.ascontiguousarray(inputs["x_prompt"][c, :NT * 512]),
             "xs": np.ascontiguousarray(inputs["x_sample"][2 * c:2 * c + 2].reshape(32, D)),
             "consts": cst}
        for src, dst in STATE_IN:
            m[dst] = np.ascontiguousarray(inputs[src][:, 2 * c:2 * c + 2])
        for w in WEIGHT_NAMES:
            m[w] = np.ascontiguousarray(inputs[w])
        in_maps.append(m)
    res = run_bass_kernel_spmd(nc, in_maps, core_ids=list(range(NCORE_)))
    R = res.results
    if NCORE_ < NCORE:
        R = list(R) + [R[0]] * (NCORE - NCORE_)
    y_prompt = np.stack([R[c]["yp"] for c in range(NCORE)], 0)
    y_sample = np.concatenate([R[c]["ys"].reshape(2, 16, D) for c in range(NCORE)], 0)
    outs = [y_prompt, y_sample]
    for k in OUT_KEYS:
        outs.append(np.stack([R[c]["p_" + k] for c in range(NCORE)], 1))
    for k in OUT_KEYS:
        outs.append(np.concatenate([R[c]["s_" + k] for c in range(NCORE)], 1))
    return tuple(np.asarray(o, np.float32) for o in outs), res


def kernel(**inputs):
    inputs = {k: np.asarray(v) for k, v in inputs.items()}
    outs, _ = run(inputs, {})
    return outs
```

```python
import numpy as np
import concourse.bass as bass
import concourse.mybir as mybir
from concourse.bass_utils import run_bass_kernel_spmd
from contextlib import ExitStack

F32 = mybir.dt.float32
F32R = mybir.dt.float32r
BF16 = mybir.dt.bfloat16
ALU = mybir.AluOpType
AF = mybir.ActivationFunctionType
AX = mybir.AxisListType

SAME_ENG_SYNC = True
ATTACH_WAIT = True

D = 1024
DEPTH = 4
SEQ = 8192
NCORE = 8
EPS = 1e-5
ALPHA = (2 * DEPTH) ** 0.25
D_IN = 5920
NSLAB = 32
NSLOT = 5

WIN_SLABS = [
    [(0, 512, 512)],
    [(0, 1024, 256), (256, 1280, 8), (264, 2824, 8), (272, 5392, 16)],
    [(0, 0, 512)],
    [(0, 3344, 512)],
    [(0, 3856, 512)],
    [(0, 1288, 512)],
    [(0, 1800, 512)],
    [(0, 4368, 512)],
    [(0, 2312, 512)],
    [(0, 2832, 512)],
    [(0, 4880, 512)],
    [(0, 5408, 512)],
]
S_XBC, S_BC, S_Z, S_XR, S_YR, S_MQ, S_MK, S_GQK, S_MV, S_MO, S_GV, S_GG = range(12)

C_ID, C_MNEG, C_M01, C_ODIV, C_ONES, C_SEL = 0, 128, 256, 384, 512, 640
C_ZERO = 640 + 1024
C_HM = C_ZERO + 128
C_N = C_HM + 4


def make_consts():
    c = np.zeros((128, C_N), np.float32)
    c[:, C_ID:C_ID + 128] = np.eye(128, dtype=np.float32)
    s = np.arange(128)[:, None]
    t = np.arange(128)[None, :]
    c[:, C_MNEG:C_MNEG + 128] = np.where(t >= s, 0.0, -30000.0)
    c[:, C_M01:C_M01 + 128] = np.where(t >= s, 1.0, 0.0)
    c[:, C_ODIV:C_ODIV + 128] = 1.0 / 1024.0
    c[:, C_ONES:C_ONES + 128] = 1.0
    for h in range(8):
        c[h, C_SEL + h * 128:C_SEL + (h + 1) * 128] = 1.0
    c[0:64, C_HM] = 1.0
    c[64:128, C_HM + 1] = 1.0
    return c


P_LN1G, P_LN1B, P_LN2G, P_LN2B, P_B1 = 0, 8, 16, 24, 32
P_SCW, P_SCB, P_SNW, P_MNW = 64, 88, 94, 98
P_RCW, P_RCB, P_RBA, P_RBX, P_RLAM, P_GGB, P_GNW = 102, 118, 122, 126, 130, 134, 136
P_SCWH, P_SCBH, P_RC4, P_RC8, P_RBAH, P_RBXH, P_GGBH = 140, 164, 170, 174, 178, 182, 186
P_B2A = 188
PN = 200


class Reg:
    __slots__ = ("name", "lw", "rd")

    def __init__(self, name):
        self.name = name
        self.lw = None
        self.rd = []


class Prog:
    ENGS = ("pe", "act", "dve", "pool", "sp")

    def __init__(self, nc):
        self.nc = nc
        self.es = ExitStack()
        self.streams = {e: [] for e in self.ENGS}
        self.cnt = {}
        self.sems = {}
        self.known = {e: {} for e in self.ENGS}
        for e in self.ENGS:
            self.newsem("E_" + e)

    def newsem(self, key):
        self.sems[key] = self.es.enter_context(self.nc.semaphore(key))
        self.cnt[key] = 0
        return key

    def sb(self, name, shape, dt):
        return self.es.enter_context(self.nc.sbuf_tensor(name, list(shape), dt))

    def ps(self, name, shape, dt):
        return self.es.enter_context(self.nc.psum_tensor(name, list(shape), dt))

    def _deps(self, eng, reads, writes):
        need = {}

        def add(tok):
            if tok is None:
                return
            k, v = tok
            if need.get(k, 0) < v:
                need[k] = v
        for r in reads:
            add(r.lw)
        for w in writes:
            add(w.lw)
            for t in w.rd:
                add(t)
        st = self.streams[eng]
        kn = self.known[eng]
        own = "E_" + eng
        for k, v in need.items():
            if k == own and (eng == "pe" or not SAME_ENG_SYNC):
                continue
            if kn.get(k, 0) >= v:
                continue
            kn[k] = v
            st.append(("w", k, v))

    def _mark(self, tok, reads, writes):
        for r in reads:
            r.rd.append(tok)
            if len(r.rd) > 48:
                d = {}
                for k, v in r.rd:
                    if d.get(k, 0) < v:
                        d[k] = v
                r.rd = list(d.items())
        for w in writes:
            w.lw = tok
            w.rd = []

    def op(self, eng, fn, reads=(), writes=()):
        self._deps(eng, reads, writes)
        key = "E_" + eng
        self.cnt[key] += 1
        tok = (key, self.cnt[key])
        self.streams[eng].append(("o", fn, key, 1))
        self._mark(tok, reads, writes)
        return tok

    def dma(self, eng, fn, semkey, reads=(), writes=()):
        self._deps(eng, reads, writes)
        self.cnt[semkey] += 16
        tok = (semkey, self.cnt[semkey])
        self.streams[eng].append(("o", fn, semkey, 16))
        self._mark(tok, reads, writes)
        return tok

    def wait_all(self, eng, toks):
        st = self.streams[eng]
        kn = self.known[eng]
        for k, v in toks:
            if kn.get(k, 0) >= v:
                continue
            kn[k] = v
            st.append(("w", k, v))

    def emit(self):
        nc = self.nc
        observed = {}
        for s in self.streams.values():
            for it in s:
                if it[0] == "w" and it[1].startswith("E_"):
                    observed.setdefault(it[1], set()).add(it[2])
        rank = {k: {v: i + 1 for i, v in enumerate(sorted(vs))} for k, vs in observed.items()}
        with nc.Block() as block:
            def mk(engname):
                def body(e):
                    n_op = 0
                    own = "E_" + engname
                    myrank = rank.get(own, {})
                    pend = []
                    for it in self.streams[engname]:
                        if it[0] == "w":
                            k, v = it[1], it[2]
                            if k.startswith("E_"):
                                v = rank[k][v]
                            pend.append((k, v))
                        else:
                            for (k, v) in pend[:-1]:
                                e.wait_ge(self.sems[k], v)
                            ins = it[1](e)
                            if pend:
                                k, v = pend[-1]
                                if ATTACH_WAIT:
                                    ins._wait_ge(self.sems[k], e.lower_val(v))
                                else:
                                    raise RuntimeError
                            pend = []
                            if it[2].startswith("E_"):
                                n_op += 1
                                if n_op in myrank:
                                    ins.then_inc(self.sems[it[2]], 1)
                            else:
                                ins.then_inc(self.sems[it[2]], it[3])
                    for (k, v) in pend:
                        e.wait_ge(self.sems[k], v)
                return body
            block.tensor(mk("pe"))
            block.scalar(mk("act"))
            block.vector(mk("dve"))
            block.gpsimd(mk("pool"))
            block.sync(mk("sp"))

    def close(self):
        self.es.close()

    def stats(self):
        return {e: (sum(1 for i in s if i[0] == "o"), sum(1 for i in s if i[0] == "w"))
                for e, s in self.streams.items()}


class TT:
    def __init__(self, P, name, shape, dt, ncell=1, psum=False):
        self.t = (P.ps if psum else P.sb)(name, shape, dt)
        self.c = [Reg("%s.%d" % (name, i)) for i in range(ncell)]

    def __getitem__(self, k):
        return self.t[k]

    def r(self, i=None):
        if i is None:
            return list(self.c)
        if isinstance(i, int):
            return [self.c[i]]
        return [self.c[j] for j in i]


def build(cfg):
    NL = cfg.get("NL", DEPTH)
    NT = cfg.get("NT", SEQ // 512)
    SAMPLE = cfg.get("SAMPLE", True)
    NTOKP = NT * 512
    DBG = cfg.get("DBG", False)

    nc = bass.Bass("TRN2", target_bir_lowering=False)
    P = Prog(nc)

    def din(name, shape, dt=F32):
        return nc.dram_tensor(name, list(shape), dt, kind="ExternalInput").ap()

    def dout(name, shape):
        return nc.dram_tensor(name, list(shape), F32, kind="ExternalOutput").ap()

    xp = din("xp", [NTOKP, D])
    xs = din("xs", [32, D])
    i_ssd_h = din("i_ssd_h", [DEPTH, 2, 8, 64, 64])
    i_ssd_conv = din("i_ssd_conv", [DEPTH, 2, 3, 768])
    i_mC = din("i_mC", [DEPTH, 2, 4, 128, 128])
    i_mn = din("i_mn", [DEPTH, 2, 4, 128])
    i_mm = din("i_mm", [DEPTH, 2, 4])
    i_rgh = din("i_rgh", [DEPTH, 2, 512])
    i_rgconv = din("i_rgconv", [DEPTH, 2, 3, 512])
    i_gla = din("i_gla", [DEPTH, 2, 4, 64, 128])
    consts = din("consts", [128, C_N])
    ln_in_g = din("ln_in_g", [D]); ln_in_b = din("ln_in_b", [D])
    w_in = din("w_in", [DEPTH, D, D_IN])
    ssd_conv_w = din("ssd_conv_w", [DEPTH, 4, 768]); ssd_conv_b = din("ssd_conv_b", [DEPTH, 768])
    ssd_dt_bias = din("ssd_dt_bias", [DEPTH, 8]); ssd_A_log = din("ssd_A_log", [DEPTH, 8]); ssd_D = din("ssd_D", [DEPTH, 8])
    ssd_norm_w = din("ssd_norm_w", [DEPTH, 512])
    mlstm_if_b = din("mlstm_if_b", [DEPTH, 8]); mlstm_norm_w = din("mlstm_norm_w", [DEPTH, 512])
    rg_conv_w = din("rg_conv_w", [DEPTH, 4, 512]); rg_conv_b = din("rg_conv_b", [DEPTH, 512])
    rg_gate_a_w = din("rg_gate_a_w", [DEPTH, 8, 64, 64]); rg_gate_a_b = din("rg_gate_a_b", [DEPTH, 512])
    rg_gate_x_w = din("rg_gate_x_w", [DEPTH, 8, 64, 64]); rg_gate_x_b = din("rg_gate_x_b", [DEPTH, 512])
    rg_lambda = din("rg_lambda", [DEPTH, 512])
    gla_gate_w2 = din("gla_gate_w2", [DEPTH, 16, 256]); gla_gate_b = din("gla_gate_b", [DEPTH, 256])
    gla_norm_w = din("gla_norm_w", [DEPTH, 512])
    w_out = din("w_out", [DEPTH, 2048, D])
    ln1_g = din("ln1_g", [DEPTH, D]); ln1_b = din("ln1_b", [DEPTH, D])
    mlp_w1 = din("mlp_w1", [DEPTH, D, 4096]); mlp_b1 = din("mlp_b1", [DEPTH, 4096])
    mlp_w2 = din("mlp_w2", [DEPTH, 4096, D]); mlp_b2 = din("mlp_b2", [DEPTH, D])
    ln2_g = din("ln2_g", [DEPTH, D]); ln2_b = din("ln2_b", [DEPTH, D])

    yp = dout("yp", [NTOKP, D])
    ys = dout("ys", [32, D])
    o_p = dict(ssd_h=dout("p_ssd_h", [DEPTH, 8, 64, 64]), ssd_conv=dout("p_ssd_conv", [DEPTH, 3, 768]),
               mC=dout("p_mC", [DEPTH, 4, 128, 128]), mn=dout("p_mn", [DEPTH, 4, 128]), mm=dout("p_mm", [DEPTH, 4]),
               rgh=dout("p_rgh", [DEPTH, 512]), rgconv=dout("p_rgconv", [DEPTH, 3, 512]), gla=dout("p_gla", [DEPTH, 4, 64, 128]))
    o_s = dict(ssd_h=dout("s_ssd_h", [DEPTH, 2, 8, 64, 64]), ssd_conv=dout("s_ssd_conv", [DEPTH, 2, 3, 768]),
               mC=dout("s_mC", [DEPTH, 2, 4, 128, 128]), mn=dout("s_mn", [DEPTH, 2, 4, 128]), mm=dout("s_mm", [DEPTH, 2, 4]),
               rgh=dout("s_rgh", [DEPTH, 2, 512]), rgconv=dout("s_rgconv", [DEPTH, 2, 3, 512]), gla=dout("s_gla", [DEPTH, 2, 4, 64, 128]))
    wbf = nc.dram_tensor("wbf", [NL, NSLAB, 128, 4096], BF16, kind="Internal").ap()
    dbg = dout("dbg_mix", [2048, 512]) if DBG else None

    def semfor(name):
        if name not in P.sems:
            P.newsem(name)
        return name

    class VW:
        def __init__(self, ap, cells_per_chunk):
            self.t = ap
            self.cc = cells_per_chunk

        def __getitem__(self, k):
            return self.t[k]

        def r(self, i=None):
            if i is None:
                out = []
                for c in self.cc:
                    out += c
                return out
            if isinstance(i, int):
                return list(self.cc[i])
            out = []
            for j in i:
                out += self.cc[j]
            return out

    CST = TT(P, "CST", [128, C_N], F32)
    IDB = TT(P, "IDB", [128, 128], BF16)
    ODIVR = TT(P, "ODIVR", [128, 128], F32R)
    KC = TT(P, "KC", [128, 4], F32)
    ONESB = TT(P, "ONESB", [128, 512], BF16)
    PRM = TT(P, "PRM", [128, NL, PN], F32)
    PS8 = TT(P, "PS8", [8, NL, 8], F32)
    DBC = TT(P, "DBC", [128, NL, 8], F32)
    RGW = TT(P, "RGW", [128, NL, 8, 128], BF16)
    GW2 = TT(P, "GW2", [16, NL, 256], BF16)
    LNIN = TT(P, "LNIN", [128, 16], F32)
    WSM = TT(P, "WSM", [128, NL, 8, 32], BF16)

    XT = TT(P, "XT", [128, 8, 512], F32, ncell=8)
    XB = TT(P, "XB", [128, 8, 512], BF16, ncell=8)
    MIX = TT(P, "MIX", [128, 16, 512], BF16, ncell=16)
    HT = TT(P, "HT", [128, 32, 512], BF16, ncell=32)
    TMPR = [TT(P, "TMPR%d" % i, [128, 512], F32R) for i in range(4)]
    LNS = TT(P, "LNS", [128, 3, 512], F32, ncell=3)
    WR = [TT(P, "WR%d" % i, [128, 4096], BF16) for i in range(NSLOT)]
    for i in range(NSLOT):
        P.newsem("W%d" % i)

    HTflat = HT.t[:].rearrange("p a b -> p (a b)")
    MIXflat = MIX.t[:].rearrange("p a b -> p (a b)")

    def aview(base, flat, byte_off, shape, dt, chunked=False):
        esz = 2 if dt == BF16 else 4
        n_el = 1
        for s_ in shape[1:]:
            n_el *= s_
        nb = n_el * esz
        v = flat[:, byte_off // 2:(byte_off + nb) // 2]
        if dt != BF16:
            v = v.bitcast(dt)
        if len(shape) == 3:
            v = v.rearrange("p (a b) -> p a b", a=shape[1])
        cells = base.c[byte_off // 1024:(byte_off + nb + 1023) // 1024]
        if chunked:
            n = shape[1]
            per = len(cells) // n
            cc = [cells[i * per:(i + 1) * per] for i in range(n)]
        else:
            cc = [cells]
        return VW(v, cc)

    FM32 = [aview(HT, HTflat, i * 2048, [128, 512], F32) for i in range(8)]
    TOK32 = [aview(HT, HTflat, 16384 + i * 2048, [128, 512], F32) for i in range(6)]
    TOK32W = aview(HT, HTflat, 16384 + 2 * 2048, [128, 4, 132], F32)
    TK = aview(HT, HTflat, 28672, [128, 8, 128], F32, chunked=False)
    STG = aview(HT, HTflat, 16384, [128, 8, 128], F32)
    STG2 = aview(HT, HTflat, 16384 + 4096, [128, 1024], F32)
    RR1 = aview(HT, HTflat, 0, [128, 8, 512], F32, chunked=True)
    RR2 = aview(MIX, MIXflat, 0, [128, 8, 512], F32, chunked=True)
    XIO = aview(MIX, MIXflat, 0, [128, 4, D], F32, chunked=True)
    HTF = HTflat.bitcast(F32)
    FM16 = [aview(HT, HTflat, i * 2048, [128, 1024], BF16) for i in range(8)]
    TMPB = [VW(LNS.t[:, i, :].bitcast(BF16), [LNS.r(i)]) for i in range(3)]

    FMB = TT(P, "FMB", [128, 8, 512], BF16, ncell=8)
    BCM = TT(P, "BCM", [128, 4, 512], BF16, ncell=4)
    UX = TT(P, "UX", [128, 2, 520], F32, ncell=2)
    SM8 = TT(P, "SM8", [16, 4, 512], F32, ncell=4)
    TOKB = TT(P, "TOKB", [128, 4, 528], BF16, ncell=4)
    SC = [TT(P, "SC%d" % i, [128, 8, 128], BF16) for i in range(2)]
    COL = [TT(P, "COL%d" % i, [128, 64], F32) for i in range(2)]
    SCG = [TT(P, "SCG%d" % i, [128, 4, 128], BF16) for i in range(2)]
    COLG = [TT(P, "COLG%d" % i, [128, 16], F32) for i in range(2)]
    GAL = TT(P, "GAL", [128, 2, 4], F32)
    EMT = TT(P, "EMT", [8, 16], F32)
    MB = TT(P, "MB", [128, 8], F32)
    RGHS = TT(P, "RGHS", [128, 2, 4], F32)

    HS = [TT(P, "HS%d" % l, [128, 256], F32) for l in range(NL)]
    HSB = TT(P, "HSB", [128, 256], BF16)
    CSS = [TT(P, "CSS%d" % l, [128, 6, 3], F32) for l in range(NL)]
    CM = [TT(P, "CM%d" % l, [128, 4, 132], F32) for l in range(NL)]
    CMB = TT(P, "CMB", [128, 4, 132], BF16)
    EM = [TT(P, "EM%d" % l, [4, 2], F32) for l in range(NL)]
    RGH = [TT(P, "RGH%d" % l, [128, 4], F32) for l in range(NL)]
    CSR = [TT(P, "CSR%d" % l, [128, 4, 3], F32) for l in range(NL)]
    GS = [TT(P, "GS%d" % l, [128, 2, 128], F32) for l in range(NL)]
    GSB = TT(P, "GSB", [128, 2, 128], BF16)

    PSB = [TT(P, "PSB%d" % i, [128, 512], F32, ncell=1, psum=True) for i in range(8)]
    pstate = {"d": 0, "m": 0}

    def psD():
        b = PSB[pstate["d"] % 3]
        pstate["d"] += 1
        return b

    def psM():
        b = PSB[3 + pstate["m"] % 5]
        pstate["m"] += 1
        return b

    def mkrot(idx):
        st = {"i": 0}

        def f():
            bk = PSB[idx[st["i"] % len(idx)]]
            st["i"] += 1
            return bk
        return f

    def qr(bank, c0, c1):
        return list(bank.c)

    def pbf(bank):
        return bank.t[:].bitcast(BF16)

    def tt(eng, out, a, b, op, R, W):
        P.op(eng, lambda e: e.tensor_tensor(out=out, in0=a, in1=b, op=op), R, W)

    def ts(eng, out, a, s1, op0, R, W, s2=None, op1=None):
        if s2 is None:
            P.op(eng, lambda e: e.tensor_scalar(out=out, in0=a, scalar1=s1, scalar2=None, op0=op0), R, W)
        else:
            P.op(eng, lambda e: e.tensor_scalar(out=out, in0=a, scalar1=s1, scalar2=s2, op0=op0, op1=op1), R, W)

    def stt(eng, out, a, s, b, op0, op1, R, W):
        P.op(eng, lambda e: e.scalar_tensor_tensor(out=out, in0=a, scalar=s, in1=b, op0=op0, op1=op1), R, W)

    def act(out, in_, func, R, W, bias=None, scale=None, accum=None):
        kw = {}
        if bias is not None:
            kw["bias"] = bias
        if scale is not None:
            kw["scale"] = scale
        if accum is not None:
            kw["accum_out"] = accum
        P.op("act", lambda e: e.activation(out=out, in_=in_, func=func, **kw), R, W)

    def sigm(out, in_, R, W, nbias=None):
        act(out, in_, AF.Exp, R, W, scale=-1.0, bias=nbias)
        act(out, out, AF.Ln, W, W, bias=1.0)
        act(out, out, AF.Exp, W, W, scale=-1.0)

    def rsq(x, R):
        act(x, x, AF.Ln, R, R)
        act(x, x, AF.Exp, R, R, scale=-0.5)

    def cp(eng, out, in_, R, W):
        if eng == "act":
            P.op("act", lambda e: e.activation(out=out, in_=in_, func=AF.Copy), R, W)
        else:
            P.op(eng, lambda e: e.tensor_copy(out=out, in_=in_), R, W)

    def mm(out, lhsT, rhs, st, sp, R, W):
        P.op("pe", lambda e: e.matmul(out, lhsT=lhsT, rhs=rhs, start=st, stop=sp), R, W)

    def tr(out, in_, idn, R, W):
        P.op("pe", lambda e: e.transpose(out, in_, idn), R, W)

    def memset(eng, ap, val, W):
        P.op(eng, lambda e: e.memset(ap, val), [], W)

    def recip(out, in_, R, W):
        P.op("dve", lambda e: e.reciprocal(out=out, in_=in_), R, W)

    def scan(out, d0, d1, init, op0, op1, R, W):
        P.op("dve", lambda e: e.tensor_tensor_scan(out=out, data0=d0, data1=d1, initial=init, op0=op0, op1=op1), R, W)

    def rmax(out, in_, R, W):
        P.op("dve", lambda e: e.reduce_max(out=out, in_=in_, axis=AX.X), R, W)

    def dma(q, out, in_, sem, R, W, slow=False):
        semfor(sem)
        if slow:
            P.dma(q, lambda e: e.dma_start(out=out, in_=in_, allow_slow_non_contiguous=True), sem, R, W)
        else:
            P.dma(q, lambda e: e.dma_start(out=out, in_=in_), sem, R, W)

    ident = CST[:, C_ID:C_ID + 128]
    mneg = CST[:, C_MNEG:C_MNEG + 128]
    m01 = CST[:, C_M01:C_M01 + 128]
    ones_f = CST[:, C_ONES:C_ONES + 128]
    zeros_f = CST[:, C_ZERO:C_ZERO + 128]
    NHALF = KC[:, 0:1]
    SIXT = KC[:, 1:2]
    PHALF = KC[:, 2:3]
    KCr = KC.r()

    dma("sp", CST[:], consts, "LDC", [], CST.r())
    cp("dve", IDB[:], ident, CST.r(), IDB.r())
    cp("act", ODIVR[:], CST[:, C_ODIV:C_ODIV + 128], CST.r(), ODIVR.r())
    memset("pool", KC[:, 0:1], -0.5, KC.r())
    memset("pool", KC[:, 1:2], 1.0 / 16.0, KC.r())
    memset("pool", KC[:, 2:3], 0.5, KC.r())
    memset("pool", ONESB[:], 1.0, ONESB.r())

    def pcol(dst_c0, src, nch, l):
        dma("pool", PRM[:, l, dst_c0:dst_c0 + nch], src.rearrange("(c p) -> p c", p=128), "PL", [], PRM.r(), slow=True)

    dma("pool", LNIN[:, 0:8], ln_in_g.rearrange("(c p) -> p c", p=128), "PL", [], PRM.r(), slow=True)
    dma("pool", LNIN[:, 8:16], ln_in_b.rearrange("(c p) -> p c", p=128), "PL", [], PRM.r(), slow=True)
    for l in range(NL):
        pcol(P_LN1G, ln1_g[l], 8, l); pcol(P_LN1B, ln1_b[l], 8, l)
        pcol(P_LN2G, ln2_g[l], 8, l); pcol(P_LN2B, ln2_b[l], 8, l)
        pcol(P_B1, mlp_b1[l], 32, l)
        pcol(P_B2A, mlp_b2[l], 8, l)
        for j in range(4):
            dma("pool", PRM[:, l, P_SCW:P_SCW + 24].rearrange("p (c j) -> p c j", j=4)[:, :, j],
                ssd_conv_w[l, j].rearrange("(c p) -> p c", p=128), "PL", [], PRM.r(), slow=True)
            dma("pool", PRM[:, l, P_RCW:P_RCW + 16].rearrange("p (c j) -> p c j", j=4)[:, :, j],
                rg_conv_w[l, j].rearrange("(c p) -> p c", p=128), "PL", [], PRM.r(), slow=True)
        pcol(P_SCB, ssd_conv_b[l], 6, l); pcol(P_SNW, ssd_norm_w[l], 4, l); pcol(P_MNW, mlstm_norm_w[l], 4, l)
        pcol(P_RCB, rg_conv_b[l], 4, l); pcol(P_RBA, rg_gate_a_b[l], 4, l); pcol(P_RBX, rg_gate_x_b[l], 4, l)
        pcol(P_RLAM, rg_lambda[l], 4, l); pcol(P_GGB, gla_gate_b[l], 2, l); pcol(P_GNW, gla_norm_w[l], 4, l)
        dma("pool", PS8[0:8, l, 0:1], ssd_dt_bias[l].rearrange("(h o) -> h o", o=1), "PL", [], PRM.r(), slow=True)
        dma("pool", PS8[0:8, l, 1:2], ssd_A_log[l].rearrange("(h o) -> h o", o=1), "PL", [], PRM.r(), slow=True)
        dma("pool", PS8[0:4, l, 2:3], mlstm_if_b[l, 0:4].rearrange("(h o) -> h o", o=1), "PL", [], PRM.r(), slow=True)
        dma("pool", PS8[0:4, l, 3:4], mlstm_if_b[l, 4:8].rearrange("(h o) -> h o", o=1), "PL", [], PRM.r(), slow=True)
        dma("pool", DBC[:, l, :], ssd_D[l].partition_broadcast(128), "PL", [], PRM.r(), slow=True)
    PR = PRM.r()
    for l in range(NL):
        ts("dve", PRM[:, l, P_B2A:P_B2A + 8], PRM[:, l, P_B2A:P_B2A + 8], 1.0 / ALPHA, ALU.mult, PR, PR)
        ts("dve", PRM[:, l, P_RBAH:P_RBAH + 4], PRM[:, l, P_RBA:P_RBA + 4], -1.0, ALU.mult, PR, PR)
        ts("dve", PRM[:, l, P_RBXH:P_RBXH + 4], PRM[:, l, P_RBX:P_RBX + 4], -1.0, ALU.mult, PR, PR)
        ts("dve", PRM[:, l, P_GGBH:P_GGBH + 2], PRM[:, l, P_GGB:P_GGB + 2], -1.0, ALU.mult, PR, PR)
        act(PRM[:, l, P_RC4:P_RC4 + 4], PRM[:, l, P_RLAM:P_RLAM + 4], AF.Exp, PR, PR, scale=-1.0)
        act(PRM[:, l, P_RC4:P_RC4 + 4], PRM[:, l, P_RC4:P_RC4 + 4], AF.Ln, PR, PR, bias=1.0)
        ts("dve", PRM[:, l, P_RC8:P_RC8 + 4], PRM[:, l, P_RC4:P_RC4 + 4], -8.0, ALU.mult, PR, PR)
        ts("dve", PRM[:, l, P_RC4:P_RC4 + 4], PRM[:, l, P_RC4:P_RC4 + 4], -16.0, ALU.mult, PR, PR)
        act(PS8[0:8, l, 1:2], PS8[0:8, l, 1:2], AF.Exp, PR, PR)
        ts("dve", PS8[0:8, l, 1:2], PS8[0:8, l, 1:2], -1.0, ALU.mult, PR, PR)
        ts("dve", PS8[0:4, l, 3:4], PS8[0:4, l, 3:4], -1.0, ALU.mult, PR, PR)
        memset("pool", STG[:], 0.0, STG.r())
        for ax, wsrc in enumerate((rg_gate_a_w, rg_gate_x_w)):
            for n in range(8):
                hh = n % 2
                dma("pool", STG[hh * 64:(hh + 1) * 64, ax * 4 + n // 2, hh * 64:(hh + 1) * 64], wsrc[l, n], "SG", [], STG.r())
        cp("dve", RGW[:, l, :, :], STG[:], STG.r(), RGW.r())
        dma("pool", STG2[0:16, 0:256], gla_gate_w2[l], "SG2", [], STG2.r())
        cp("dve", GW2[0:16, l, :], STG2[0:16, 0:256], STG2.r(), GW2.r())

    WBFR = [[Reg("wbf%d_%d" % (l, j)) for j in range(NSLAB)] for l in range(NL)]

    def slab_srcs(l, j):
        res = []
        if j < 12:
            src = w_in[l].rearrange("(k p) c -> p k c", p=128)
            for (off, c0, n) in WIN_SLABS[j]:
                res.append((8, 512, off, n, src[:, :, c0:c0 + n]))
        elif j < 16:
            jj = j - 12
            res.append((16, 256, 0, 256, w_out[l].rearrange("(k p) c -> p k c", p=128)[:, :, jj * 256:(jj + 1) * 256]))
        elif j < 24:
            jj = j - 16
            res.append((8, 512, 0, 512, mlp_w1[l].rearrange("(k p) c -> p k c", p=128)[:, :, jj * 512:(jj + 1) * 512]))
        else:
            jj = j - 24
            res.append((32, 128, 0, 128, mlp_w2[l].rearrange("(k p) c -> p k c", p=128)[:, :, jj * 128:(jj + 1) * 128]))
        return res

    MIXF = MIXflat.bitcast(F32)
    XTF = XT.t[:].rearrange("p a b -> p (a b)")
    STAGE = [(HTF[:, 0:4096], HT.r(list(range(0, 16)))), (HTF[:, 4096:8192], HT.r(list(range(16, 32)))),
             (MIXF[:, 0:4096], MIX.r()), (XTF[:, 0:4096], XT.r())]
    pl_list = [(l, j) for l in range(NL) for j in range(NSLAB)]
    PD = 3
    for i in range(len(pl_list) + PD):
        if i < len(pl_list):
            l, j = pl_list[i]
            stg, sreg = STAGE[i % 4]
            if j == 1:
                memset("pool", stg, 0.0, sreg)
            for (kk, cw, off, n, src) in slab_srcs(l, j):
                dstv = stg.rearrange("p (k c) -> p k c", k=kk)[:, :, off:off + n]
                dma("sp", dstv, src, "LDS%d" % (i % 4), [], sreg)
        n_pl = i - PD
        if n_pl >= 0:
            l, j = pl_list[n_pl]
            stg, sreg = STAGE[n_pl % 4]
            slot = WR[n_pl % NSLOT]
            cp("act", slot[:, 0:2048], stg[:, 0:2048], sreg, slot.r())
            cp("dve", slot[:, 2048:4096], stg[:, 2048:4096], sreg, slot.r())
            if j == 1:
                cp("pool", WSM[:, l, :, :], slot[:, :].rearrange("p (k c) -> p k c", k=8)[:, :, 256:288], slot.r(), WSM.r())
            dma("sp", wbf[l, j], slot[:], "W%d" % (n_pl % NSLOT), slot.r(), [WBFR[l][j]])

    slab_seq = []
    wstate = {"issued": 0, "released": 0}

    def pump():
        while wstate["issued"] < len(slab_seq) and wstate["issued"] < wstate["released"] + NSLOT:
            i = wstate["issued"]
            l, j = slab_seq[i]
            slot = WR[i % NSLOT]
            dma("sp", slot[:], wbf[l, j], "W%d" % (i % NSLOT), [WBFR[l][j]], slot.r())
            wstate["issued"] += 1

    def slab(i):
        pump()
        assert i < wstate["issued"], (i, wstate)
        assert i >= wstate["released"], (i, wstate)
        return WR[i % NSLOT]

    def release_upto(i):
        if i + 1 > wstate["released"]:
            wstate["released"] = i + 1
        pump()

    def proj_fm(wslot, c0, M, T, bank=None):
        b = bank or psD()
        wv = wslot[:, :].rearrange("p (k c) -> p k c", k=8)
        for k in range(8):
            mm(b[0:M, 0:T], wv[:, k, c0:c0 + M], XB[:, k, 0:T], k == 0, k == 7, wslot.r() + XB.r(k), qr(b, 0, T))
        return b

    def proj_wsm(l, c0, M, T):
        b = psD()
        for k in range(8):
            mm(b[0:M, 0:T], WSM[:, l, k, c0:c0 + M], XB[:, k, 0:T], k == 0, k == 7, WSM.r() + XB.r(k), qr(b, 0, T))
        return b

    def proj_tm(wslot, c0, ncols, tc0, L, bank=None):
        b = bank or psD()
        wv = wslot[:, :].rearrange("p (k c) -> p k c", k=8)
        for k in range(8):
            mm(b[0:L, 0:ncols], XB[:, k, tc0:tc0 + L], wv[:, k, c0:c0 + ncols], k == 0, k == 7,
               wslot.r() + XB.r(k), qr(b, 0, ncols))
        return b

    def layer_norm_fm(l, T, RR, gcol, bcol, last):
        bm = psM(); bq = psM()
        for k in range(8):
            t1 = TMPR[(2 * k) % 4]; t2 = TMPR[(2 * k + 1) % 4]
            cp("act", t1[:, 0:T], RR[:, k, 0:T], RR.r(k), t1.r())
            tt("dve", t2[:, 0:T], RR[:, k, 0:T], RR[:, k, 0:T], ALU.mult, RR.r(k), t2.r())
            mm(bm[:, 0:T], ODIVR[:, :], t1[:, 0:T], k == 0, k == 7, ODIVR.r() + t1.r(), qr(bm, 0, T))
            mm(bq[:, 0:T], ODIVR[:, :], t2[:, 0:T], k == 0, k == 7, ODIVR.r() + t2.r(), qr(bq, 0, T))
        cp("act", LNS[:, 0, 0:T], bm[:, 0:T], qr(bm, 0, T), LNS.r(0))
        tt("pool", LNS[:, 2, 0:T], LNS[:, 0, 0:T], LNS[:, 0, 0:T], ALU.mult, LNS.r(0), LNS.r(2))
        stt("dve", LNS[:, 1, 0:T], bq[:, 0:T], EPS / (ALPHA * ALPHA), LNS[:, 2, 0:T], ALU.add, ALU.subtract,
            qr(bq, 0, T) + LNS.r(2), LNS.r(1))
        rsq(LNS[:, 1, 0:T], LNS.r(1))
        stt("dve", LNS[:, 2, 0:T], LNS[:, 0, 0:T], -1.0, LNS[:, 1, 0:T], ALU.mult, ALU.mult, LNS.r([0, 1]), LNS.r(2))
        for k in range(8):
            tt("dve", RR[:, k, 0:T], RR[:, k, 0:T], LNS[:, 1, 0:T], ALU.mult, RR.r(k) + LNS.r(1), RR.r(k))
            tt("pool" if k % 2 else "dve", RR[:, k, 0:T], RR[:, k, 0:T], LNS[:, 2, 0:T], ALU.add, RR.r(k) + LNS.r(2), RR.r(k))
            act(XT[:, k, 0:T], RR[:, k, 0:T], AF.Identity, RR.r(k) + PR, XT.r(k),
                bias=PRM[:, l, bcol + k:bcol + k + 1], scale=PRM[:, l, gcol + k:gcol + k + 1])
            if not last:
                cp("pool" if k % 2 else "dve", XB[:, k, 0:T], XT[:, k, 0:T], XT.r(k), XB.r(k))

    def load_tile_and_ln_in(src, chunks, T):
        for ci, (c0, L) in enumerate(chunks):
            dma("sp", XIO[0:L, ci, :], src[c0:c0 + L, :], "XI%d" % ci, [], XIO.r(ci))
        for ci, (c0, L) in enumerate(chunks):
            col = COL[ci % 2]
            memset("pool", col[0:L, 0:16], 0.0, col.r())
            for hh in range(2):
                act(LNS[0:L, 0, :], XIO[0:L, ci, hh * 512:(hh + 1) * 512], AF.Copy, XIO.r(ci), LNS.r(0) + col.r(), accum=col[0:L, 8 + hh:9 + hh])
                act(LNS[0:L, 1, :], XIO[0:L, ci, hh * 512:(hh + 1) * 512], AF.Square, XIO.r(ci), LNS.r(1) + col.r(), accum=col[0:L, 10 + hh:11 + hh])
            tt("dve", col[0:L, 0:1], col[0:L, 8:9], col[0:L, 9:10], ALU.add, col.r(), col.r())
            tt("dve", col[0:L, 1:2], col[0:L, 10:11], col[0:L, 11:12], ALU.add, col.r(), col.r())
            ts("dve", col[0:L, 2:3], col[0:L, 0:1], 1.0 / 1024, ALU.mult, col.r(), col.r())
            tt("dve", col[0:L, 3:4], col[0:L, 2:3], col[0:L, 2:3], ALU.mult, col.r(), col.r())
            stt("dve", col[0:L, 4:5], col[0:L, 1:2], 1.0 / 1024, col[0:L, 3:4], ALU.mult, ALU.subtract, col.r(), col.r())
            ts("dve", col[0:L, 4:5], col[0:L, 4:5], EPS, ALU.add, col.r(), col.r())
            cp("dve", col[0:L, 5:6], col[0:L, 4:5], col.r(), col.r())
            rsq(col[0:L, 5:6], col.r())
            stt("dve", col[0:L, 6:7], col[0:L, 2:3], -1.0, col[0:L, 5:6], ALU.mult, ALU.mult, col.r(), col.r())
            ts("dve", XIO[0:L, ci, :], XIO[0:L, ci, :], col[0:L, 5:6], ALU.mult, XIO.r(ci) + col.r(), XIO.r(ci),
               s2=col[0:L, 6:7], op1=ALU.add)
            for half in range(2):
                b = psM()
                for kk in range(4):
                    k = half * 4 + kk
                    tr(b[:, kk * 128:kk * 128 + L], XIO[0:L, ci, k * 128:(k + 1) * 128], ident[0:L, 0:L],
                       XIO.r(ci) + CST.r(), qr(b, kk * 128, kk * 128 + L))
                for kk in range(4):
                    k = half * 4 + kk
                    act(XT[:, k, c0:c0 + L], b[:, kk * 128:kk * 128 + L], AF.Identity, qr(b, kk * 128, kk * 128 + L) + PR,
                        XT.r(k), bias=LNIN[:, 8 + k:9 + k], scale=LNIN[:, k:k + 1])
        for k in range(8):
            cp("dve" if k % 2 else "pool", XB[:, k, 0:T], XT[:, k, 0:T], XT.r(k), XB.r(k))

    def store_tile(dst, chunks, T):
        for ci, (c0, L) in enumerate(chunks):
            for half in range(2):
                b = psM()
                for kk in range(4):
                    k = half * 4 + kk
                    tr(b[0:L, kk * 128:(kk + 1) * 128], XT[:, k, c0:c0 + L], ident, XT.r(k) + CST.r(), qr(b, kk * 128, (kk + 1) * 128))
                cp("act" if half else "dve", XIO[0:L, ci, half * 512:(half + 1) * 512], b[0:L, :], b.r(), XIO.r(ci))
            dma("sp", dst[c0:c0 + L, :], XIO[0:L, ci, :], "XO%d" % ci, XIO.r(ci), [])

    def run_tile(src, dst, chunks, segs, T, is_sample, gi, fin):
        load_tile_and_ln_in(src, chunks, T)
        for l in range(NL):
            gi = run_layer(l, chunks, segs, T, is_sample, gi, fin)
        store_tile(dst, chunks, T)
        return gi

    def run_layer(l, chunks, segs, T, is_sample, gi, fin):
        if cfg.get("STOP", 0) == 2:
            return gi + NSLAB
        def SL(j):
            return slab(gi + j)
        SL.rel = lambda j: release_upto(gi + j)
        def interleave(gens):
            gens = list(gens)
            while gens:
                for g in list(gens):
                    try:
                        next(g)
                    except StopIteration:
                        gens.remove(g)
        PH = cfg.get("PH", 99)

        def cutoff():
            SL.rel(NSLAB - 1)
            return gi + NSLAB
        if PH == 0:
            return cutoff()
        ssd_fm(l, chunks, segs, T, is_sample, SL)
        if PH == 1:
            return cutoff()
        interleave([ssd_loop(l, chunks, segs, T, is_sample, SL, fin), rg_gen(l, chunks, segs, T, is_sample, SL, fin, 0),
                    rg_gen(l, chunks, segs, T, is_sample, SL, fin, 1)])
        rg_fin(l, fin)
        SL.rel(S_YR)
        if PH == 2:
            return cutoff()
        mlstm_fm(l, chunks, segs, T, is_sample, SL)
        gla_fm(l, chunks, segs, T, is_sample, SL)
        if PH == 3:
            return cutoff()
        interleave([mlstm_loop(l, chunks, segs, T, is_sample, SL, fin), gla_loop(l, chunks, segs, T, is_sample, SL, fin)])
        SL.rel(S_GG)
        if PH == 4:
            return cutoff()
        if DBG and l == 0 and not is_sample:
            for j in range(16):
                jv = LNS[:, j % 3, :]
                cp("dve", jv[:, 0:T], MIX[:, j, 0:T], MIX.r(j), LNS.r(j % 3))
                dma("sp", dbg[j * 128:(j + 1) * 128, 0:T], jv[:, 0:T], "DBG%d" % (j % 3), LNS.r(j % 3), [])
        for jj in range(4):
            ws = SL(12 + jj)
            wv = ws[:, :].rearrange("p (k c) -> p k c", k=16)
            for ee in range(2):
                e = jj * 2 + ee
                b = psD()
                for k in range(16):
                    mm(b[:, 0:T], wv[:, k, ee * 128:(ee + 1) * 128], MIX[:, k, 0:T], k == 0, k == 15,
                       ws.r() + MIX.r(k), qr(b, 0, T))
                stt("dve", RR1[:, e, 0:T], b[:, 0:T], 1.0 / ALPHA, XT[:, e, 0:T], ALU.mult, ALU.add,
                    qr(b, 0, T) + XT.r(e), RR1.r(e))
            SL.rel(12 + jj)
        layer_norm_fm(l, T, RR1, P_LN1G, P_LN1B, False)
        if PH == 5:
            return cutoff()
        for jj in range(8):
            ws = SL(16 + jj)
            wv = ws[:, :].rearrange("p (k c) -> p k c", k=8)
            for ff in range(4):
                f = jj * 4 + ff
                b = psD()
                for k in range(8):
                    mm(b[:, 0:T], wv[:, k, ff * 128:(ff + 1) * 128], XB[:, k, 0:T], k == 0, k == 7,
                       ws.r() + XB.r(k), qr(b, 0, T))
                tv = LNS[:, f % 3, :]
                act(tv[:, 0:T], b[:, 0:T], AF.Relu, qr(b, 0, T) + PR, LNS.r(f % 3), bias=PRM[:, l, P_B1 + f:P_B1 + f + 1])
                tt("pool" if f % 2 else "dve", HT[:, f, 0:T], tv[:, 0:T], tv[:, 0:T], ALU.mult, LNS.r(f % 3), HT.r(f))
            SL.rel(16 + jj)
        for e in range(8):
            ws = SL(24 + e)
            wv = ws[:, :].rearrange("p (k c) -> p k c", k=32)
            b = psD()
            for f in range(32):
                mm(b[:, 0:T], wv[:, f, :], HT[:, f, 0:T], f == 0, f == 31, ws.r() + HT.r(f), qr(b, 0, T))
            act(RR2[:, e, 0:T], b[:, 0:T], AF.Identity, qr(b, 0, T) + PR, RR2.r(e), bias=PRM[:, l, P_B2A + e:P_B2A + e + 1],
                scale=1.0 / ALPHA)
            tt("pool", RR2[:, e, 0:T], RR2[:, e, 0:T], XT[:, e, 0:T], ALU.add, RR2.r(e) + XT.r(e), RR2.r(e))
            SL.rel(24 + e)
        layer_norm_fm(l, T, RR2, P_LN2G, P_LN2B, l == NL - 1)
        return gi + NSLAB

    def to_mix(src_bf, L, c0, jbase, pcol0, l, bank=None):
        ap, cells = src_bf
        bo = bank or psM(); bob = pbf(bo)
        for j in range(4):
            tr(bob[:, j * 128:j * 128 + L], ap[0:L, j * 128:(j + 1) * 128], IDB[0:L, 0:L], cells + IDB.r(),
               qr(bo, j * 64, j * 64 + 64))
        for j in range(4):
            act(MIX[:, jbase + j, c0:c0 + L], bob[:, j * 128:j * 128 + L], AF.Identity, qr(bo, j * 64, j * 64 + 64) + PR,
                MIX.r(jbase + j), scale=PRM[:, l, pcol0 + j:pcol0 + j + 1])

    def conv_hist_in(kind, l, c, u, segs, is_sample):
        for si, (s0, Ls, seq) in enumerate(segs):
            base = si * (Ls + 3)
            if is_sample:
                src = (i_ssd_conv if kind == "ssd" else i_rgconv)[l, seq].rearrange("j (c p) -> p c j", p=128)[:, c, :]
                dma("pool", UX[:, u, base:base + 3], src, "UXH%d" % u, [], UX.r(u), slow=True)
            else:
                st = (CSS if kind == "ssd" else CSR)[l]
                cp("act", UX[:, u, base:base + 3], st[:, c, :], st.r(), UX.r(u))

    def conv_hist_out(kind, l, c, u, segs, is_sample):
        for si, (s0, Ls, seq) in enumerate(segs):
            base = si * (Ls + 3)
            if is_sample:
                dst = o_s["ssd_conv" if kind == "ssd" else "rgconv"][l, seq].rearrange("j (c p) -> p c j", p=128)[:, c, :]
                dma("pool", dst, UX[:, u, base + Ls:base + Ls + 3], "UXO%d" % u, UX.r(u), [], slow=True)
            else:
                st = (CSS if kind == "ssd" else CSR)[l]
                cp("dve", st[:, c, :], UX[:, u, base + Ls:base + Ls + 3], UX.r(u), st.r())

    def ssd_load_state(l, seq, pm=None):
        for h in range(8):
            g, hh = h // 4, h % 4
            dma("pool", HS[l][g * 64:(g + 1) * 64, hh * 64:(hh + 1) * 64], i_ssd_h[l, seq, h].rearrange("p n -> n p"),
                "SSI", [], HS[l].r(), slow=True)
        cp("dve", HSB[:, :], HS[l][:, :], HS[l].r(), HSB.r())

    def ssd_store_state(l, dst, pm=None):
        pm = pm or psM
        stg = TOK32[5]
        for g in range(2):
            b = pm()
            for hh in range(4):
                tr(b[0:64, hh * 64:(hh + 1) * 64], HS[l][g * 64:(g + 1) * 64, hh * 64:(hh + 1) * 64],
                   ident[g * 64:(g + 1) * 64, g * 64:(g + 1) * 64], HS[l].r() + CST.r(), qr(b, 0, 256))
            cp("dve", stg[0:64, g * 256:(g + 1) * 256], b[0:64, 0:256], b.r(), stg.r())
        dma("sp", dst.rearrange("h p n -> p h n"), stg[0:64, :].rearrange("p (h n) -> p h n", h=8), "SSO", stg.r(), [])

    def ssd_fm_chain(l, cs, segs, T, is_sample, SL):
        XC = FMB
        for c in cs:
            u = c % 2
            ws, c0w = (SL(S_XBC), c * 128) if c < 4 else (SL(S_BC), (c - 4) * 128)
            conv_hist_in("ssd", l, c, u, segs, is_sample)
            b = proj_fm(ws, c0w, 128, T, bank=PSB[c % 2])
            yield
            for si, (s0, Ls, seq) in enumerate(segs):
                base = si * (Ls + 3)
                cp("act", UX[:, u, base + 3:base + 3 + Ls], b[:, s0:s0 + Ls], qr(b, 0, T), UX.r(u))
            yield
            acc = FM32[c % 2]; th = FM32[2 + c % 2]
            for si, (s0, Ls, seq) in enumerate(segs):
                base = si * (Ls + 3)
                wc = P_SCW + c * 4
                ts("dve", acc[:, s0:s0 + Ls], UX[:, u, base:base + Ls], PRM[:, l, wc:wc + 1], ALU.mult, UX.r(u) + PR, acc.r(),
                   s2=PRM[:, l, P_SCB + c:P_SCB + c + 1], op1=ALU.add)
                for j in range(1, 4):
                    stt("dve", acc[:, s0:s0 + Ls], UX[:, u, base + j:base + j + Ls], PRM[:, l, wc + j:wc + j + 1],
                        acc[:, s0:s0 + Ls], ALU.mult, ALU.add, UX.r(u) + PR + acc.r(), acc.r())
                    yield
            act(th[:, 0:T], acc[:, 0:T], AF.Exp, acc.r(), th.r(), scale=-1.0)
            yield
            act(th[:, 0:T], th[:, 0:T], AF.Ln, th.r(), th.r(), bias=1.0)
            yield
            act(th[:, 0:T], th[:, 0:T], AF.Exp, th.r(), th.r(), scale=-1.0)
            yield
            tt("dve", XC[:, c, 0:T], th[:, 0:T], acc[:, 0:T], ALU.mult, th.r() + acc.r(), XC.r(c))
            conv_hist_out("ssd", l, c, u, segs, is_sample)
            yield

    def ssd_fm(l, chunks, segs, T, is_sample, SL):
        XC = FMB
        gens = [ssd_fm_chain(l, [0, 2, 4], segs, T, is_sample, SL), ssd_fm_chain(l, [1, 3, 5], segs, T, is_sample, SL)]
        while gens:
            for g in list(gens):
                try:
                    next(g)
                except StopIteration:
                    gens.remove(g)
        SL.rel(S_BC)
        for g in range(2):
            ts("dve", BCM[:, g, 0:T], XC[:, 4, 0:T], CST[:, C_HM + g:C_HM + g + 1], ALU.mult, XC.r(4) + CST.r(), BCM.r(g))
            ts("dve", BCM[:, 2 + g, 0:T], XC[:, 5, 0:T], CST[:, C_HM + g:C_HM + g + 1], ALU.mult, XC.r(5) + CST.r(), BCM.r(2 + g))
        b = proj_wsm(l, 0, 8, T)
        act(SM8[0:8, 0, 0:T], b[0:8, 0:T], AF.Exp, qr(b, 0, T) + PR, SM8.r(0), bias=PS8[0:8, l, 0:1])
        act(SM8[0:8, 0, 0:T], SM8[0:8, 0, 0:T], AF.Ln, SM8.r(0), SM8.r(0), bias=1.0)
        ts("dve", SM8[0:8, 1, 0:T], SM8[0:8, 0, 0:T], PS8[0:8, l, 1:2], ALU.mult, SM8.r(0) + PR, SM8.r(1))
        for (c0, L) in chunks:
            scan(SM8[0:8, 2, c0:c0 + L], ones_f[0:8, 0:L], SM8[0:8, 1, c0:c0 + L], 0.0, ALU.mult, ALU.add,
                 SM8.r(1) + CST.r(), SM8.r(2))

    def ssd_loop(l, chunks, segs, T, is_sample, SL, fin):
        XC = FMB
        pm = mkrot([3, 4, 5, 6, 7]); pd = mkrot([0])
        if not is_sample:
            cp("dve", HSB[:, :], HS[l][:, :], HS[l].r(), HSB.r())
        for ci, (c0, L) in enumerate(chunks):
            p = ci % 2
            col = COL[p]; cr = col.r()
            seq = segs[ci][2] if is_sample else None
            if is_sample:
                ssd_load_state(l, seq, pm)
            bz = proj_tm(SL(S_Z), 0, 512, c0, L, bank=pd())
            t2 = TOK32[2]
            sigm(t2[0:L, :], bz[0:L, :], bz.r(), t2.r())
            tt("dve", t2[0:L, :], t2[0:L, :], bz[0:L, :], ALU.mult, t2.r() + bz.r(), t2.r())
            yield
            bt = pm()
            tr(bt[0:L, 0:8], SM8[0:8, 2, c0:c0 + L], ident[0:8, 0:8], SM8.r(2) + CST.r(), qr(bt, 0, 16))
            tr(bt[0:L, 8:16], SM8[0:8, 0, c0:c0 + L], ident[0:8, 0:8], SM8.r(0) + CST.r(), qr(bt, 0, 16))
            cp("dve", col[0:L, 0:16], bt[0:L, 0:16], qr(bt, 0, 16), cr)
            bx = pm(); bxb = pbf(bx)
            for j in range(5):
                tr(bxb[0:L, j * 128:(j + 1) * 128], XC[:, j, c0:c0 + L], IDB[:, :], XC.r(j) + IDB.r(), qr(bx, j * 64, j * 64 + 64))
            xbf = TOKB[:, 0, :]; xbr = TOKB.r(0)
            btok = TOKB[:, 1, :]; btr = TOKB.r(1)
            cp("act", xbf[0:L, 0:512], bxb[0:L, 0:512], qr(bx, 0, 256), xbr)
            cp("act", btok[0:L, 0:128], bxb[0:L, 512:640], qr(bx, 256, 320), btr)
            yield
            bcb = pm()
            for g in range(2):
                mm(bcb[0:L, g * 128:g * 128 + L], BCM[:, g, c0:c0 + L], XC[:, 5, c0:c0 + L],
                   True, True, BCM.r(g) + XC.r(5), qr(bcb, g * 128, g * 128 + L))
            bb = [pm(), pm()]
            for h in range(8):
                bk = bb[h // 4]; q = h % 4
                mm(bk[:, q * 128:q * 128 + L], CST[0:8, C_SEL + h * 128:C_SEL + (h + 1) * 128], SM8[0:8, 2, c0:c0 + L],
                   True, True, CST.r() + SM8.r(2), qr(bk, q * 128, q * 128 + L))
            byi = pm()
            for g in range(2):
                mm(byi[0:L, g * 256:(g + 1) * 256], BCM[:, 2 + g, c0:c0 + L], HSB[:, 0:256],
                   True, True, BCM.r(2 + g) + HSB.r(), qr(byi, g * 256, (g + 1) * 256))
            for half in range(2):
                bk = bb[half]
                tt("dve", col[0:L, 24 + half * 4:28 + half * 4], bk[0:L, :].rearrange("p (h t) -> p h t", h=4)[:, :, L - 1],
                   col[0:L, half * 4:half * 4 + 4], ALU.subtract, bk.r() + cr, cr)
                act(col[:, 32 + half * 4:36 + half * 4], bk[:, :].rearrange("p (h t) -> p h t", h=4)[:, :, L - 1], AF.Exp, bk.r(), cr)
            act(col[0:L, 24:32], col[0:L, 24:32], AF.Exp, cr, cr)
            tt("dve", col[0:L, 24:32], col[0:L, 24:32], col[0:L, 8:16], ALU.mult, cr, cr)
            xw = TOKB[:, 2, :]; xwr = TOKB.r(2)
            tt("dve", xw[0:L, 0:512].rearrange("p (h j) -> p h j", h=8), xbf[0:L, 0:512].rearrange("p (h j) -> p h j", h=8),
               col[0:L, 24:32].unsqueeze(2).to_broadcast([L, 8, 64]), ALU.mult, xbr + cr, xwr)
            yield
            bd = pm()
            for g in range(2):
                mm(bd[g * 64:(g + 1) * 64, 0:256], btok[0:L, g * 64:(g + 1) * 64], xw[0:L, g * 256:(g + 1) * 256], True, True,
                   btr + xwr, qr(bd, 0, 256))
            for g in range(2):
                rows = slice(g * 64, (g + 1) * 64)
                tt("dve", HS[l][rows, :].rearrange("p (h j) -> p h j", h=4), HS[l][rows, :].rearrange("p (h j) -> p h j", h=4),
                   col[rows, 32 + g * 4:36 + g * 4].unsqueeze(2).to_broadcast([64, 4, 64]), ALU.mult, HS[l].r() + cr, HS[l].r())
                tt("dve", HS[l][rows, :], HS[l][rows, :], bd[rows, 0:256], ALU.add, HS[l].r() + qr(bd, 0, 256), HS[l].r())
            yield
            cp("dve", HSB[:, :], HS[l][:, :], HS[l].r(), HSB.r())
            yield
            sc = SC[p]
            for h in range(8):
                bk = bb[h // 4]; q = h % 4
                stt("dve", TK[0:L, h, 0:L], bk[0:L, q * 128:q * 128 + L], col[0:L, h:h + 1], mneg[0:L, 0:L],
                    ALU.subtract, ALU.add, qr(bk, q * 128, q * 128 + L) + cr + CST.r(), TK.r())
            yield
            act(TK[0:L, :, 0:L], TK[0:L, :, 0:L], AF.Exp, TK.r(), TK.r())
            yield
            for h in range(8):
                g = h // 4
                stt("dve", sc[0:L, h, 0:L], bcb[0:L, g * 128:g * 128 + L], col[0:L, 8 + h:9 + h], TK[0:L, h, 0:L],
                    ALU.mult, ALU.mult, qr(bcb, g * 128, g * 128 + L) + cr + TK.r(), sc.r())
            yield
            by = pm()
            for h in range(8):
                mm(by[0:L, h * 64:(h + 1) * 64], sc[0:L, h, 0:L], xbf[0:L, h * 64:(h + 1) * 64], True, True,
                   sc.r() + xbr, qr(by, h * 64, (h + 1) * 64))
            yield
            act(col[0:L, 16:24], col[0:L, 0:8], AF.Exp, cr, cr)
            t0 = TOK32[0]; t1 = TOK32[1]; t2 = TOK32[2]
            tt("dve", t0[0:L, :].rearrange("p (h j) -> p h j", h=8), byi[0:L, :].rearrange("p (h j) -> p h j", h=8),
               col[0:L, 16:24].unsqueeze(2).to_broadcast([L, 8, 64]), ALU.mult, byi.r() + cr, t0.r())
            tt("dve", t0[0:L, :], t0[0:L, :], by[0:L, :], ALU.add, t0.r() + by.r(), t0.r())
            tt("dve", t1[0:L, :].rearrange("p (h j) -> p h j", h=8), xbf[0:L, 0:512].rearrange("p (h j) -> p h j", h=8),
               DBC[0:L, l, :].unsqueeze(2).to_broadcast([L, 8, 64]), ALU.mult, xbr + PR, t1.r())
            tt("dve", t0[0:L, :], t0[0:L, :], t1[0:L, :], ALU.add, t0.r() + t1.r(), t0.r())
            tt("dve", t0[0:L, :], t0[0:L, :], t2[0:L, :], ALU.mult, t0.r() + t2.r(), t0.r())
            yield
            memset("pool", col[0:L, 40:42], 0.0, cr)
            act(t1[0:L, :], t0[0:L, :], AF.Square, t0.r(), t1.r() + cr, accum=col[0:L, 40:41])
            ts("dve", col[0:L, 41:42], col[0:L, 40:41], 1.0 / 512, ALU.mult, cr, cr, s2=EPS, op1=ALU.add)
            rsq(col[0:L, 41:42], cr)
            gn = TOKB[:, 3, :]; gnr = TOKB.r(3)
            ts("dve", gn[0:L, 0:512], t0[0:L, :], col[0:L, 41:42], ALU.mult, t0.r() + cr, gnr)
            yield
            to_mix((gn, gnr), L, c0, 0, P_SNW, l, bank=pm())
            yield
            if is_sample:
                ssd_store_state(l, o_s["ssd_h"][l, seq], pm)
                yield
        if fin:
            ssd_store_state(l, o_p["ssd_h"][l], pm)
            for c in range(6):
                dma("pool", o_p["ssd_conv"][l].rearrange("j (c p) -> p c j", p=128)[:, c, :], CSS[l][:, c, :], "FSO", CSS[l].r(), [], slow=True)

    def em_bcast(l, dstcol, pm=None):
        ts("dve", EMT[0:4, 4:8], ident[0:4, 0:4], EM[l][0:4, 0:1], ALU.mult, CST.r() + EM[l].r(), EMT.r())
        b = (pm or psM)()
        mm(b[:, 0:4], ones_f[0:4, 0:128], EMT[0:4, 4:8], True, True, CST.r() + EMT.r(), qr(b, 0, 4))
        return b

    def mlstm_load_state(l, seq):
        stg = TOK32W
        dma("sp", stg[:, :, 0:128], i_mC[l, seq].rearrange("h d v -> d h v"), "MSI", [], stg.r())
        dma("pool", stg[:, :, 128], i_mn[l, seq].rearrange("h d -> d h"), "MSIp", [], stg.r(), slow=True)
        dma("pool", MB[:, 0:4], i_mm[l, seq].partition_broadcast(128), "MSI2", [], MB.r(), slow=True)
        dma("pool", EM[l][0:4, 0:1], i_mm[l, seq].rearrange("(h o) -> h o", o=1), "MSI3", [], EM[l].r(), slow=True)
        act(MB[:, 0:4], MB[:, 0:4], AF.Exp, MB.r(), MB.r())
        act(EM[l][0:4, 0:1], EM[l][0:4, 0:1], AF.Exp, EM[l].r(), EM[l].r())
        tt("dve", CM[l][:, :, 0:129], stg[:, :, 0:129], MB[:, 0:4].unsqueeze(2).to_broadcast([128, 4, 129]), ALU.mult,
           stg.r() + MB.r(), CM[l].r())
        cp("pool", CMB[:, :, 0:129], CM[l][:, :, 0:129], CM[l].r(), CMB.r())

    def mlstm_store_state(l, dC, dn, dm, pm=None):
        b = em_bcast(l, None, pm)
        recip(MB[:, 4:8], b[:, 0:4], qr(b, 0, 4), MB.r())
        stg = TOK32W
        tt("dve", stg[:, :, 0:129], CM[l][:, :, 0:129], MB[:, 4:8].unsqueeze(2).to_broadcast([128, 4, 129]), ALU.mult,
           CM[l].r() + MB.r(), stg.r())
        dma("sp", dC.rearrange("h d v -> d h v"), stg[:, :, 0:128], "MSO", stg.r(), [])
        dma("pool", dn.rearrange("h d -> d h"), stg[:, :, 128], "MSOp", stg.r(), [], slow=True)
        act(EMT[0:4, 8:9], EM[l][0:4, 0:1], AF.Ln, EM[l].r(), EMT.r())
        dma("pool", dm.rearrange("(h o) -> h o", o=1), EMT[0:4, 8:9], "MSO2", EMT.r(), [], slow=True)

    def mlstm_fm(l, chunks, segs, T, is_sample, SL):
        for h in range(4):
            b = proj_fm(SL(S_MQ), h * 128, 128, T)
            cp("act", FMB[:, h, 0:T], b[:, 0:T], qr(b, 0, T), FMB.r(h))
        SL.rel(S_MQ)
        for h in range(4):
            b = proj_fm(SL(S_MK), h * 128, 128, T)
            act(FMB[:, 4 + h, 0:T], b[:, 0:T], AF.Identity, qr(b, 0, T), FMB.r(4 + h), scale=float(128 ** -0.5))
        SL.rel(S_MK)
        bi = proj_wsm(l, 8, 4, T)
        act(SM8[0:4, 0, 0:T], bi[0:4, 0:T], AF.Exp, qr(bi, 0, T) + PR, SM8.r(0), bias=PS8[0:4, l, 2:3])
        bf_ = proj_wsm(l, 12, 4, T)
        sigm(SM8[0:4, 1, 0:T], bf_[0:4, 0:T], qr(bf_, 0, T) + PR, SM8.r(1), nbias=PS8[0:4, l, 3:4])
        for (c0, L) in chunks:
            scan(SM8[0:4, 3, c0:c0 + L], SM8[0:4, 1, c0:c0 + L], zeros_f[0:4, 0:L], 1.0, ALU.mult, ALU.add,
                 SM8.r(1) + CST.r(), SM8.r(3))
        recip(SM8[0:4, 2, 0:T], SM8[0:4, 3, 0:T], SM8.r(3), SM8.r(2))
        tt("dve", SM8[0:4, 2, 0:T], SM8[0:4, 2, 0:T], SM8[0:4, 0, 0:T], ALU.mult, SM8.r([0, 2]), SM8.r(2))

    def mlstm_loop(l, chunks, segs, T, is_sample, SL, fin):
        pm = mkrot([3, 4, 5, 6]); pd = mkrot([0])
        if not is_sample:
            cp("dve", CMB[:, :, 0:129], CM[l][:, :, 0:129], CM[l].r(), CMB.r())
        for ci, (c0, L) in enumerate(chunks):
            p = ci % 2
            col = COL[p]; cr = col.r()
            seq = segs[ci][2] if is_sample else None
            if is_sample:
                mlstm_load_state(l, seq)
            yield
            bt = pm()
            tr(bt[0:L, 0:4], SM8[0:4, 2, c0:c0 + L], ident[0:4, 0:4], SM8.r(2) + CST.r(), qr(bt, 0, 8))
            tr(bt[0:L, 4:8], SM8[0:4, 3, c0:c0 + L], ident[0:4, 0:4], SM8.r(3) + CST.r(), qr(bt, 0, 8))
            cp("dve", col[0:L, 0:8], bt[0:L, 0:8], qr(bt, 0, 8), cr)
            rmax(EMT[0:4, 0:1], SM8[0:4, 2, c0:c0 + L], SM8.r(2), EMT.r())
            tt("dve", EM[l][0:4, 0:1], EM[l][0:4, 0:1], EMT[0:4, 0:1], ALU.max, EM[l].r() + EMT.r(), EM[l].r())
            tt("dve", EM[l][0:4, 0:1], EM[l][0:4, 0:1], SM8[0:4, 3, c0 + L - 1:c0 + L], ALU.mult, EM[l].r() + SM8.r(3), EM[l].r())
            ts("dve", EMT[0:4, 4:8], ident[0:4, 0:4], SM8[0:4, 3, c0 + L - 1:c0 + L], ALU.mult, CST.r() + SM8.r(3), EMT.r())
            bfl = pm()
            mm(bfl[:, 0:4], ones_f[0:4, 0:128], EMT[0:4, 4:8], True, True, CST.r() + EMT.r(), qr(bfl, 0, 4))
            cp("dve", col[:, 8:12], bfl[:, 0:4], qr(bfl, 0, 4), cr)
            yield
            bv = proj_tm(SL(S_MV), 0, 512, c0, L, bank=pd())
            va = TOKB[:, 1, :].rearrange("p (h v) -> p h v", h=4); var_ = TOKB.r(1)
            tt("dve", va[0:L, :, 0:128], bv[0:L, :].rearrange("p (h v) -> p h v", h=4),
               col[0:L, 0:4].unsqueeze(2).to_broadcast([L, 4, 128]), ALU.mult, bv.r() + cr, var_)
            cp("dve", va[0:L, :, 128:129], col[0:L, 0:4].unsqueeze(2), cr, var_)
            yield
            bo_ = proj_tm(SL(S_MO), 0, 512, c0, L, bank=pd())
            tho = TOK32[0]
            sigm(tho[0:L, :], bo_[0:L, :], bo_.r(), tho.r())
            yield
            bk_ = pm(); bkb = pbf(bk_)
            for h in range(4):
                tr(bkb[0:L, h * 128:(h + 1) * 128], FMB[:, 4 + h, c0:c0 + L], IDB[:, :], FMB.r(4 + h) + IDB.r(), qr(bk_, h * 64, h * 64 + 64))
            ktok = TOKB[:, 0, :]; ktr = TOKB.r(0)
            cp("act", ktok[0:L, 0:512], bkb[0:L, 0:512], qr(bk_, 0, 256), ktr)
            yield
            bs = pm()
            for h in range(4):
                mm(bs[0:L, h * 128:h * 128 + L], FMB[:, 4 + h, c0:c0 + L], FMB[:, h, c0:c0 + L], True, True,
                   FMB.r([h, 4 + h]), qr(bs, h * 128, h * 128 + L))
            sc = SC[p]
            tt("dve", sc[0:L, 0:4, 0:L], bs[0:L, :].rearrange("p (h t) -> p h t", h=4)[:, :, 0:L],
               m01[0:L, 0:L].unsqueeze(1).to_broadcast([L, 4, L]), ALU.mult, bs.r() + CST.r(), sc.r())
            yield
            by = [pm(), pm()]
            for h in range(4):
                bk2 = by[h // 2]; o = (h % 2) * 132
                mm(bk2[0:L, o:o + 129], sc[0:L, h, 0:L], va[0:L, h, 0:129], True, False, sc.r() + var_, qr(bk2, o, o + 129))
                mm(bk2[0:L, o:o + 129], FMB[:, h, c0:c0 + L], CMB[:, h, 0:129], False, True, FMB.r(h) + CMB.r(), qr(bk2, o, o + 129))
            yield
            for h in range(4):
                bk2 = by[h // 2]; o = (h % 2) * 132
                cp("dve", col[0:L, 12 + h:13 + h], bk2[0:L, o + 128:o + 129], qr(bk2, o, o + 129), cr)
            tt("dve", col[0:L, 16:20], col[0:L, 12:16], col[0:L, 4:8], ALU.mult, cr, cr)
            stt("dve", col[0:L, 16:20], col[0:L, 16:20], -1.0, col[0:L, 16:20], ALU.mult, ALU.max, cr, cr)
            ts("dve", col[0:L, 16:20], col[0:L, 16:20], 1.0, ALU.max, cr, cr)
            recip(col[0:L, 16:20], col[0:L, 16:20], cr, cr)
            tt("dve", col[0:L, 16:20], col[0:L, 16:20], col[0:L, 4:8], ALU.mult, cr, cr)
            yield
            memset("pool", col[0:L, 20:24], 0.0, cr)
            junk = TOK32[1]
            for h in range(4):
                bk2 = by[h // 2]; o = (h % 2) * 132
                act(junk[0:L, h * 128:(h + 1) * 128], bk2[0:L, o:o + 128], AF.Square, qr(bk2, o, o + 128), junk.r() + cr,
                    accum=col[0:L, 20 + h:21 + h])
            tt("dve", col[0:L, 24:28], col[0:L, 16:20], col[0:L, 16:20], ALU.mult, cr, cr)
            tt("dve", col[0:L, 24:28], col[0:L, 24:28], col[0:L, 20:24], ALU.mult, cr, cr)
            ts("dve", col[0:L, 24:28], col[0:L, 24:28], 1.0 / 128, ALU.mult, cr, cr, s2=EPS, op1=ALU.add)
            rsq(col[0:L, 24:28], cr)
            tt("dve", col[0:L, 24:28], col[0:L, 24:28], col[0:L, 16:20], ALU.mult, cr, cr)
            yield
            t2 = TOK32[4]
            for h in range(4):
                bk2 = by[h // 2]; o = (h % 2) * 132
                ts("dve", t2[0:L, h * 128:(h + 1) * 128], bk2[0:L, o:o + 128], col[0:L, 24 + h:25 + h], ALU.mult,
                   qr(bk2, o, o + 128) + cr, t2.r())
            mo = TOKB[:, 2, :]; mor = TOKB.r(2)
            tt("dve", mo[0:L, 0:512], tho[0:L, :], t2[0:L, :], ALU.mult, tho.r() + t2.r(), mor)
            yield
            to_mix((mo, mor), L, c0, 4, P_MNW, l, bank=pm())
            yield
            bd = [pm(), pm()]
            for h in range(4):
                bk2 = bd[h // 2]; o = (h % 2) * 132
                mm(bk2[:, o:o + 129], ktok[0:L, h * 128:(h + 1) * 128], va[0:L, h, 0:129], True, True, ktr + var_, qr(bk2, o, o + 129))
            for h in range(4):
                bk2 = bd[h // 2]; o = (h % 2) * 132
                tt("dve", CM[l][:, h, 0:129], CM[l][:, h, 0:129], bk2[:, o:o + 129], ALU.add, CM[l].r() + qr(bk2, o, o + 129), CM[l].r())
                ts("dve", CM[l][:, h, 0:129], CM[l][:, h, 0:129], col[:, 8 + h:9 + h], ALU.mult, CM[l].r() + cr, CM[l].r())
            yield
            cp("dve", CMB[:, :, 0:129], CM[l][:, :, 0:129], CM[l].r(), CMB.r())
            if is_sample:
                mlstm_store_state(l, o_s["mC"][l, seq], o_s["mn"][l, seq], o_s["mm"][l, seq], pm)
            yield
        if fin:
            mlstm_store_state(l, o_p["mC"][l], o_p["mn"][l], o_p["mm"][l], pm)

    def rg_gen(l, chunks, segs, T, is_sample, SL, fin, par=0):
        pd = mkrot([1 + par])
        if is_sample and par == 0:
            for si, (s0, Ls, seq) in enumerate(segs):
                dma("pool", RGHS[:, si, :], i_rgh[l, seq].rearrange("(c p) -> p c", p=128), "RGI", [], RGHS.r(), slow=True)
        for c in (par, par + 2):
            u = c % 2
            conv_hist_in("rg", l, c, u, segs, is_sample)
            b = proj_fm(SL(S_XR), c * 128, 128, T, bank=pd())
            yield
            for si, (s0, Ls, seq) in enumerate(segs):
                base = si * (Ls + 3)
                cp("act", UX[:, u, base + 3:base + 3 + Ls], b[:, s0:s0 + Ls], qr(b, 0, T), UX.r(u))
            xr = FM32[c % 2]
            for si, (s0, Ls, seq) in enumerate(segs):
                base = si * (Ls + 3)
                wc = P_RCW + c * 4
                ts("dve", xr[:, s0:s0 + Ls], UX[:, u, base:base + Ls], PRM[:, l, wc:wc + 1], ALU.mult, UX.r(u) + PR, xr.r(),
                   s2=PRM[:, l, P_RCB + c:P_RCB + c + 1], op1=ALU.add)
                for j in range(1, 4):
                    stt("dve", xr[:, s0:s0 + Ls], UX[:, u, base + j:base + j + Ls], PRM[:, l, wc + j:wc + j + 1],
                        xr[:, s0:s0 + Ls], ALU.mult, ALU.add, UX.r(u) + PR + xr.r(), xr.r())
            conv_hist_out("rg", l, c, u, segs, is_sample)
            yield
            xrb = FMB[:, 6 + c % 2, :]; xrbr = FMB.r(6 + c % 2)
            cp("act", xrb[:, 0:T], xr[:, 0:T], xr.r(), xrbr)
            ba = pd()
            mm(ba[:, 0:T], RGW[:, l, c, :], xrb[:, 0:T], True, True, RGW.r() + xrbr, qr(ba, 0, T))
            tha = FM32[2 + c % 2]
            yield
            sigm(tha[:, 0:T], ba[:, 0:T], qr(ba, 0, T) + PR, tha.r(), nbias=PRM[:, l, P_RBAH + c:P_RBAH + c + 1])
            yield
            a = FM32[4 + c % 2]
            act(a[:, 0:T], tha[:, 0:T], AF.Exp, tha.r() + PR, a.r(), scale=PRM[:, l, P_RC8 + c:P_RC8 + c + 1])
            if par == 0:
                sq = LNS[:, 0, :]; sqr = LNS.r(0)
            else:
                sq = TOK32[3]; sqr = TOK32[3].r()
            act(sq[:, 0:T], tha[:, 0:T], AF.Exp, tha.r() + PR, sqr, scale=PRM[:, l, P_RC4 + c:P_RC4 + c + 1])
            ts("dve", sq[:, 0:T], sq[:, 0:T], -1.0, ALU.mult, sqr, sqr, s2=1.0, op1=ALU.add)
            act(sq[:, 0:T], sq[:, 0:T], AF.Ln, sqr, sqr)
            act(sq[:, 0:T], sq[:, 0:T], AF.Exp, sqr, sqr, scale=0.5)
            yield
            bx = pd()
            mm(bx[:, 0:T], RGW[:, l, 4 + c, :], xrb[:, 0:T], True, True, RGW.r() + xrbr, qr(bx, 0, T))
            if par == 0:
                thx = LNS[:, 1, :]; thxr = LNS.r(1)
            else:
                thx = TOK32[4]; thxr = TOK32[4].r()
            sigm(thx[:, 0:T], bx[:, 0:T], qr(bx, 0, T) + PR, thxr, nbias=PRM[:, l, P_RBXH + c:P_RBXH + c + 1])
            tt("dve", thx[:, 0:T], thx[:, 0:T], xr[:, 0:T], ALU.mult, thxr + xr.r(), thxr)
            tt("dve", thx[:, 0:T], thx[:, 0:T], sq[:, 0:T], ALU.mult, thxr + sqr, thxr)
            yield
            hr = FM32[6 + c % 2]
            for si, (s0, Ls, seq) in enumerate(segs):
                if is_sample:
                    init = RGHS[:, si, c:c + 1]; ir = RGHS.r()
                else:
                    init = RGH[l][:, c:c + 1]; ir = RGH[l].r()
                scan(hr[:, s0:s0 + Ls], a[:, s0:s0 + Ls], thx[:, s0:s0 + Ls], init, ALU.mult, ALU.add, a.r() + thxr + ir, hr.r())
                if is_sample:
                    dma("pool", o_s["rgh"][l, seq].rearrange("(c p) -> p c", p=128)[:, c:c + 1], hr[:, s0 + Ls - 1:s0 + Ls],
                        "RGO%d" % (c % 2), hr.r(), [], slow=True)
                else:
                    cp("dve", RGH[l][:, c:c + 1], hr[:, s0 + Ls - 1:s0 + Ls], hr.r(), RGH[l].r())
            yield
            by = proj_fm(SL(S_YR), c * 128, 128, T, bank=pd())
            yield
            if par == 0:
                gy = LNS[:, 2, :]; gyr = LNS.r(2)
            else:
                gy = a; gyr = a.r()
            yv = FM32[2 + c % 2]
            cp("act", yv[:, 0:T], by[:, 0:T], qr(by, 0, T), yv.r())
            act(gy[:, 0:T], by[:, 0:T], AF.Square, qr(by, 0, T), gyr)
            ts("dve", gy[:, 0:T], gy[:, 0:T], 0.044715, ALU.mult, gyr, gyr, s2=1.0, op1=ALU.add)
            tt("dve", gy[:, 0:T], gy[:, 0:T], yv[:, 0:T], ALU.mult, gyr + yv.r(), gyr)
            yield
            act(gy[:, 0:T], gy[:, 0:T], AF.Exp, gyr, gyr, scale=-1.5957691216057308)
            act(gy[:, 0:T], gy[:, 0:T], AF.Ln, gyr, gyr, bias=1.0)
            act(gy[:, 0:T], gy[:, 0:T], AF.Exp, gyr, gyr, scale=-1.0)
            tt("dve", gy[:, 0:T], gy[:, 0:T], yv[:, 0:T], ALU.mult, gyr + yv.r(), gyr)
            tt("pool", MIX[:, 8 + c, 0:T], hr[:, 0:T], gy[:, 0:T], ALU.mult, hr.r() + gyr, MIX.r(8 + c))
            yield

    def rg_fin(l, fin):
        if fin:
            dma("pool", o_p["rgh"][l].rearrange("(c p) -> p c", p=128), RGH[l][:, :], "FSO", RGH[l].r(), [], slow=True)
            for c in range(4):
                dma("pool", o_p["rgconv"][l].rearrange("j (c p) -> p c j", p=128)[:, c, :], CSR[l][:, c, :], "FSO", CSR[l].r(), [], slow=True)

    def gla_load_state(l, seq):
        stg = FM32[6]
        dma("sp", stg[:, 0:256].rearrange("p (a v) -> p a v", a=2), i_gla[l, seq].rearrange("(a hh) d v -> (hh d) a v", hh=2),
            "GSI", [], stg.r())
        cp("dve", GS[l][:, :, :], stg[:, 0:256].rearrange("p (a v) -> p a v", a=2), stg.r(), GS[l].r())
        cp("pool", GSB[:, :, :], stg[:, 0:256].rearrange("p (a v) -> p a v", a=2), stg.r(), GSB.r())

    def gla_store_state(l, dst):
        dma("sp", dst.rearrange("(a hh) d v -> (hh d) a v", hh=2), GS[l][:, :, :], "GSO%d" % l, GS[l].r(), [])

    GQ = [(TMPB[0][:, pc * 512:(pc + 1) * 512], TMPB[0].r()) for pc in range(2)]
    GK = [(TMPB[1][:, pc * 512:(pc + 1) * 512], TMPB[1].r()) for pc in range(2)]
    GKD = [(TMPB[2][:, pc * 512:(pc + 1) * 512], TMPB[2].r()) for pc in range(2)]

    def gla_fm(l, chunks, segs, T, is_sample, SL):
        bag = proj_wsm(l, 16, 16, T)
        agb = TOKB[:, 3, :]; agr = TOKB.r(3)
        cp("act", agb[0:16, 0:T], bag[0:16, 0:T], qr(bag, 0, T), agr)
        Ac = [FM32[0], FM32[1]]; rAc = [FM32[2], FM32[3]]
        for pc in range(2):
            bl = psD()
            mm(bl[:, 0:T], GW2[0:16, l, pc * 128:(pc + 1) * 128], agb[0:16, 0:T], True, True, GW2.r() + agr, qr(bl, 0, T))
            th = FM32[4 + pc]
            act(th[:, 0:T], bl[:, 0:T], AF.Exp, qr(bl, 0, T) + PR, th.r(), bias=PRM[:, l, P_GGBH + pc:P_GGBH + pc + 1], scale=-1.0)
            act(th[:, 0:T], th[:, 0:T], AF.Ln, th.r(), th.r(), bias=1.0)
            act(th[:, 0:T], th[:, 0:T], AF.Exp, th.r(), th.r(), scale=-1.0 / 16.0)
            for (c0, L) in chunks:
                scan(Ac[pc][:, c0:c0 + L], th[:, c0:c0 + L], zeros_f[:, 0:L], 1.0, ALU.mult, ALU.add, th.r() + CST.r(), Ac[pc].r())
            recip(rAc[pc][:, 0:T], Ac[pc][:, 0:T], Ac[pc].r(), rAc[pc].r())
            bq = proj_fm(SL(S_GQK), pc * 128, 128, T)
            stt("dve", GQ[pc][0][:, 0:T], bq[:, 0:T], 0.125, Ac[pc][:, 0:T], ALU.mult, ALU.mult, qr(bq, 0, T) + Ac[pc].r(), GQ[pc][1])
            bk = proj_fm(SL(S_GQK), 256 + pc * 128, 128, T)
            tt("dve", GK[pc][0][:, 0:T], bk[:, 0:T], rAc[pc][:, 0:T], ALU.mult, qr(bk, 0, T) + rAc[pc].r(), GK[pc][1])
            for ci, (c0, L) in enumerate(chunks):
                stt("dve", GKD[pc][0][:, c0:c0 + L], bk[:, c0:c0 + L], Ac[pc][:, c0 + L - 1:c0 + L], rAc[pc][:, c0:c0 + L],
                    ALU.mult, ALU.mult, qr(bk, 0, T) + Ac[pc].r() + rAc[pc].r(), GKD[pc][1])
                cp("dve", GAL[:, pc, ci:ci + 1], Ac[pc][:, c0 + L - 1:c0 + L], Ac[pc].r(), GAL.r())
        SL.rel(S_GQK)
        for h in range(4):
            ts("dve", BCM[:, h, 0:T], GQ[h // 2][0][:, 0:T], CST[:, C_HM + h % 2:C_HM + h % 2 + 1], ALU.mult,
               GQ[h // 2][1] + CST.r(), BCM.r(h))

    def gla_loop(l, chunks, segs, T, is_sample, SL, fin):
        pm = mkrot([7, 2]); pd = mkrot([1])
        if not is_sample:
            cp("dve", GSB[:, :, :], GS[l][:, :, :], GS[l].r(), GSB.r())
        for ci, (c0, L) in enumerate(chunks):
            p = ci % 2
            col = COLG[p]; cr = col.r()
            seq = segs[ci][2] if is_sample else None
            if is_sample:
                gla_load_state(l, seq)
            yield
            bs = pm()
            for h in range(4):
                pc = h // 2; r0 = (h % 2) * 64
                mm(bs[0:L, h * 128:h * 128 + L], GK[pc][0][:, c0:c0 + L], BCM[:, h, c0:c0 + L], True, True,
                   GK[pc][1] + BCM.r(h), qr(bs, h * 128, h * 128 + L))
            yield
            sc = SCG[p]
            tt("dve", sc[0:L, 0:4, 0:L], bs[0:L, :].rearrange("p (h t) -> p h t", h=4)[:, :, 0:L],
               m01[0:L, 0:L].unsqueeze(1).to_broadcast([L, 4, L]), ALU.mult, bs.r() + CST.r(), sc.r())
            yield
            bv = proj_tm(SL(S_GV), 0, 512, c0, L, bank=pd())
            vbf = FM16[0]; vbr = FM16[0].r()
            cp("act", vbf[0:L, 0:512], bv[0:L, :], bv.r(), vbr)
            yield
            bo = pm()
            for h in range(4):
                pc = h // 2; r0 = (h % 2) * 64
                mm(bo[0:L, h * 128:(h + 1) * 128], sc[0:L, h, 0:L], vbf[0:L, h * 128:(h + 1) * 128], True, False,
                   sc.r() + vbr, qr(bo, h * 128, (h + 1) * 128))
                mm(bo[0:L, h * 128:(h + 1) * 128], BCM[:, h, c0:c0 + L], GSB[:, pc, :], False, True,
                   BCM.r(h) + GSB.r(), qr(bo, h * 128, (h + 1) * 128))
            yield
            memset("pool", col[0:L, 0:4], 0.0, cr)
            junk = FM32[4]
            for h in range(4):
                act(junk[0:L, h * 128:(h + 1) * 128], bo[0:L, h * 128:(h + 1) * 128], AF.Square, qr(bo, h * 128, (h + 1) * 128),
                    junk.r() + cr, accum=col[0:L, h:h + 1])
            ts("dve", col[0:L, 4:8], col[0:L, 0:4], 1.0 / 128, ALU.mult, cr, cr, s2=EPS, op1=ALU.add)
            rsq(col[0:L, 4:8], cr)
            yield
            bg = proj_tm(SL(S_GG), 0, 512, c0, L, bank=pd())
            yield
            thg = FM32[3]
            sigm(thg[0:L, :], bg[0:L, :], bg.r(), thg.r())
            tt("dve", thg[0:L, :], thg[0:L, :], bg[0:L, :], ALU.mult, thg.r() + bg.r(), thg.r())
            yield
            t2 = FM32[5]
            tt("dve", t2[0:L, :].rearrange("p (h v) -> p h v", h=4), bo[0:L, :].rearrange("p (h v) -> p h v", h=4),
               col[0:L, 4:8].unsqueeze(2).to_broadcast([L, 4, 128]), ALU.mult, bo.r() + cr, t2.r())
            go = FM16[2]; gor = FM16[2].r()
            tt("dve", go[0:L, 0:512], t2[0:L, :], thg[0:L, :], ALU.mult, t2.r() + thg.r(), gor)
            yield
            to_mix((go, gor), L, c0, 12, P_GNW, l, bank=pm())
            yield
            bkd = pm(); bkdb = pbf(bkd)
            for pc in range(2):
                tr(bkdb[0:L, pc * 128:(pc + 1) * 128], GKD[pc][0][:, c0:c0 + L], IDB[:, :], GKD[pc][1] + IDB.r(), qr(bkd, pc * 64, pc * 64 + 64))
            kdt = FM16[1]; kdr = FM16[1].r()
            cp("act", kdt[0:L, 0:256], bkdb[0:L, 0:256], qr(bkd, 0, 128), kdr)
            yield
            bd = pm()
            for h in range(4):
                pc = h // 2; r0 = (h % 2) * 64
                mm(bd[r0:r0 + 64, pc * 128:(pc + 1) * 128], kdt[0:L, pc * 128 + r0:pc * 128 + r0 + 64], vbf[0:L, h * 128:(h + 1) * 128],
                   True, True, kdr + vbr, qr(bd, pc * 128, (pc + 1) * 128))
            for pc in range(2):
                stt("dve", GS[l][:, pc, :], GS[l][:, pc, :], GAL[:, pc, ci:ci + 1], bd[:, pc * 128:(pc + 1) * 128],
                    ALU.mult, ALU.add, GS[l].r() + GAL.r() + qr(bd, pc * 128, (pc + 1) * 128), GS[l].r())
            yield
            cp("dve", GSB[:, :, :], GS[l][:, :, :], GS[l].r(), GSB.r())
            if is_sample:
                gla_store_state(l, o_s["gla"][l, seq])
            yield
        if fin:
            gla_store_state(l, o_p["gla"][l])

    tiles = []
    if SAMPLE:
        tiles.append(("s", None))
    for t in range(NT):
        tiles.append(("p", t))
    for kind, t in tiles:
        for l in range(NL):
            for j in range(NSLAB):
                slab_seq.append((l, j))

    def zero_states():
        for l in range(NL):
            memset("pool", HS[l][:], 0.0, HS[l].r())
            memset("pool", CSS[l][:], 0.0, CSS[l].r())
            memset("pool", CM[l][:], 0.0, CM[l].r())
            memset("pool", EM[l][:], 1.0, EM[l].r())
            memset("pool", RGH[l][:], 0.0, RGH[l].r())
            memset("pool", CSR[l][:], 0.0, CSR[l].r())
            memset("pool", GS[l][:], 0.0, GS[l].r())
        memset("pool", HSB[:], 0.0, HSB.r())
        memset("pool", CMB[:], 0.0, CMB.r())
        memset("pool", GSB[:], 0.0, GSB.r())

    zero_states()
    gi = 0
    STOP = cfg.get("STOP", 0)
    if STOP == 1:
        tiles = []
    for kind, t in tiles:
        if kind == "s":
            gi = run_tile(xs, ys, [(0, 16), (16, 16)], [(0, 16, 0), (16, 16, 1)], 32, True, gi, False)
            zero_states()
        else:
            gi = run_tile(xp[t * 512:(t + 1) * 512, :], yp[t * 512:(t + 1) * 512, :],
                          [(0, 128), (128, 128), (256, 128), (384, 128)], [(0, 512, None)], 512, False, gi, t == NT - 1)

    fin_toks = [(k, v) for k, v in P.cnt.items() if not k.startswith("E_") and v > 0]
    P.wait_all("sp", fin_toks)
    P.emit()
    P.close()
    return nc, P


_CACHE = {}


def _get_program(cfg_key, cfg):
    if cfg_key not in _CACHE:
        _CACHE[cfg_key] = build(cfg)
    return _CACHE[cfg_key]


WEIGHT_NAMES = ["ln_in_g", "ln_in_b", "w_in", "ssd_conv_w", "ssd_conv_b", "ssd_dt_bias", "ssd_A_log", "ssd_D", "ssd_norm_w",
                "mlstm_if_b", "mlstm_norm_w", "rg_conv_w", "rg_conv_b", "rg_gate_a_w", "rg_gate_a_b", "rg_gate_x_w",
                "rg_gate_x_b", "rg_lambda", "gla_gate_w2", "gla_gate_b", "gla_norm_w", "w_out", "ln1_g", "ln1_b",
                "mlp_w1", "mlp_b1", "mlp_w2", "mlp_b2", "ln2_g", "ln2_b"]
STATE_IN = [("state_ssd_h", "i_ssd_h"), ("state_ssd_conv", "i_ssd_conv"), ("state_mlstm_C", "i_mC"), ("state_mlstm_n", "i_mn"),
            ("state_mlstm_m", "i_mm"), ("state_rglru_h", "i_rgh"), ("state_rglru_conv", "i_rgconv"), ("state_gla_S", "i_gla")]
OUT_KEYS = ["ssd_h", "ssd_conv", "mC", "mn", "mm", "rgh", "rgconv", "gla"]


def run(inputs, cfg=None, ncores=NCORE):
    cfg = dict(cfg or {})
    NCORE_ = ncores
    NT = cfg.get("NT", SEQ // 512)
    nc, P = _get_program(tuple(sorted(cfg.items())), cfg)
    cst = make_consts()
    in_maps = []
    for c in range(NCORE_):
        m = {"xp": np.ascontiguousarray(inputs["x_prompt"][c, :NT * 512]),
             "xs": np.ascontiguousarray(inputs["x_sample"][2 * c:2 * c + 2].reshape(32, D)),
             "consts": cst}
        for src, dst in STATE_IN:
            m[dst] = np.ascontiguousarray(inputs[src][:, 2 * c:2 * c + 2])
        for w in WEIGHT_NAMES:
            m[w] = np.ascontiguousarray(inputs[w])
        in_maps.append(m)
    res = run_bass_kernel_spmd(nc, in_maps, core_ids=list(range(NCORE_)))
    R = res.results
    if NCORE_ < NCORE:
        R = list(R) + [R[0]] * (NCORE - NCORE_)
    y_prompt = np.stack([R[c]["yp"] for c in range(NCORE)], 0)
    y_sample = np.concatenate([R[c]["ys"].reshape(2, 16, D) for c in range(NCORE)], 0)
    outs = [y_prompt, y_sample]
    for k in OUT_KEYS:
        outs.append(np.stack([R[c]["p_" + k] for c in range(NCORE)], 1))
    for k in OUT_KEYS:
        outs.append(np.concatenate([R[c]["s_" + k] for c in range(NCORE)], 1))
    return tuple(np.asarray(o, np.float32) for o in outs), res


def kernel(**inputs):
    inputs = {k: np.asarray(v) for k, v in inputs.items()}
    outs, _ = run(inputs, {})
    return outs
```

```python
import numpy as np
import concourse.bass as bass
import concourse.mybir as mybir
from concourse.bass_utils import run_bass_kernel_spmd
from contextlib import ExitStack

F32 = mybir.dt.float32
F32R = mybir.dt.float32r
BF16 = mybir.dt.bfloat16
ALU = mybir.AluOpType
AF = mybir.ActivationFunctionType
AX = mybir.AxisListType

SAME_ENG_SYNC = True
ATTACH_WAIT = True

D = 1024
DEPTH = 4
SEQ = 8192
NCORE = 8
EPS = 1e-5
ALPHA = (2 * DEPTH) ** 0.25
D_IN = 5920
NSLAB = 32
NSLOT = 5

WIN_SLABS = [
    [(0, 512, 512)],
    [(0, 1024, 256), (256, 1280, 8), (264, 2824, 8), (272, 5392, 16)],
    [(0, 0, 512)],
    [(0, 3344, 512)],
    [(0, 3856, 512)],
    [(0, 1288, 512)],
    [(0, 1800, 512)],
    [(0, 4368, 512)],
    [(0, 2312, 512)],
    [(0, 2832, 512)],
    [(0, 4880, 512)],
    [(0, 5408, 512)],
]
S_XBC, S_BC, S_Z, S_XR, S_YR, S_MQ, S_MK, S_GQK, S_MV, S_MO, S_GV, S_GG = range(12)

C_ID, C_MNEG, C_M01, C_ODIV, C_ONES, C_SEL = 0, 128, 256, 384, 512, 640
C_ZERO = 640 + 1024
C_HM = C_ZERO + 128
C_N = C_HM + 4


def make_consts():
    c = np.zeros((128, C_N), np.float32)
    c[:, C_ID:C_ID + 128] = np.eye(128, dtype=np.float32)
    s = np.arange(128)[:, None]
    t = np.arange(128)[None, :]
    c[:, C_MNEG:C_MNEG + 128] = np.where(t >= s, 0.0, -30000.0)
    c[:, C_M01:C_M01 + 128] = np.where(t >= s, 1.0, 0.0)
    c[:, C_ODIV:C_ODIV + 128] = 1.0 / 1024.0
    c[:, C_ONES:C_ONES + 128] = 1.0
    for h in range(8):
        c[h, C_SEL + h * 128:C_SEL + (h + 1) * 128] = 1.0
    c[0:64, C_HM] = 1.0
    c[64:128, C_HM + 1] = 1.0
    return c


P_LN1G, P_LN1B, P_LN2G, P_LN2B, P_B1 = 0, 8, 16, 24, 32
P_SCW, P_SCB, P_SNW, P_MNW = 64, 88, 94, 98
P_RCW, P_RCB, P_RBA, P_RBX, P_RLAM, P_GGB, P_GNW = 102, 118, 122, 126, 130, 134, 136
P_SCWH, P_SCBH, P_RC4, P_RC8, P_RBAH, P_RBXH, P_GGBH = 140, 164, 170, 174, 178, 182, 186
P_B2A = 188
PN = 200


class Reg:
    __slots__ = ("name", "lw", "rd")

    def __init__(self, name):
        self.name = name
        self.lw = None
        self.rd = []


class Prog:
    ENGS = ("pe", "act", "dve", "pool", "sp")

    def __init__(self, nc):
        self.nc = nc
        self.es = ExitStack()
        self.streams = {e: [] for e in self.ENGS}
        self.cnt = {}
        self.sems = {}
        self.known = {e: {} for e in self.ENGS}
        for e in self.ENGS:
            self.newsem("E_" + e)

    def newsem(self, key):
        self.sems[key] = self.es.enter_context(self.nc.semaphore(key))
        self.cnt[key] = 0
        return key

    def sb(self, name, shape, dt):
        return self.es.enter_context(self.nc.sbuf_tensor(name, list(shape), dt))

    def ps(self, name, shape, dt):
        return self.es.enter_context(self.nc.psum_tensor(name, list(shape), dt))

    def _deps(self, eng, reads, writes):
        need = {}

        def add(tok):
            if tok is None:
                return
            k, v = tok
            if need.get(k, 0) < v:
                need[k] = v
        for r in reads:
            add(r.lw)
        for w in writes:
            add(w.lw)
            for t in w.rd:
                add(t)
        st = self.streams[eng]
        kn = self.known[eng]
        own = "E_" + eng
        for k, v in need.items():
            if k == own and (eng == "pe" or not SAME_ENG_SYNC):
                continue
            if kn.get(k, 0) >= v:
                continue
            kn[k] = v
            st.append(("w", k, v))

    def _mark(self, tok, reads, writes):
        for r in reads:
            r.rd.append(tok)
            if len(r.rd) > 48:
                d = {}
                for k, v in r.rd:
                    if d.get(k, 0) < v:
                        d[k] = v
                r.rd = list(d.items())
        for w in writes:
            w.lw = tok
            w.rd = []

    def op(self, eng, fn, reads=(), writes=()):
        self._deps(eng, reads, writes)
        key = "E_" + eng
        self.cnt[key] += 1
        tok = (key, self.cnt[key])
        self.streams[eng].append(("o", fn, key, 1))
        self._mark(tok, reads, writes)
        return tok

    def dma(self, eng, fn, semkey, reads=(), writes=()):
        self._deps(eng, reads, writes)
        self.cnt[semkey] += 16
        tok = (semkey, self.cnt[semkey])
        self.streams[eng].append(("o", fn, semkey, 16))
        self._mark(tok, reads, writes)
        return tok

    def wait_all(self, eng, toks):
        st = self.streams[eng]
        kn = self.known[eng]
        for k, v in toks:
            if kn.get(k, 0) >= v:
                continue
            kn[k] = v
            st.append(("w", k, v))

    def emit(self):
        nc = self.nc
        observed = {}
        for s in self.streams.values():
            for it in s:
                if it[0] == "w" and it[1].startswith("E_"):
                    observed.setdefault(it[1], set()).add(it[2])
        rank = {k: {v: i + 1 for i, v in enumerate(sorted(vs))} for k, vs in observed.items()}
        with nc.Block() as block:
            def mk(engname):
                def body(e):
                    n_op = 0
                    own = "E_" + engname
                    myrank = rank.get(own, {})
                    pend = []
                    for it in self.streams[engname]:
                        if it[0] == "w":
                            k, v = it[1], it[2]
                            if k.startswith("E_"):
                                v = rank[k][v]
                            pend.append((k, v))
                        else:
                            for (k, v) in pend[:-1]:
                                e.wait_ge(self.sems[k], v)
                            ins = it[1](e)
                            if pend:
                                k, v = pend[-1]
                                if ATTACH_WAIT:
                                    ins._wait_ge(self.sems[k], e.lower_val(v))
                                else:
                                    raise RuntimeError
                            pend = []
                            if it[2].startswith("E_"):
                                n_op += 1
                                if n_op in myrank:
                                    ins.then_inc(self.sems[it[2]], 1)
                            else:
                                ins.then_inc(self.sems[it[2]], it[3])
                    for (k, v) in pend:
                        e.wait_ge(self.sems[k], v)
                return body
            block.tensor(mk("pe"))
            block.scalar(mk("act"))
            block.vector(mk("dve"))
            block.gpsimd(mk("pool"))
            block.sync(mk("sp"))

    def close(self):
        self.es.close()

    def stats(self):
        return {e: (sum(1 for i in s if i[0] == "o"), sum(1 for i in s if i[0] == "w"))
                for e, s in self.streams.items()}


class TT:
    def __init__(self, P, name, shape, dt, ncell=1, psum=False):
        self.t = (P.ps if psum else P.sb)(name, shape, dt)
        self.c = [Reg("%s.%d" % (name, i)) for i in range(ncell)]

    def __getitem__(self, k):
        return self.t[k]

    def r(self, i=None):
        if i is None:
            return list(self.c)
        if isinstance(i, int):
            return [self.c[i]]
        return [self.c[j] for j in i]


def build(cfg):
    NL = cfg.get("NL", DEPTH)
    NT = cfg.get("NT", SEQ // 512)
    SAMPLE = cfg.get("SAMPLE", True)
    NTOKP = NT * 512
    DBG = cfg.get("DBG", False)

    nc = bass.Bass("TRN2", target_bir_lowering=False)
    P = Prog(nc)

    def din(name, shape, dt=F32):
        return nc.dram_tensor(name, list(shape), dt, kind="ExternalInput").ap()

    def dout(name, shape):
        return nc.dram_tensor(name, list(shape), F32, kind="ExternalOutput").ap()

    xp = din("xp", [NTOKP, D])
    xs = din("xs", [32, D])
    i_ssd_h = din("i_ssd_h", [DEPTH, 2, 8, 64, 64])
    i_ssd_conv = din("i_ssd_conv", [DEPTH, 2, 3, 768])
    i_mC = din("i_mC", [DEPTH, 2, 4, 128, 128])
    i_mn = din("i_mn", [DEPTH, 2, 4, 128])
    i_mm = din("i_mm", [DEPTH, 2, 4])
    i_rgh = din("i_rgh", [DEPTH, 2, 512])
    i_rgconv = din("i_rgconv", [DEPTH, 2, 3, 512])
    i_gla = din("i_gla", [DEPTH, 2, 4, 64, 128])
    consts = din("consts", [128, C_N])
    ln_in_g = din("ln_in_g", [D]); ln_in_b = din("ln_in_b", [D])
    w_in = din("w_in", [DEPTH, D, D_IN])
    ssd_conv_w = din("ssd_conv_w", [DEPTH, 4, 768]); ssd_conv_b = din("ssd_conv_b", [DEPTH, 768])
    ssd_dt_bias = din("ssd_dt_bias", [DEPTH, 8]); ssd_A_log = din("ssd_A_log", [DEPTH, 8]); ssd_D = din("ssd_D", [DEPTH, 8])
    ssd_norm_w = din("ssd_norm_w", [DEPTH, 512])
    mlstm_if_b = din("mlstm_if_b", [DEPTH, 8]); mlstm_norm_w = din("mlstm_norm_w", [DEPTH, 512])
    rg_conv_w = din("rg_conv_w", [DEPTH, 4, 512]); rg_conv_b = din("rg_conv_b", [DEPTH, 512])
    rg_gate_a_w = din("rg_gate_a_w", [DEPTH, 8, 64, 64]); rg_gate_a_b = din("rg_gate_a_b", [DEPTH, 512])
    rg_gate_x_w = din("rg_gate_x_w", [DEPTH, 8, 64, 64]); rg_gate_x_b = din("rg_gate_x_b", [DEPTH, 512])
    rg_lambda = din("rg_lambda", [DEPTH, 512])
    gla_gate_w2 = din("gla_gate_w2", [DEPTH, 16, 256]); gla_gate_b = din("gla_gate_b", [DEPTH, 256])
    gla_norm_w = din("gla_norm_w", [DEPTH, 512])
    w_out = din("w_out", [DEPTH, 2048, D])
    ln1_g = din("ln1_g", [DEPTH, D]); ln1_b = din("ln1_b", [DEPTH, D])
    mlp_w1 = din("mlp_w1", [DEPTH, D, 4096]); mlp_b1 = din("mlp_b1", [DEPTH, 4096])
    mlp_w2 = din("mlp_w2", [DEPTH, 4096, D]); mlp_b2 = din("mlp_b2", [DEPTH, D])
    ln2_g = din("ln2_g", [DEPTH, D]); ln2_b = din("ln2_b", [DEPTH, D])

    yp = dout("yp", [NTOKP, D])
    ys = dout("ys", [32, D])
    o_p = dict(ssd_h=dout("p_ssd_h", [DEPTH, 8, 64, 64]), ssd_conv=dout("p_ssd_conv", [DEPTH, 3, 768]),
               mC=dout("p_mC", [DEPTH, 4, 128, 128]), mn=dout("p_mn", [DEPTH, 4, 128]), mm=dout("p_mm", [DEPTH, 4]),
               rgh=dout("p_rgh", [DEPTH, 512]), rgconv=dout("p_rgconv", [DEPTH, 3, 512]), gla=dout("p_gla", [DEPTH, 4, 64, 128]))
    o_s = dict(ssd_h=dout("s_ssd_h", [DEPTH, 2, 8, 64, 64]), ssd_conv=dout("s_ssd_conv", [DEPTH, 2, 3, 768]),
               mC=dout("s_mC", [DEPTH, 2, 4, 128, 128]), mn=dout("s_mn", [DEPTH, 2, 4, 128]), mm=dout("s_mm", [DEPTH, 2, 4]),
               rgh=dout("s_rgh", [DEPTH, 2, 512]), rgconv=dout("s_rgconv", [DEPTH, 2, 3, 512]), gla=dout("s_gla", [DEPTH, 2, 4, 64, 128]))
    wbf = nc.dram_tensor("wbf", [NL, NSLAB, 128, 4096], BF16, kind="Internal").ap()
    dbg = dout("dbg_mix", [2048, 512]) if DBG else None

    def semfor(name):
        if name not in P.sems:
            P.newsem(name)
        return name

    class VW:
        def __init__(self, ap, cells_per_chunk):
            self.t = ap
            self.cc = cells_per_chunk

        def __getitem__(self, k):
            return self.t[k]

        def r(self, i=None):
            if i is None:
                out = []
                for c in self.cc:
                    out += c
                return out
            if isinstance(i, int):
                return list(self.cc[i])
            out = []
            for j in i:
                out += self.cc[j]
            return out

    CST = TT(P, "CST", [128, C_N], F32)
    IDB = TT(P, "IDB", [128, 128], BF16)
    ODIVR = TT(P, "ODIVR", [128, 128], F32R)
    KC = TT(P, "KC", [128, 4], F32)
    ONESB = TT(P, "ONESB", [128, 512], BF16)
    PRM = TT(P, "PRM", [128, NL, PN], F32)
    PS8 = TT(P, "PS8", [8, NL, 8], F32)
    DBC = TT(P, "DBC", [128, NL, 8], F32)
    RGW = TT(P, "RGW", [128, NL, 8, 128], BF16)
    GW2 = TT(P, "GW2", [16, NL, 256], BF16)
    LNIN = TT(P, "LNIN", [128, 16], F32)
    WSM = TT(P, "WSM", [128, NL, 8, 32], BF16)

    XT = TT(P, "XT", [128, 8, 512], F32, ncell=8)
    XB = TT(P, "XB", [128, 8, 512], BF16, ncell=8)
    MIX = TT(P, "MIX", [128, 16, 512], BF16, ncell=16)
    HT = TT(P, "HT", [128, 32, 512], BF16, ncell=32)
    TMPR = [TT(P, "TMPR%d" % i, [128, 512], F32R) for i in range(4)]
    LNS = TT(P, "LNS", [128, 3, 512], F32, ncell=3)
    WR = [TT(P, "WR%d" % i, [128, 4096], BF16) for i in range(NSLOT)]
    for i in range(NSLOT):
        P.newsem("W%d" % i)

    HTflat = HT.t[:].rearrange("p a b -> p (a b)")
    MIXflat = MIX.t[:].rearrange("p a b -> p (a b)")

    def aview(base, flat, byte_off, shape, dt, chunked=False):
        esz = 2 if dt == BF16 else 4
        n_el = 1
        for s_ in shape[1:]:
            n_el *= s_
        nb = n_el * esz
        v = flat[:, byte_off // 2:(byte_off + nb) // 2]
        if dt != BF16:
            v = v.bitcast(dt)
        if len(shape) == 3:
            v = v.rearrange("p (a b) -> p a b", a=shape[1])
        cells = base.c[byte_off // 1024:(byte_off + nb + 1023) // 1024]
        if chunked:
            n = shape[1]
            per = len(cells) // n
            cc = [cells[i * per:(i + 1) * per] for i in range(n)]
        else:
            cc = [cells]
        return VW(v, cc)

    FM32 = [aview(HT, HTflat, i * 2048, [128, 512], F32) for i in range(8)]
    TOK32 = [aview(HT, HTflat, 16384 + i * 2048, [128, 512], F32) for i in range(6)]
    TOK32W = aview(HT, HTflat, 16384 + 2 * 2048, [128, 4, 132], F32)
    TK = aview(HT, HTflat, 28672, [128, 8, 128], F32, chunked=False)
    STG = aview(HT, HTflat, 16384, [128, 8, 128], F32)
    STG2 = aview(HT, HTflat, 16384 + 4096, [128, 1024], F32)
    RR1 = aview(HT, HTflat, 0, [128, 8, 512], F32, chunked=True)
    RR2 = aview(MIX, MIXflat, 0, [128, 8, 512], F32, chunked=True)
    XIO = aview(MIX, MIXflat, 0, [128, 4, D], F32, chunked=True)
    HTF = HTflat.bitcast(F32)
    FM16 = [aview(HT, HTflat, i * 2048, [128, 1024], BF16) for i in range(8)]
    TMPB = [VW(LNS.t[:, i, :].bitcast(BF16), [LNS.r(i)]) for i in range(3)]

    FMB = TT(P, "FMB", [128, 8, 512], BF16, ncell=8)
    BCM = TT(P, "BCM", [128, 4, 512], BF16, ncell=4)
    UX = TT(P, "UX", [128, 2, 520], F32, ncell=2)
    SM8 = TT(P, "SM8", [16, 4, 512], F32, ncell=4)
    TOKB = TT(P, "TOKB", [128, 4, 528], BF16, ncell=4)
    SC = [TT(P, "SC%d" % i, [128, 8, 128], BF16) for i in range(2)]
    COL = [TT(P, "COL%d" % i, [128, 64], F32) for i in range(2)]
    SCG = [TT(P, "SCG%d" % i, [128, 4, 128], BF16) for i in range(2)]
    COLG = [TT(P, "COLG%d" % i, [128, 16], F32) for i in range(2)]
    GAL = TT(P, "GAL", [128, 2, 4], F32)
    EMT = TT(P, "EMT", [8, 16], F32)
    MB = TT(P, "MB", [128, 8], F32)
    RGHS = TT(P, "RGHS", [128, 2, 4], F32)

    HS = [TT(P, "HS%d" % l, [128, 256], F32) for l in range(NL)]
    HSB = TT(P, "HSB", [128, 256], BF16)
    CSS = [TT(P, "CSS%d" % l, [128, 6, 3], F32) for l in range(NL)]
    CM = [TT(P, "CM%d" % l, [128, 4, 132], F32) for l in range(NL)]
    CMB = TT(P, "CMB", [128, 4, 132], BF16)
    EM = [TT(P, "EM%d" % l, [4, 2], F32) for l in range(NL)]
    RGH = [TT(P, "RGH%d" % l, [128, 4], F32) for l in range(NL)]
    CSR = [TT(P, "CSR%d" % l, [128, 4, 3], F32) for l in range(NL)]
    GS = [TT(P, "GS%d" % l, [128, 2, 128], F32) for l in range(NL)]
    GSB = TT(P, "GSB", [128, 2, 128], BF16)

    PSB = [TT(P, "PSB%d" % i, [128, 512], F32, ncell=1, psum=True) for i in range(8)]
    pstate = {"d": 0, "m": 0}

    def psD():
        b = PSB[pstate["d"] % 3]
        pstate["d"] += 1
        return b

    def psM():
        b = PSB[3 + pstate["m"] % 5]
        pstate["m"] += 1
        return b

    def mkrot(idx):
        st = {"i": 0}

        def f():
            bk = PSB[idx[st["i"] % len(idx)]]
            st["i"] += 1
            return bk
        return f

    def qr(bank, c0, c1):
        return list(bank.c)

    def pbf(bank):
        return bank.t[:].bitcast(BF16)

    def tt(eng, out, a, b, op, R, W):
        P.op(eng, lambda e: e.tensor_tensor(out=out, in0=a, in1=b, op=op), R, W)

    def ts(eng, out, a, s1, op0, R, W, s2=None, op1=None):
        if s2 is None:
            P.op(eng, lambda e: e.tensor_scalar(out=out, in0=a, scalar1=s1, scalar2=None, op0=op0), R, W)
        else:
            P.op(eng, lambda e: e.tensor_scalar(out=out, in0=a, scalar1=s1, scalar2=s2, op0=op0, op1=op1), R, W)

    def stt(eng, out, a, s, b, op0, op1, R, W):
        P.op(eng, lambda e: e.scalar_tensor_tensor(out=out, in0=a, scalar=s, in1=b, op0=op0, op1=op1), R, W)

    def act(out, in_, func, R, W, bias=None, scale=None, accum=None):
        kw = {}
        if bias is not None:
            kw["bias"] = bias
        if scale is not None:
            kw["scale"] = scale
        if accum is not None:
            kw["accum_out"] = accum
        P.op("act", lambda e: e.activation(out=out, in_=in_, func=func, **kw), R, W)

    def sigm(out, in_, R, W, nbias=None):
        act(out, in_, AF.Exp, R, W, scale=-1.0, bias=nbias)
        act(out, out, AF.Ln, W, W, bias=1.0)
        act(out, out, AF.Exp, W, W, scale=-1.0)

    def rsq(x, R):
        act(x, x, AF.Ln, R, R)
        act(x, x, AF.Exp, R, R, scale=-0.5)

    def cp(eng, out, in_, R, W):
        if eng == "act":
            P.op("act", lambda e: e.activation(out=out, in_=in_, func=AF.Copy), R, W)
        else:
            P.op(eng, lambda e: e.tensor_copy(out=out, in_=in_), R, W)

    def mm(out, lhsT, rhs, st, sp, R, W):
        P.op("pe", lambda e: e.matmul(out, lhsT=lhsT, rhs=rhs, start=st, stop=sp), R, W)

    def tr(out, in_, idn, R, W):
        P.op("pe", lambda e: e.transpose(out, in_, idn), R, W)

    def memset(eng, ap, val, W):
        P.op(eng, lambda e: e.memset(ap, val), [], W)

    def recip(out, in_, R, W):
        P.op("dve", lambda e: e.reciprocal(out=out, in_=in_), R, W)

    def scan(out, d0, d1, init, op0, op1, R, W):
        P.op("dve", lambda e: e.tensor_tensor_scan(out=out, data0=d0, data1=d1, initial=init, op0=op0, op1=op1), R, W)

    def rmax(out, in_, R, W):
        P.op("dve", lambda e: e.reduce_max(out=out, in_=in_, axis=AX.X), R, W)

    def dma(q, out, in_, sem, R, W, slow=False):
        semfor(sem)
        if slow:
            P.dma(q, lambda e: e.dma_start(out=out, in_=in_, allow_slow_non_contiguous=True), sem, R, W)
        else:
            P.dma(q, lambda e: e.dma_start(out=out, in_=in_), sem, R, W)

    ident = CST[:, C_ID:C_ID + 128]
    mneg = CST[:, C_MNEG:C_MNEG + 128]
    m01 = CST[:, C_M01:C_M01 + 128]
    ones_f = CST[:, C_ONES:C_ONES + 128]
    zeros_f = CST[:, C_ZERO:C_ZERO + 128]
    NHALF = KC[:, 0:1]
    SIXT = KC[:, 1:2]
    PHALF = KC[:, 2:3]
    KCr = KC.r()

    dma("sp", CST[:], consts, "LDC", [], CST.r())
    cp("dve", IDB[:], ident, CST.r(), IDB.r())
    cp("act", ODIVR[:], CST[:, C_ODIV:C_ODIV + 128], CST.r(), ODIVR.r())
    memset("pool", KC[:, 0:1], -0.5, KC.r())
    memset("pool", KC[:, 1:2], 1.0 / 16.0, KC.r())
    memset("pool", KC[:, 2:3], 0.5, KC.r())
    memset("pool", ONESB[:], 1.0, ONESB.r())

    def pcol(dst_c0, src, nch, l):
        dma("pool", PRM[:, l, dst_c0:dst_c0 + nch], src.rearrange("(c p) -> p c", p=128), "PL", [], PRM.r(), slow=True)

    dma("pool", LNIN[:, 0:8], ln_in_g.rearrange("(c p) -> p c", p=128), "PL", [], PRM.r(), slow=True)
    dma("pool", LNIN[:, 8:16], ln_in_b.rearrange("(c p) -> p c", p=128), "PL", [], PRM.r(), slow=True)
    for l in range(NL):
        pcol(P_LN1G, ln1_g[l], 8, l); pcol(P_LN1B, ln1_b[l], 8, l)
        pcol(P_LN2G, ln2_g[l], 8, l); pcol(P_LN2B, ln2_b[l], 8, l)
        pcol(P_B1, mlp_b1[l], 32, l)
        pcol(P_B2A, mlp_b2[l], 8, l)
        for j in range(4):
            dma("pool", PRM[:, l, P_SCW:P_SCW + 24].rearrange("p (c j) -> p c j", j=4)[:, :, j],
                ssd_conv_w[l, j].rearrange("(c p) -> p c", p=128), "PL", [], PRM.r(), slow=True)
            dma("pool", PRM[:, l, P_RCW:P_RCW + 16].rearrange("p (c j) -> p c j", j=4)[:, :, j],
                rg_conv_w[l, j].rearrange("(c p) -> p c", p=128), "PL", [], PRM.r(), slow=True)
        pcol(P_SCB, ssd_conv_b[l], 6, l); pcol(P_SNW, ssd_norm_w[l], 4, l); pcol(P_MNW, mlstm_norm_w[l], 4, l)
        pcol(P_RCB, rg_conv_b[l], 4, l); pcol(P_RBA, rg_gate_a_b[l], 4, l); pcol(P_RBX, rg_gate_x_b[l], 4, l)
        pcol(P_RLAM, rg_lambda[l], 4, l); pcol(P_GGB, gla_gate_b[l], 2, l); pcol(P_GNW, gla_norm_w[l], 4, l)
        dma("pool", PS8[0:8, l, 0:1], ssd_dt_bias[l].rearrange("(h o) -> h o", o=1), "PL", [], PRM.r(), slow=True)
        dma("pool", PS8[0:8, l, 1:2], ssd_A_log[l].rearrange("(h o) -> h o", o=1), "PL", [], PRM.r(), slow=True)
        dma("pool", PS8[0:4, l, 2:3], mlstm_if_b[l, 0:4].rearrange("(h o) -> h o", o=1), "PL", [], PRM.r(), slow=True)
        dma("pool", PS8[0:4, l, 3:4], mlstm_if_b[l, 4:8].rearrange("(h o) -> h o", o=1), "PL", [], PRM.r(), slow=True)
        dma("pool", DBC[:, l, :], ssd_D[l].partition_broadcast(128), "PL", [], PRM.r(), slow=True)
    PR = PRM.r()
    for l in range(NL):
        ts("dve", PRM[:, l, P_B2A:P_B2A + 8], PRM[:, l, P_B2A:P_B2A + 8], 1.0 / ALPHA, ALU.mult, PR, PR)
        ts("dve", PRM[:, l, P_RBAH:P_RBAH + 4], PRM[:, l, P_RBA:P_RBA + 4], -1.0, ALU.mult, PR, PR)
        ts("dve", PRM[:, l, P_RBXH:P_RBXH + 4], PRM[:, l, P_RBX:P_RBX + 4], -1.0, ALU.mult, PR, PR)
        ts("dve", PRM[:, l, P_GGBH:P_GGBH + 2], PRM[:, l, P_GGB:P_GGB + 2], -1.0, ALU.mult, PR, PR)
        act(PRM[:, l, P_RC4:P_RC4 + 4], PRM[:, l, P_RLAM:P_RLAM + 4], AF.Exp, PR, PR, scale=-1.0)
        act(PRM[:, l, P_RC4:P_RC4 + 4], PRM[:, l, P_RC4:P_RC4 + 4], AF.Ln, PR, PR, bias=1.0)
        ts("dve", PRM[:, l, P_RC8:P_RC8 + 4], PRM[:, l, P_RC4:P_RC4 + 4], -8.0, ALU.mult, PR, PR)
        ts("dve", PRM[:, l, P_RC4:P_RC4 + 4], PRM[:, l, P_RC4:P_RC4 + 4], -16.0, ALU.mult, PR, PR)
        act(PS8[0:8, l, 1:2], PS8[0:8, l, 1:2], AF.Exp, PR, PR)
        ts("dve", PS8[0:8, l, 1:2], PS8[0:8, l, 1:2], -1.0, ALU.mult, PR, PR)
        ts("dve", PS8[0:4, l, 3:4], PS8[0:4, l, 3:4], -1.0, ALU.mult, PR, PR)
        memset("pool", STG[:], 0.0, STG.r())
        for ax, wsrc in enumerate((rg_gate_a_w, rg_gate_x_w)):
            for n in range(8):
                hh = n % 2
                dma("pool", STG[hh * 64:(hh + 1) * 64, ax * 4 + n // 2, hh * 64:(hh + 1) * 64], wsrc[l, n], "SG", [], STG.r())
        cp("dve", RGW[:, l, :, :], STG[:], STG.r(), RGW.r())
        dma("pool", STG2[0:16, 0:256], gla_gate_w2[l], "SG2", [], STG2.r())
        cp("dve", GW2[0:16, l, :], STG2[0:16, 0:256], STG2.r(), GW2.r())

    WBFR = [[Reg("wbf%d_%d" % (l, j)) for j in range(NSLAB)] for l in range(NL)]

    def slab_srcs(l, j):
        res = []
        if j < 12:
            src = w_in[l].rearrange("(k p) c -> p k c", p=128)
            for (off, c0, n) in WIN_SLABS[j]:
                res.append((8, 512, off, n, src[:, :, c0:c0 + n]))
        elif j < 16:
            jj = j - 12
            res.append((16, 256, 0, 256, w_out[l].rearrange("(k p) c -> p k c", p=128)[:, :, jj * 256:(jj + 1) * 256]))
        elif j < 24:
            jj = j - 16
            res.append((8, 512, 0, 512, mlp_w1[l].rearrange("(k p) c -> p k c", p=128)[:, :, jj * 512:(jj + 1) * 512]))
        else:
            jj = j - 24
            res.append((32, 128, 0, 128, mlp_w2[l].rearrange("(k p) c -> p k c", p=128)[:, :, jj * 128:(jj + 1) * 128]))
        return res

    MIXF = MIXflat.bitcast(F32)
    XTF = XT.t[:].rearrange("p a b -> p (a b)")
    STAGE = [(HTF[:, 0:4096], HT.r(list(range(0, 16)))), (HTF[:, 4096:8192], HT.r(list(range(16, 32)))),
             (MIXF[:, 0:4096], MIX.r()), (XTF[:, 0:4096], XT.r())]
    pl_list = [(l, j) for l in range(NL) for j in range(NSLAB)]
    PD = 3
    for i in range(len(pl_list) + PD):
        if i < len(pl_list):
            l, j = pl_list[i]
            stg, sreg = STAGE[i % 4]
            if j == 1:
                memset("pool", stg, 0.0, sreg)
            for (kk, cw, off, n, src) in slab_srcs(l, j):
                dstv = stg.rearrange("p (k c) -> p k c", k=kk)[:, :, off:off + n]
                dma("sp", dstv, src, "LDS%d" % (i % 4), [], sreg)
        n_pl = i - PD
        if n_pl >= 0:
            l, j = pl_list[n_pl]
            stg, sreg = STAGE[n_pl % 4]
            slot = WR[n_pl % NSLOT]
            cp("act", slot[:, 0:2048], stg[:, 0:2048], sreg, slot.r())
            cp("dve", slot[:, 2048:4096], stg[:, 2048:4096], sreg, slot.r())
            if j == 1:
                cp("pool", WSM[:, l, :, :], slot[:, :].rearrange("p (k c) -> p k c", k=8)[:, :, 256:288], slot.r(), WSM.r())
            dma("sp", wbf[l, j], slot[:], "W%d" % (n_pl % NSLOT), slot.r(), [WBFR[l][j]])

    slab_seq = []
    wstate = {"issued": 0, "released": 0}

    def pump():
        while wstate["issued"] < len(slab_seq) and wstate["issued"] < wstate["released"] + NSLOT:
            i = wstate["issued"]
            l, j = slab_seq[i]
            slot = WR[i % NSLOT]
            dma("sp", slot[:], wbf[l, j], "W%d" % (i % NSLOT), [WBFR[l][j]], slot.r())
            wstate["issued"] += 1

    def slab(i):
        pump()
        assert i < wstate["issued"], (i, wstate)
        assert i >= wstate["released"], (i, wstate)
        return WR[i % NSLOT]

    def release_upto(i):
        if i + 1 > wstate["released"]:
            wstate["released"] = i + 1
        pump()

    def proj_fm(wslot, c0, M, T, bank=None):
        b = bank or psD()
        wv = wslot[:, :].rearrange("p (k c) -> p k c", k=8)
        for k in range(8):
            mm(b[0:M, 0:T], wv[:, k, c0:c0 + M], XB[:, k, 0:T], k == 0, k == 7, wslot.r() + XB.r(k), qr(b, 0, T))
        return b

    def proj_wsm(l, c0, M, T):
        b = psD()
        for k in range(8):
            mm(b[0:M, 0:T], WSM[:, l, k, c0:c0 + M], XB[:, k, 0:T], k == 0, k == 7, WSM.r() + XB.r(k), qr(b, 0, T))
        return b

    def proj_tm(wslot, c0, ncols, tc0, L, bank=None):
        b = bank or psD()
        wv = wslot[:, :].rearrange("p (k c) -> p k c", k=8)
        for k in range(8):
            mm(b[0:L, 0:ncols], XB[:, k, tc0:tc0 + L], wv[:, k, c0:c0 + ncols], k == 0, k == 7,
               wslot.r() + XB.r(k), qr(b, 0, ncols))
        return b

    def layer_norm_fm(l, T, RR, gcol, bcol, last):
        bm = psM(); bq = psM()
        for k in range(8):
            t1 = TMPR[(2 * k) % 4]; t2 = TMPR[(2 * k + 1) % 4]
            cp("act", t1[:, 0:T], RR[:, k, 0:T], RR.r(k), t1.r())
            tt("dve", t2[:, 0:T], RR[:, k, 0:T], RR[:, k, 0:T], ALU.mult, RR.r(k), t2.r())
            mm(bm[:, 0:T], ODIVR[:, :], t1[:, 0:T], k == 0, k == 7, ODIVR.r() + t1.r(), qr(bm, 0, T))
            mm(bq[:, 0:T], ODIVR[:, :], t2[:, 0:T], k == 0, k == 7, ODIVR.r() + t2.r(), qr(bq, 0, T))
        cp("act", LNS[:, 0, 0:T], bm[:, 0:T], qr(bm, 0, T), LNS.r(0))
        tt("pool", LNS[:, 2, 0:T], LNS[:, 0, 0:T], LNS[:, 0, 0:T], ALU.mult, LNS.r(0), LNS.r(2))
        stt("dve", LNS[:, 1, 0:T], bq[:, 0:T], EPS / (ALPHA * ALPHA), LNS[:, 2, 0:T], ALU.add, ALU.subtract,
            qr(bq, 0, T) + LNS.r(2), LNS.r(1))
        rsq(LNS[:, 1, 0:T], LNS.r(1))
        stt("dve", LNS[:, 2, 0:T], LNS[:, 0, 0:T], -1.0, LNS[:, 1, 0:T], ALU.mult, ALU.mult, LNS.r([0, 1]), LNS.r(2))
        for k in range(8):
            tt("dve", RR[:, k, 0:T], RR[:, k, 0:T], LNS[:, 1, 0:T], ALU.mult, RR.r(k) + LNS.r(1), RR.r(k))
            tt("pool" if k % 2 else "dve", RR[:, k, 0:T], RR[:, k, 0:T], LNS[:, 2, 0:T], ALU.add, RR.r(k) + LNS.r(2), RR.r(k))
            act(XT[:, k, 0:T], RR[:, k, 0:T], AF.Identity, RR.r(k) + PR, XT.r(k),
                bias=PRM[:, l, bcol + k:bcol + k + 1], scale=PRM[:, l, gcol + k:gcol + k + 1])
            if not last:
                cp("pool" if k % 2 else "dve", XB[:, k, 0:T], XT[:, k, 0:T], XT.r(k), XB.r(k))

    def load_tile_and_ln_in(src, chunks, T):
        for ci, (c0, L) in enumerate(chunks):
            dma("sp", XIO[0:L, ci, :], src[c0:c0 + L, :], "XI%d" % ci, [], XIO.r(ci))
        for ci, (c0, L) in enumerate(chunks):
            col = COL[ci % 2]
            memset("pool", col[0:L, 0:16], 0.0, col.r())
            for hh in range(2):
                act(LNS[0:L, 0, :], XIO[0:L, ci, hh * 512:(hh + 1) * 512], AF.Copy, XIO.r(ci), LNS.r(0) + col.r(), accum=col[0:L, 8 + hh:9 + hh])
                act(LNS[0:L, 1, :], XIO[0:L, ci, hh * 512:(hh + 1) * 512], AF.Square, XIO.r(ci), LNS.r(1) + col.r(), accum=col[0:L, 10 + hh:11 + hh])
            tt("dve", col[0:L, 0:1], col[0:L, 8:9], col[0:L, 9:10], ALU.add, col.r(), col.r())
            tt("dve", col[0:L, 1:2], col[0:L, 10:11], col[0:L, 11:12], ALU.add, col.r(), col.r())
            ts("dve", col[0:L, 2:3], col[0:L, 0:1], 1.0 / 1024, ALU.mult, col.r(), col.r())
            tt("dve", col[0:L, 3:4], col[0:L, 2:3], col[0:L, 2:3], ALU.mult, col.r(), col.r())
            stt("dve", col[0:L, 4:5], col[0:L, 1:2], 1.0 / 1024, col[0:L, 3:4], ALU.mult, ALU.subtract, col.r(), col.r())
            ts("dve", col[0:L, 4:5], col[0:L, 4:5], EPS, ALU.add, col.r(), col.r())
            cp("dve", col[0:L, 5:6], col[0:L, 4:5], col.r(), col.r())
            rsq(col[0:L, 5:6], col.r())
            stt("dve", col[0:L, 6:7], col[0:L, 2:3], -1.0, col[0:L, 5:6], ALU.mult, ALU.mult, col.r(), col.r())
            ts("dve", XIO[0:L, ci, :], XIO[0:L, ci, :], col[0:L, 5:6], ALU.mult, XIO.r(ci) + col.r(), XIO.r(ci),
               s2=col[0:L, 6:7], op1=ALU.add)
            for half in range(2):
                b = psM()
                for kk in range(4):
                    k = half * 4 + kk
                    tr(b[:, kk * 128:kk * 128 + L], XIO[0:L, ci, k * 128:(k + 1) * 128], ident[0:L, 0:L],
                       XIO.r(ci) + CST.r(), qr(b, kk * 128, kk * 128 + L))
                for kk in range(4):
                    k = half * 4 + kk
                    act(XT[:, k, c0:c0 + L], b[:, kk * 128:kk * 128 + L], AF.Identity, qr(b, kk * 128, kk * 128 + L) + PR,
                        XT.r(k), bias=LNIN[:, 8 + k:9 + k], scale=LNIN[:, k:k + 1])
        for k in range(8):
            cp("dve" if k % 2 else "pool", XB[:, k, 0:T], XT[:, k, 0:T], XT.r(k), XB.r(k))

    def store_tile(dst, chunks, T):
        for ci, (c0, L) in enumerate(chunks):
            for half in range(2):
                b = psM()
                for kk in range(4):
                    k = half * 4 + kk
                    tr(b[0:L, kk * 128:(kk + 1) * 128], XT[:, k, c0:c0 + L], ident, XT.r(k) + CST.r(), qr(b, kk * 128, (kk + 1) * 128))
                cp("act" if half else "dve", XIO[0:L, ci, half * 512:(half + 1) * 512], b[0:L, :], b.r(), XIO.r(ci))
            dma("sp", dst[c0:c0 + L, :], XIO[0:L, ci, :], "XO%d" % ci, XIO.r(ci), [])

    def run_tile(src, dst, chunks, segs, T, is_sample, gi, fin):
        load_tile_and_ln_in(src, chunks, T)
        for l in range(NL):
            gi = run_layer(l, chunks, segs, T, is_sample, gi, fin)
        store_tile(dst, chunks, T)
        return gi

    def run_layer(l, chunks, segs, T, is_sample, gi, fin):
        if cfg.get("STOP", 0) == 2:
            return gi + NSLAB
        def SL(j):
            return slab(gi + j)
        SL.rel = lambda j: release_upto(gi + j)
        def interleave(gens):
            gens = list(gens)
            while gens:
                for g in list(gens):
                    try:
                        next(g)
                    except StopIteration:
                        gens.remove(g)
        PH = cfg.get("PH", 99)

        def cutoff():
            SL.rel(NSLAB - 1)
            return gi + NSLAB
        if PH == 0:
            return cutoff()
        ssd_fm(l, chunks, segs, T, is_sample, SL)
        if PH == 1:
            return cutoff()
        interleave([ssd_loop(l, chunks, segs, T, is_sample, SL, fin), rg_gen(l, chunks, segs, T, is_sample, SL, fin, 0),
                    rg_gen(l, chunks, segs, T, is_sample, SL, fin, 1)])
        rg_fin(l, fin)
        SL.rel(S_YR)
        if PH == 2:
            return cutoff()
        mlstm_fm(l, chunks, segs, T, is_sample, SL)
        gla_fm(l, chunks, segs, T, is_sample, SL)
        if PH == 3:
            return cutoff()
        interleave([mlstm_loop(l, chunks, segs, T, is_sample, SL, fin), gla_loop(l, chunks, segs, T, is_sample, SL, fin)])
        SL.rel(S_GG)
        if PH == 4:
            return cutoff()
        if DBG and l == 0 and not is_sample:
            for j in range(16):
                jv = LNS[:, j % 3, :]
                cp("dve", jv[:, 0:T], MIX[:, j, 0:T], MIX.r(j), LNS.r(j % 3))
                dma("sp", dbg[j * 128:(j + 1) * 128, 0:T], jv[:, 0:T], "DBG%d" % (j % 3), LNS.r(j % 3), [])
        for jj in range(4):
            ws = SL(12 + jj)
            wv = ws[:, :].rearrange("p (k c) -> p k c", k=16)
            for ee in range(2):
                e = jj * 2 + ee
                b = psD()
                for k in range(16):
                    mm(b[:, 0:T], wv[:, k, ee * 128:(ee + 1) * 128], MIX[:, k, 0:T], k == 0, k == 15,
                       ws.r() + MIX.r(k), qr(b, 0, T))
                stt("dve", RR1[:, e, 0:T], b[:, 0:T], 1.0 / ALPHA, XT[:, e, 0:T], ALU.mult, ALU.add,
                    qr(b, 0, T) + XT.r(e), RR1.r(e))
            SL.rel(12 + jj)
        layer_norm_fm(l, T, RR1, P_LN1G, P_LN1B, False)
        if PH == 5:
            return cutoff()
        for jj in range(8):
            ws = SL(16 + jj)
            wv = ws[:, :].rearrange("p (k c) -> p k c", k=8)
            for ff in range(4):
                f = jj * 4 + ff
                b = psD()
                for k in range(8):
                    mm(b[:, 0:T], wv[:, k, ff * 128:(ff + 1) * 128], XB[:, k, 0:T], k == 0, k == 7,
                       ws.r() + XB.r(k), qr(b, 0, T))
                tv = LNS[:, f % 3, :]
                act(tv[:, 0:T], b[:, 0:T], AF.Relu, qr(b, 0, T) + PR, LNS.r(f % 3), bias=PRM[:, l, P_B1 + f:P_B1 + f + 1])
                tt("pool" if f % 2 else "dve", HT[:, f, 0:T], tv[:, 0:T], tv[:, 0:T], ALU.mult, LNS.r(f % 3), HT.r(f))
            SL.rel(16 + jj)
        for e in range(8):
            ws = SL(24 + e)
            wv = ws[:, :].rearrange("p (k c) -> p k c", k=32)
            b = psD()
            for f in range(32):
                mm(b[:, 0:T], wv[:, f, :], HT[:, f, 0:T], f == 0, f == 31, ws.r() + HT.r(f), qr(b, 0, T))
            act(RR2[:, e, 0:T], b[:, 0:T], AF.Identity, qr(b, 0, T) + PR, RR2.r(e), bias=PRM[:, l, P_B2A + e:P_B2A + e + 1],
                scale=1.0 / ALPHA)
            tt("pool", RR2[:, e, 0:T], RR2[:, e, 0:T], XT[:, e, 0:T], ALU.add, RR2.r(e) + XT.r(e), RR2.r(e))
            SL.rel(24 + e)
        layer_norm_fm(l, T, RR2, P_LN2G, P_LN2B, l == NL - 1)
        return gi + NSLAB

    def to_mix(src_bf, L, c0, jbase, pcol0, l, bank=None):
        ap, cells = src_bf
        bo = bank or psM(); bob = pbf(bo)
        for j in range(4):
            tr(bob[:, j * 128:j * 128 + L], ap[0:L, j * 128:(j + 1) * 128], IDB[0:L, 0:L], cells + IDB.r(),
               qr(bo, j * 64, j * 64 + 64))
        for j in range(4):
            act(MIX[:, jbase + j, c0:c0 + L], bob[:, j * 128:j * 128 + L], AF.Identity, qr(bo, j * 64, j * 64 + 64) + PR,
                MIX.r(jbase + j), scale=PRM[:, l, pcol0 + j:pcol0 + j + 1])

    def conv_hist_in(kind, l, c, u, segs, is_sample):
        for si, (s0, Ls, seq) in enumerate(segs):
            base = si * (Ls + 3)
            if is_sample:
                src = (i_ssd_conv if kind == "ssd" else i_rgconv)[l, seq].rearrange("j (c p) -> p c j", p=128)[:, c, :]
                dma("pool", UX[:, u, base:base + 3], src, "UXH%d" % u, [], UX.r(u), slow=True)
            else:
                st = (CSS if kind == "ssd" else CSR)[l]
                cp("act", UX[:, u, base:base + 3], st[:, c, :], st.r(), UX.r(u))

    def conv_hist_out(kind, l, c, u, segs, is_sample):
        for si, (s0, Ls, seq) in enumerate(segs):
            base = si * (Ls + 3)
            if is_sample:
                dst = o_s["ssd_conv" if kind == "ssd" else "rgconv"][l, seq].rearrange("j (c p) -> p c j", p=128)[:, c, :]
                dma("pool", dst, UX[:, u, base + Ls:base + Ls + 3], "UXO%d" % u, UX.r(u), [], slow=True)
            else:
                st = (CSS if kind == "ssd" else CSR)[l]
                cp("dve", st[:, c, :], UX[:, u, base + Ls:base + Ls + 3], UX.r(u), st.r())

    def ssd_load_state(l, seq, pm=None):
        for h in range(8):
            g, hh = h // 4, h % 4
            dma("pool", HS[l][g * 64:(g + 1) * 64, hh * 64:(hh + 1) * 64], i_ssd_h[l, seq, h].rearrange("p n -> n p"),
                "SSI", [], HS[l].r(), slow=True)
        cp("dve", HSB[:, :], HS[l][:, :], HS[l].r(), HSB.r())

    def ssd_store_state(l, dst, pm=None):
        pm = pm or psM
        stg = TOK32[5]
        for g in range(2):
            b = pm()
            for hh in range(4):
                tr(b[0:64, hh * 64:(hh + 1) * 64], HS[l][g * 64:(g + 1) * 64, hh * 64:(hh + 1) * 64],
                   ident[g * 64:(g + 1) * 64, g * 64:(g + 1) * 64], HS[l].r() + CST.r(), qr(b, 0, 256))
            cp("dve", stg[0:64, g * 256:(g + 1) * 256], b[0:64, 0:256], b.r(), stg.r())
        dma("sp", dst.rearrange("h p n -> p h n"), stg[0:64, :].rearrange("p (h n) -> p h n", h=8), "SSO", stg.r(), [])

    def ssd_fm_chain(l, cs, segs, T, is_sample, SL):
        XC = FMB
        for c in cs:
            u = c % 2
            ws, c0w = (SL(S_XBC), c * 128) if c < 4 else (SL(S_BC), (c - 4) * 128)
            conv_hist_in("ssd", l, c, u, segs, is_sample)
            b = proj_fm(ws, c0w, 128, T, bank=PSB[c % 2])
            yield
            for si, (s0, Ls, seq) in enumerate(segs):
                base = si * (Ls + 3)
                cp("act", UX[:, u, base + 3:base + 3 + Ls], b[:, s0:s0 + Ls], qr(b, 0, T), UX.r(u))
            yield
            acc = FM32[c % 2]; th = FM32[2 + c % 2]
            for si, (s0, Ls, seq) in enumerate(segs):
                base = si * (Ls + 3)
                wc = P_SCW + c * 4
                ts("dve", acc[:, s0:s0 + Ls], UX[:, u, base:base + Ls], PRM[:, l, wc:wc + 1], ALU.mult, UX.r(u) + PR, acc.r(),
                   s2=PRM[:, l, P_SCB + c:P_SCB + c + 1], op1=ALU.add)
                for j in range(1, 4):
                    stt("dve", acc[:, s0:s0 + Ls], UX[:, u, base + j:base + j + Ls], PRM[:, l, wc + j:wc + j + 1],
                        acc[:, s0:s0 + Ls], ALU.mult, ALU.add, UX.r(u) + PR + acc.r(), acc.r())
                    yield
            act(th[:, 0:T], acc[:, 0:T], AF.Exp, acc.r(), th.r(), scale=-1.0)
            yield
            act(th[:, 0:T], th[:, 0:T], AF.Ln, th.r(), th.r(), bias=1.0)
            yield
            act(th[:, 0:T], th[:, 0:T], AF.Exp, th.r(), th.r(), scale=-1.0)
            yield
            tt("dve", XC[:, c, 0:T], th[:, 0:T], acc[:, 0:T], ALU.mult, th.r() + acc.r(), XC.r(c))
            conv_hist_out("ssd", l, c, u, segs, is_sample)
            yield

    def ssd_fm(l, chunks, segs, T, is_sample, SL):
        XC = FMB
        gens = [ssd_fm_chain(l, [0, 2, 4], segs, T, is_sample, SL), ssd_fm_chain(l, [1, 3, 5], segs, T, is_sample, SL)]
        while gens:
            for g in list(gens):
                try:
                    next(g)
                except StopIteration:
                    gens.remove(g)
        SL.rel(S_BC)
        for g in range(2):
            ts("dve", BCM[:, g, 0:T], XC[:, 4, 0:T], CST[:, C_HM + g:C_HM + g + 1], ALU.mult, XC.r(4) + CST.r(), BCM.r(g))
            ts("dve", BCM[:, 2 + g, 0:T], XC[:, 5, 0:T], CST[:, C_HM + g:C_HM + g + 1], ALU.mult, XC.r(5) + CST.r(), BCM.r(2 + g))
        b = proj_wsm(l, 0, 8, T)
        act(SM8[0:8, 0, 0:T], b[0:8, 0:T], AF.Exp, qr(b, 0, T) + PR, SM8.r(0), bias=PS8[0:8, l, 0:1])
        act(SM8[0:8, 0, 0:T], SM8[0:8, 0, 0:T], AF.Ln, SM8.r(0), SM8.r(0), bias=1.0)
        ts("dve", SM8[0:8, 1, 0:T], SM8[0:8, 0, 0:T], PS8[0:8, l, 1:2], ALU.mult, SM8.r(0) + PR, SM8.r(1))
        for (c0, L) in chunks:
            scan(SM8[0:8, 2, c0:c0 + L], ones_f[0:8, 0:L], SM8[0:8, 1, c0:c0 + L], 0.0, ALU.mult, ALU.add,
                 SM8.r(1) + CST.r(), SM8.r(2))

    def ssd_loop(l, chunks, segs, T, is_sample, SL, fin):
        XC = FMB
        pm = mkrot([3, 4, 5, 6, 7]); pd = mkrot([0])
        if not is_sample:
            cp("dve", HSB[:, :], HS[l][:, :], HS[l].r(), HSB.r())
        for ci, (c0, L) in enumerate(chunks):
            p = ci % 2
            col = COL[p]; cr = col.r()
            seq = segs[ci][2] if is_sample else None
            if is_sample:
                ssd_load_state(l, seq, pm)
            bz = proj_tm(SL(S_Z), 0, 512, c0, L, bank=pd())
            t2 = TOK32[2]
            sigm(t2[0:L, :], bz[0:L, :], bz.r(), t2.r())
            tt("dve", t2[0:L, :], t2[0:L, :], bz[0:L, :], ALU.mult, t2.r() + bz.r(), t2.r())
            yield
            bt = pm()
            tr(bt[0:L, 0:8], SM8[0:8, 2, c0:c0 + L], ident[0:8, 0:8], SM8.r(2) + CST.r(), qr(bt, 0, 16))
            tr(bt[0:L, 8:16], SM8[0:8, 0, c0:c0 + L], ident[0:8, 0:8], SM8.r(0) + CST.r(), qr(bt, 0, 16))
            cp("dve", col[0:L, 0:16], bt[0:L, 0:16], qr(bt, 0, 16), cr)
            bx = pm(); bxb = pbf(bx)
            for j in range(5):
                tr(bxb[0:L, j * 128:(j + 1) * 128], XC[:, j, c0:c0 + L], IDB[:, :], XC.r(j) + IDB.r(), qr(bx, j * 64, j * 64 + 64))
            xbf = TOKB[:, 0, :]; xbr = TOKB.r(0)
            btok = TOKB[:, 1, :]; btr = TOKB.r(1)
            cp("act", xbf[0:L, 0:512], bxb[0:L, 0:512], qr(bx, 0, 256), xbr)
            cp("act", btok[0:L, 0:128], bxb[0:L, 512:640], qr(bx, 256, 320), btr)
            xd = TOKB[:, 3, :]; xdr = TOKB.r(3)
            tt("dve", xd[0:L, 0:512].rearrange("p (h j) -> p h j", h=8), xbf[0:L, 0:512].rearrange("p (h j) -> p h j", h=8),
               col[0:L, 8:16].unsqueeze(2).to_broadcast([L, 8, 64]), ALU.mult, xbr + cr, xdr)
            yield
            bcb = pm()
            for g in range(2):
                mm(bcb[0:L, g * 128:g * 128 + L], BCM[:, g, c0:c0 + L], XC[:, 5, c0:c0 + L],
                   True, True, BCM.r(g) + XC.r(5), qr(bcb, g * 128, g * 128 + L))
            bb = [pm(), pm()]
            for h in range(8):
                bk = bb[h // 4]; q = h % 4
                mm(bk[:, q * 128:q * 128 + L], CST[0:8, C_SEL + h * 128:C_SEL + (h + 1) * 128], SM8[0:8, 2, c0:c0 + L],
                   True, True, CST.r() + SM8.r(2), qr(bk, q * 128, q * 128 + L))
            byi = pm()
            for g in range(2):
                mm(byi[0:L, g * 256:(g + 1) * 256], BCM[:, 2 + g, c0:c0 + L], HSB[:, 0:256],
                   True, True, BCM.r(2 + g) + HSB.r(), qr(byi, g * 256, (g + 1) * 256))
            for half in range(2):
                bk = bb[half]
                tt("dve", col[0:L, 24 + half * 4:28 + half * 4], bk[0:L, :].rearrange("p (h t) -> p h t", h=4)[:, :, L - 1],
                   col[0:L, half * 4:half * 4 + 4], ALU.subtract, bk.r() + cr, cr)
                act(col[:, 32 + half * 4:36 + half * 4], bk[:, :].rearrange("p (h t) -> p h t", h=4)[:, :, L - 1], AF.Exp, bk.r(), cr)
            act(col[0:L, 24:32], col[0:L, 24:32], AF.Exp, cr, cr)
            tt("dve", col[0:L, 24:32], col[0:L, 24:32], col[0:L, 8:16], ALU.mult, cr, cr)
            xw = TOKB[:, 2, :]; xwr = TOKB.r(2)
            tt("dve", xw[0:L, 0:512].rearrange("p (h j) -> p h j", h=8), xbf[0:L, 0:512].rearrange("p (h j) -> p h j", h=8),
               col[0:L, 24:32].unsqueeze(2).to_broadcast([L, 8, 64]), ALU.mult, xbr + cr, xwr)
            yield
            bd = pm()
            for g in range(2):
                mm(bd[g * 64:(g + 1) * 64, 0:256], btok[0:L, g * 64:(g + 1) * 64], xw[0:L, g * 256:(g + 1) * 256], True, True,
                   btr + xwr, qr(bd, 0, 256))
            for g in range(2):
                rows = slice(g * 64, (g + 1) * 64)
                tt("dve", HS[l][rows, :].rearrange("p (h j) -> p h j", h=4), HS[l][rows, :].rearrange("p (h j) -> p h j", h=4),
                   col[rows, 32 + g * 4:36 + g * 4].unsqueeze(2).to_broadcast([64, 4, 64]), ALU.mult, HS[l].r() + cr, HS[l].r())
                tt("dve", HS[l][rows, :], HS[l][rows, :], bd[rows, 0:256], ALU.add, HS[l].r() + qr(bd, 0, 256), HS[l].r())
            yield
            cp("dve", HSB[:, :], HS[l][:, :], HS[l].r(), HSB.r())
            yield
            sc = SC[p]
            for h in range(8):
                bk = bb[h // 4]; q = h % 4
                stt("dve", TK[0:L, h, 0:L], bk[0:L, q * 128:q * 128 + L], col[0:L, h:h + 1], mneg[0:L, 0:L],
                    ALU.subtract, ALU.add, qr(bk, q * 128, q * 128 + L) + cr + CST.r(), TK.r())
            yield
            act(TK[0:L, :, 0:L], TK[0:L, :, 0:L], AF.Exp, TK.r(), TK.r())
            yield
            for g in range(2):
                tt("dve", sc[0:L, 4 * g:4 * g + 4, 0:L], TK[0:L, 4 * g:4 * g + 4, 0:L],
                   bcb[0:L, g * 128:g * 128 + L].unsqueeze(1).to_broadcast([L, 4, L]), ALU.mult,
                   qr(bcb, g * 128, g * 128 + L) + TK.r(), sc.r())
            yield
            by = pm()
            for h in range(8):
                mm(by[0:L, h * 64:(h + 1) * 64], sc[0:L, h, 0:L], xd[0:L, h * 64:(h + 1) * 64], True, True,
                   sc.r() + xdr, qr(by, h * 64, (h + 1) * 64))
            yield
            act(col[0:L, 16:24], col[0:L, 0:8], AF.Exp, cr, cr)
            t0 = TOK32[0]; t1 = TOK32[1]; t2 = TOK32[2]
            tt("dve", t0[0:L, :].rearrange("p (h j) -> p h j", h=8), byi[0:L, :].rearrange("p (h j) -> p h j", h=8),
               col[0:L, 16:24].unsqueeze(2).to_broadcast([L, 8, 64]), ALU.mult, byi.r() + cr, t0.r())
            tt("dve", t0[0:L, :], t0[0:L, :], by[0:L, :], ALU.add, t0.r() + by.r(), t0.r())
            tt("dve", t1[0:L, :].rearrange("p (h j) -> p h j", h=8), xbf[0:L, 0:512].rearrange("p (h j) -> p h j", h=8),
               DBC[0:L, l, :].unsqueeze(2).to_broadcast([L, 8, 64]), ALU.mult, xbr + PR, t1.r())
            tt("dve", t0[0:L, :], t0[0:L, :], t1[0:L, :], ALU.add, t0.r() + t1.r(), t0.r())
            tt("dve", t0[0:L, :], t0[0:L, :], t2[0:L, :], ALU.mult, t0.r() + t2.r(), t0.r())
            yield
            memset("pool", col[0:L, 40:42], 0.0, cr)
            act(t1[0:L, :], t0[0:L, :], AF.Square, t0.r(), t1.r() + cr, accum=col[0:L, 40:41])
            ts("dve", col[0:L, 41:42], col[0:L, 40:41], 1.0 / 512, ALU.mult, cr, cr, s2=EPS, op1=ALU.add)
            rsq(col[0:L, 41:42], cr)
            gn = TOKB[:, 3, :]; gnr = TOKB.r(3)
            ts("dve", gn[0:L, 0:512], t0[0:L, :], col[0:L, 41:42], ALU.mult, t0.r() + cr, gnr)
            yield
            to_mix((gn, gnr), L, c0, 0, P_SNW, l, bank=pm())
            yield
            if is_sample:
                ssd_store_state(l, o_s["ssd_h"][l, seq], pm)
                yield
        if fin:
            ssd_store_state(l, o_p["ssd_h"][l], pm)
            for c in range(6):
                dma("pool", o_p["ssd_conv"][l].rearrange("j (c p) -> p c j", p=128)[:, c, :], CSS[l][:, c, :], "FSO", CSS[l].r(), [], slow=True)

    def em_bcast(l, dstcol, pm=None):
        ts("dve", EMT[0:4, 4:8], ident[0:4, 0:4], EM[l][0:4, 0:1], ALU.mult, CST.r() + EM[l].r(), EMT.r())
        b = (pm or psM)()
        mm(b[:, 0:4], ones_f[0:4, 0:128], EMT[0:4, 4:8], True, True, CST.r() + EMT.r(), qr(b, 0, 4))
        return b

    def mlstm_load_state(l, seq):
        stg = TOK32W
        dma("sp", stg[:, :, 0:128], i_mC[l, seq].rearrange("h d v -> d h v"), "MSI", [], stg.r())
        dma("pool", stg[:, :, 128], i_mn[l, seq].rearrange("h d -> d h"), "MSIp", [], stg.r(), slow=True)
        dma("pool", MB[:, 0:4], i_mm[l, seq].partition_broadcast(128), "MSI2", [], MB.r(), slow=True)
        dma("pool", EM[l][0:4, 0:1], i_mm[l, seq].rearrange("(h o) -> h o", o=1), "MSI3", [], EM[l].r(), slow=True)
        act(MB[:, 0:4], MB[:, 0:4], AF.Exp, MB.r(), MB.r())
        act(EM[l][0:4, 0:1], EM[l][0:4, 0:1], AF.Exp, EM[l].r(), EM[l].r())
        tt("dve", CM[l][:, :, 0:129], stg[:, :, 0:129], MB[:, 0:4].unsqueeze(2).to_broadcast([128, 4, 129]), ALU.mult,
           stg.r() + MB.r(), CM[l].r())
        cp("pool", CMB[:, :, 0:129], CM[l][:, :, 0:129], CM[l].r(), CMB.r())

    def mlstm_store_state(l, dC, dn, dm, pm=None):
        b = em_bcast(l, None, pm)
        recip(MB[:, 4:8], b[:, 0:4], qr(b, 0, 4), MB.r())
        stg = TOK32W
        tt("dve", stg[:, :, 0:129], CM[l][:, :, 0:129], MB[:, 4:8].unsqueeze(2).to_broadcast([128, 4, 129]), ALU.mult,
           CM[l].r() + MB.r(), stg.r())
        dma("sp", dC.rearrange("h d v -> d h v"), stg[:, :, 0:128], "MSO", stg.r(), [])
        dma("pool", dn.rearrange("h d -> d h"), stg[:, :, 128], "MSOp", stg.r(), [], slow=True)
        act(EMT[0:4, 8:9], EM[l][0:4, 0:1], AF.Ln, EM[l].r(), EMT.r())
        dma("pool", dm.rearrange("(h o) -> h o", o=1), EMT[0:4, 8:9], "MSO2", EMT.r(), [], slow=True)

    def mlstm_fm(l, chunks, segs, T, is_sample, SL):
        for h in range(4):
            b = proj_fm(SL(S_MQ), h * 128, 128, T)
            cp("act", FMB[:, h, 0:T], b[:, 0:T], qr(b, 0, T), FMB.r(h))
        SL.rel(S_MQ)
        for h in range(4):
            b = proj_fm(SL(S_MK), h * 128, 128, T)
            act(FMB[:, 4 + h, 0:T], b[:, 0:T], AF.Identity, qr(b, 0, T), FMB.r(4 + h), scale=float(128 ** -0.5))
        SL.rel(S_MK)
        bi = proj_wsm(l, 8, 4, T)
        act(SM8[0:4, 0, 0:T], bi[0:4, 0:T], AF.Exp, qr(bi, 0, T) + PR, SM8.r(0), bias=PS8[0:4, l, 2:3])
        bf_ = proj_wsm(l, 12, 4, T)
        sigm(SM8[0:4, 1, 0:T], bf_[0:4, 0:T], qr(bf_, 0, T) + PR, SM8.r(1), nbias=PS8[0:4, l, 3:4])
        for (c0, L) in chunks:
            scan(SM8[0:4, 3, c0:c0 + L], SM8[0:4, 1, c0:c0 + L], zeros_f[0:4, 0:L], 1.0, ALU.mult, ALU.add,
                 SM8.r(1) + CST.r(), SM8.r(3))
        recip(SM8[0:4, 2, 0:T], SM8[0:4, 3, 0:T], SM8.r(3), SM8.r(2))
        tt("dve", SM8[0:4, 2, 0:T], SM8[0:4, 2, 0:T], SM8[0:4, 0, 0:T], ALU.mult, SM8.r([0, 2]), SM8.r(2))

    def mlstm_loop(l, chunks, segs, T, is_sample, SL, fin):
        pm = mkrot([3, 4, 5, 6]); pd = mkrot([0])
        if not is_sample:
            cp("dve", CMB[:, :, 0:129], CM[l][:, :, 0:129], CM[l].r(), CMB.r())
        for ci, (c0, L) in enumerate(chunks):
            p = ci % 2
            col = COL[p]; cr = col.r()
            seq = segs[ci][2] if is_sample else None
            if is_sample:
                mlstm_load_state(l, seq)
            yield
            bt = pm()
            tr(bt[0:L, 0:4], SM8[0:4, 2, c0:c0 + L], ident[0:4, 0:4], SM8.r(2) + CST.r(), qr(bt, 0, 8))
            tr(bt[0:L, 4:8], SM8[0:4, 3, c0:c0 + L], ident[0:4, 0:4], SM8.r(3) + CST.r(), qr(bt, 0, 8))
            cp("dve", col[0:L, 0:8], bt[0:L, 0:8], qr(bt, 0, 8), cr)
            rmax(EMT[0:4, 0:1], SM8[0:4, 2, c0:c0 + L], SM8.r(2), EMT.r())
            tt("dve", EM[l][0:4, 0:1], EM[l][0:4, 0:1], EMT[0:4, 0:1], ALU.max, EM[l].r() + EMT.r(), EM[l].r())
            tt("dve", EM[l][0:4, 0:1], EM[l][0:4, 0:1], SM8[0:4, 3, c0 + L - 1:c0 + L], ALU.mult, EM[l].r() + SM8.r(3), EM[l].r())
            ts("dve", EMT[0:4, 4:8], ident[0:4, 0:4], SM8[0:4, 3, c0 + L - 1:c0 + L], ALU.mult, CST.r() + SM8.r(3), EMT.r())
            bfl = pm()
            mm(bfl[:, 0:4], ones_f[0:4, 0:128], EMT[0:4, 4:8], True, True, CST.r() + EMT.r(), qr(bfl, 0, 4))
            cp("dve", col[:, 8:12], bfl[:, 0:4], qr(bfl, 0, 4), cr)
            yield
            bv = proj_tm(SL(S_MV), 0, 512, c0, L, bank=pd())
            va = TOKB[:, 1, :].rearrange("p (h v) -> p h v", h=4); var_ = TOKB.r(1)
            tt("dve", va[0:L, :, 0:128], bv[0:L, :].rearrange("p (h v) -> p h v", h=4),
               col[0:L, 0:4].unsqueeze(2).to_broadcast([L, 4, 128]), ALU.mult, bv.r() + cr, var_)
            cp("dve", va[0:L, :, 128:129], col[0:L, 0:4].unsqueeze(2), cr, var_)
            yield
            bo_ = proj_tm(SL(S_MO), 0, 512, c0, L, bank=pd())
            tho = TOK32[0]
            sigm(tho[0:L, :], bo_[0:L, :], bo_.r(), tho.r())
            yield
            bk_ = pm(); bkb = pbf(bk_)
            for h in range(4):
                tr(bkb[0:L, h * 128:(h + 1) * 128], FMB[:, 4 + h, c0:c0 + L], IDB[:, :], FMB.r(4 + h) + IDB.r(), qr(bk_, h * 64, h * 64 + 64))
            ktok = TOKB[:, 0, :]; ktr = TOKB.r(0)
            cp("act", ktok[0:L, 0:512], bkb[0:L, 0:512], qr(bk_, 0, 256), ktr)
            yield
            bs = pm()
            for h in range(4):
                mm(bs[0:L, h * 128:h * 128 + L], FMB[:, 4 + h, c0:c0 + L], FMB[:, h, c0:c0 + L], True, True,
                   FMB.r([h, 4 + h]), qr(bs, h * 128, h * 128 + L))
            sc = SC[p]
            tt("dve", sc[0:L, 0:4, 0:L], bs[0:L, :].rearrange("p (h t) -> p h t", h=4)[:, :, 0:L],
               m01[0:L, 0:L].unsqueeze(1).to_broadcast([L, 4, L]), ALU.mult, bs.r() + CST.r(), sc.r())
            yield
            by = [pm(), pm()]
            for h in range(4):
                bk2 = by[h // 2]; o = (h % 2) * 132
                mm(bk2[0:L, o:o + 129], sc[0:L, h, 0:L], va[0:L, h, 0:129], True, False, sc.r() + var_, qr(bk2, o, o + 129))
                mm(bk2[0:L, o:o + 129], FMB[:, h, c0:c0 + L], CMB[:, h, 0:129], False, True, FMB.r(h) + CMB.r(), qr(bk2, o, o + 129))
            yield
            for j in range(2):
                cp("dve", col[0:L, 12 + 2 * j:14 + 2 * j].unsqueeze(2),
                   by[j][0:L, 0:264].rearrange("p (h v) -> p h v", h=2)[:, :, 128:129], by[j].r(), cr)
            tt("dve", col[0:L, 16:20], col[0:L, 12:16], col[0:L, 4:8], ALU.mult, cr, cr)
            stt("dve", col[0:L, 16:20], col[0:L, 16:20], -1.0, col[0:L, 16:20], ALU.mult, ALU.max, cr, cr)
            ts("dve", col[0:L, 16:20], col[0:L, 16:20], 1.0, ALU.max, cr, cr)
            recip(col[0:L, 16:20], col[0:L, 16:20], cr, cr)
            tt("dve", col[0:L, 16:20], col[0:L, 16:20], col[0:L, 4:8], ALU.mult, cr, cr)
            yield
            memset("pool", col[0:L, 20:24], 0.0, cr)
            junk = TOK32[1]
            for h in range(4):
                bk2 = by[h // 2]; o = (h % 2) * 132
                act(junk[0:L, h * 128:(h + 1) * 128], bk2[0:L, o:o + 128], AF.Square, qr(bk2, o, o + 128), junk.r() + cr,
                    accum=col[0:L, 20 + h:21 + h])
            tt("dve", col[0:L, 24:28], col[0:L, 16:20], col[0:L, 16:20], ALU.mult, cr, cr)
            tt("dve", col[0:L, 24:28], col[0:L, 24:28], col[0:L, 20:24], ALU.mult, cr, cr)
            ts("dve", col[0:L, 24:28], col[0:L, 24:28], 1.0 / 128, ALU.mult, cr, cr, s2=EPS, op1=ALU.add)
            rsq(col[0:L, 24:28], cr)
            tt("dve", col[0:L, 24:28], col[0:L, 24:28], col[0:L, 16:20], ALU.mult, cr, cr)
            yield
            t2 = TOK32[4]
            for h in range(4):
                bk2 = by[h // 2]; o = (h % 2) * 132
                ts("dve", t2[0:L, h * 128:(h + 1) * 128], bk2[0:L, o:o + 128], col[0:L, 24 + h:25 + h], ALU.mult,
                   qr(bk2, o, o + 128) + cr, t2.r())
            mo = TOKB[:, 2, :]; mor = TOKB.r(2)
            tt("dve", mo[0:L, 0:512], tho[0:L, :], t2[0:L, :], ALU.mult, tho.r() + t2.r(), mor)
            yield
            to_mix((mo, mor), L, c0, 4, P_MNW, l, bank=pm())
            yield
            bd = [pm(), pm()]
            for h in range(4):
                bk2 = bd[h // 2]; o = (h % 2) * 132
                mm(bk2[:, o:o + 129], ktok[0:L, h * 128:(h + 1) * 128], va[0:L, h, 0:129], True, True, ktr + var_, qr(bk2, o, o + 129))
            for j in range(2):
                tt("dve", CM[l][:, 2 * j:2 * j + 2, 0:129], CM[l][:, 2 * j:2 * j + 2, 0:129],
                   bd[j][:, 0:264].rearrange("p (h v) -> p h v", h=2)[:, :, 0:129], ALU.add, CM[l].r() + bd[j].r(), CM[l].r())
            tt("dve", CM[l][:, :, 0:129], CM[l][:, :, 0:129], col[:, 8:12].unsqueeze(2).to_broadcast([128, 4, 129]), ALU.mult,
               CM[l].r() + cr, CM[l].r())
            yield
            cp("dve", CMB[:, :, 0:129], CM[l][:, :, 0:129], CM[l].r(), CMB.r())
            if is_sample:
                mlstm_store_state(l, o_s["mC"][l, seq], o_s["mn"][l, seq], o_s["mm"][l, seq], pm)
            yield
        if fin:
            mlstm_store_state(l, o_p["mC"][l], o_p["mn"][l], o_p["mm"][l], pm)

    def rg_gen(l, chunks, segs, T, is_sample, SL, fin, par=0):
        pd = mkrot([1 + par])
        if is_sample and par == 0:
            for si, (s0, Ls, seq) in enumerate(segs):
                dma("pool", RGHS[:, si, :], i_rgh[l, seq].rearrange("(c p) -> p c", p=128), "RGI", [], RGHS.r(), slow=True)
        for c in (par, par + 2):
            u = c % 2
            conv_hist_in("rg", l, c, u, segs, is_sample)
            b = proj_fm(SL(S_XR), c * 128, 128, T, bank=pd())
            yield
            for si, (s0, Ls, seq) in enumerate(segs):
                base = si * (Ls + 3)
                cp("act", UX[:, u, base + 3:base + 3 + Ls], b[:, s0:s0 + Ls], qr(b, 0, T), UX.r(u))
            xr = FM32[c % 2]
            for si, (s0, Ls, seq) in enumerate(segs):
                base = si * (Ls + 3)
                wc = P_RCW + c * 4
                ts("dve", xr[:, s0:s0 + Ls], UX[:, u, base:base + Ls], PRM[:, l, wc:wc + 1], ALU.mult, UX.r(u) + PR, xr.r(),
                   s2=PRM[:, l, P_RCB + c:P_RCB + c + 1], op1=ALU.add)
                for j in range(1, 4):
                    stt("dve", xr[:, s0:s0 + Ls], UX[:, u, base + j:base + j + Ls], PRM[:, l, wc + j:wc + j + 1],
                        xr[:, s0:s0 + Ls], ALU.mult, ALU.add, UX.r(u) + PR + xr.r(), xr.r())
            conv_hist_out("rg", l, c, u, segs, is_sample)
            yield
            xrb = FMB[:, 6 + c % 2, :]; xrbr = FMB.r(6 + c % 2)
            cp("act", xrb[:, 0:T], xr[:, 0:T], xr.r(), xrbr)
            ba = pd()
            mm(ba[:, 0:T], RGW[:, l, c, :], xrb[:, 0:T], True, True, RGW.r() + xrbr, qr(ba, 0, T))
            tha = FM32[2 + c % 2]
            yield
            sigm(tha[:, 0:T], ba[:, 0:T], qr(ba, 0, T) + PR, tha.r(), nbias=PRM[:, l, P_RBAH + c:P_RBAH + c + 1])
            yield
            a = FM32[4 + c % 2]
            act(a[:, 0:T], tha[:, 0:T], AF.Exp, tha.r() + PR, a.r(), scale=PRM[:, l, P_RC8 + c:P_RC8 + c + 1])
            if par == 0:
                sq = LNS[:, 0, :]; sqr = LNS.r(0)
            else:
                sq = TOK32[3]; sqr = TOK32[3].r()
            act(sq[:, 0:T], tha[:, 0:T], AF.Exp, tha.r() + PR, sqr, scale=PRM[:, l, P_RC4 + c:P_RC4 + c + 1])
            ts("dve", sq[:, 0:T], sq[:, 0:T], -1.0, ALU.mult, sqr, sqr, s2=1.0, op1=ALU.add)
            act(sq[:, 0:T], sq[:, 0:T], AF.Ln, sqr, sqr)
            act(sq[:, 0:T], sq[:, 0:T], AF.Exp, sqr, sqr, scale=0.5)
            yield
            bx = pd()
            mm(bx[:, 0:T], RGW[:, l, 4 + c, :], xrb[:, 0:T], True, True, RGW.r() + xrbr, qr(bx, 0, T))
            if par == 0:
                thx = LNS[:, 1, :]; thxr = LNS.r(1)
            else:
                thx = TOK32[4]; thxr = TOK32[4].r()
            sigm(thx[:, 0:T], bx[:, 0:T], qr(bx, 0, T) + PR, thxr, nbias=PRM[:, l, P_RBXH + c:P_RBXH + c + 1])
            tt("dve", thx[:, 0:T], thx[:, 0:T], xr[:, 0:T], ALU.mult, thxr + xr.r(), thxr)
            tt("dve", thx[:, 0:T], thx[:, 0:T], sq[:, 0:T], ALU.mult, thxr + sqr, thxr)
            yield
            hr = FM32[6 + c % 2]
            for si, (s0, Ls, seq) in enumerate(segs):
                if is_sample:
                    init = RGHS[:, si, c:c + 1]; ir = RGHS.r()
                else:
                    init = RGH[l][:, c:c + 1]; ir = RGH[l].r()
                scan(hr[:, s0:s0 + Ls], a[:, s0:s0 + Ls], thx[:, s0:s0 + Ls], init, ALU.mult, ALU.add, a.r() + thxr + ir, hr.r())
                if is_sample:
                    dma("pool", o_s["rgh"][l, seq].rearrange("(c p) -> p c", p=128)[:, c:c + 1], hr[:, s0 + Ls - 1:s0 + Ls],
                        "RGO%d" % (c % 2), hr.r(), [], slow=True)
                else:
                    cp("dve", RGH[l][:, c:c + 1], hr[:, s0 + Ls - 1:s0 + Ls], hr.r(), RGH[l].r())
            yield
            by = proj_fm(SL(S_YR), c * 128, 128, T, bank=pd())
            yield
            if par == 0:
                gy = LNS[:, 2, :]; gyr = LNS.r(2)
            else:
                gy = a; gyr = a.r()
            yv = FM32[2 + c % 2]
            cp("act", yv[:, 0:T], by[:, 0:T], qr(by, 0, T), yv.r())
            act(gy[:, 0:T], by[:, 0:T], AF.Square, qr(by, 0, T), gyr)
            ts("dve", gy[:, 0:T], gy[:, 0:T], 0.044715, ALU.mult, gyr, gyr, s2=1.0, op1=ALU.add)
            tt("dve", gy[:, 0:T], gy[:, 0:T], yv[:, 0:T], ALU.mult, gyr + yv.r(), gyr)
            yield
            act(gy[:, 0:T], gy[:, 0:T], AF.Exp, gyr, gyr, scale=-1.5957691216057308)
            act(gy[:, 0:T], gy[:, 0:T], AF.Ln, gyr, gyr, bias=1.0)
            act(gy[:, 0:T], gy[:, 0:T], AF.Exp, gyr, gyr, scale=-1.0)
            tt("dve", gy[:, 0:T], gy[:, 0:T], yv[:, 0:T], ALU.mult, gyr + yv.r(), gyr)
            tt("pool", MIX[:, 8 + c, 0:T], hr[:, 0:T], gy[:, 0:T], ALU.mult, hr.r() + gyr, MIX.r(8 + c))
            yield

    def rg_fin(l, fin):
        if fin:
            dma("pool", o_p["rgh"][l].rearrange("(c p) -> p c", p=128), RGH[l][:, :], "FSO", RGH[l].r(), [], slow=True)
            for c in range(4):
                dma("pool", o_p["rgconv"][l].rearrange("j (c p) -> p c j", p=128)[:, c, :], CSR[l][:, c, :], "FSO", CSR[l].r(), [], slow=True)

    def gla_load_state(l, seq):
        stg = FM32[6]
        dma("sp", stg[:, 0:256].rearrange("p (a v) -> p a v", a=2), i_gla[l, seq].rearrange("(a hh) d v -> (hh d) a v", hh=2),
            "GSI", [], stg.r())
        cp("dve", GS[l][:, :, :], stg[:, 0:256].rearrange("p (a v) -> p a v", a=2), stg.r(), GS[l].r())
        cp("pool", GSB[:, :, :], stg[:, 0:256].rearrange("p (a v) -> p a v", a=2), stg.r(), GSB.r())

    def gla_store_state(l, dst):
        dma("sp", dst.rearrange("(a hh) d v -> (hh d) a v", hh=2), GS[l][:, :, :], "GSO%d" % l, GS[l].r(), [])

    GQ = [(TMPB[0][:, pc * 512:(pc + 1) * 512], TMPB[0].r()) for pc in range(2)]
    GK = [(TMPB[1][:, pc * 512:(pc + 1) * 512], TMPB[1].r()) for pc in range(2)]
    GKD = [(TMPB[2][:, pc * 512:(pc + 1) * 512], TMPB[2].r()) for pc in range(2)]

    def gla_fm(l, chunks, segs, T, is_sample, SL):
        bag = proj_wsm(l, 16, 16, T)
        agb = TOKB[:, 3, :]; agr = TOKB.r(3)
        cp("act", agb[0:16, 0:T], bag[0:16, 0:T], qr(bag, 0, T), agr)
        Ac = [FM32[0], FM32[1]]; rAc = [FM32[2], FM32[3]]
        for pc in range(2):
            bl = psD()
            mm(bl[:, 0:T], GW2[0:16, l, pc * 128:(pc + 1) * 128], agb[0:16, 0:T], True, True, GW2.r() + agr, qr(bl, 0, T))
            th = FM32[4 + pc]
            act(th[:, 0:T], bl[:, 0:T], AF.Exp, qr(bl, 0, T) + PR, th.r(), bias=PRM[:, l, P_GGBH + pc:P_GGBH + pc + 1], scale=-1.0)
            act(th[:, 0:T], th[:, 0:T], AF.Ln, th.r(), th.r(), bias=1.0)
            act(th[:, 0:T], th[:, 0:T], AF.Exp, th.r(), th.r(), scale=-1.0 / 16.0)
            for (c0, L) in chunks:
                scan(Ac[pc][:, c0:c0 + L], th[:, c0:c0 + L], zeros_f[:, 0:L], 1.0, ALU.mult, ALU.add, th.r() + CST.r(), Ac[pc].r())
            recip(rAc[pc][:, 0:T], Ac[pc][:, 0:T], Ac[pc].r(), rAc[pc].r())
            bq = proj_fm(SL(S_GQK), pc * 128, 128, T)
            stt("dve", GQ[pc][0][:, 0:T], bq[:, 0:T], 0.125, Ac[pc][:, 0:T], ALU.mult, ALU.mult, qr(bq, 0, T) + Ac[pc].r(), GQ[pc][1])
            bk = proj_fm(SL(S_GQK), 256 + pc * 128, 128, T)
            tt("dve", GK[pc][0][:, 0:T], bk[:, 0:T], rAc[pc][:, 0:T], ALU.mult, qr(bk, 0, T) + rAc[pc].r(), GK[pc][1])
            for ci, (c0, L) in enumerate(chunks):
                stt("dve", GKD[pc][0][:, c0:c0 + L], bk[:, c0:c0 + L], Ac[pc][:, c0 + L - 1:c0 + L], rAc[pc][:, c0:c0 + L],
                    ALU.mult, ALU.mult, qr(bk, 0, T) + Ac[pc].r() + rAc[pc].r(), GKD[pc][1])
                cp("dve", GAL[:, pc, ci:ci + 1], Ac[pc][:, c0 + L - 1:c0 + L], Ac[pc].r(), GAL.r())
        SL.rel(S_GQK)
        for h in range(4):
            ts("dve", BCM[:, h, 0:T], GQ[h // 2][0][:, 0:T], CST[:, C_HM + h % 2:C_HM + h % 2 + 1], ALU.mult,
               GQ[h // 2][1] + CST.r(), BCM.r(h))

    def gla_loop(l, chunks, segs, T, is_sample, SL, fin):
        pm = mkrot([7, 2]); pd = mkrot([1])
        if not is_sample:
            cp("dve", GSB[:, :, :], GS[l][:, :, :], GS[l].r(), GSB.r())
        for ci, (c0, L) in enumerate(chunks):
            p = ci % 2
            col = COLG[p]; cr = col.r()
            seq = segs[ci][2] if is_sample else None
            if is_sample:
                gla_load_state(l, seq)
            yield
            bs = pm()
            for h in range(4):
                pc = h // 2; r0 = (h % 2) * 64
                mm(bs[0:L, h * 128:h * 128 + L], GK[pc][0][:, c0:c0 + L], BCM[:, h, c0:c0 + L], True, True,
                   GK[pc][1] + BCM.r(h), qr(bs, h * 128, h * 128 + L))
            yield
            sc = SCG[p]
            tt("dve", sc[0:L, 0:4, 0:L], bs[0:L, :].rearrange("p (h t) -> p h t", h=4)[:, :, 0:L],
               m01[0:L, 0:L].unsqueeze(1).to_broadcast([L, 4, L]), ALU.mult, bs.r() + CST.r(), sc.r())
            yield
            bv = proj_tm(SL(S_GV), 0, 512, c0, L, bank=pd())
            vbf = FM16[0]; vbr = FM16[0].r()
            cp("act", vbf[0:L, 0:512], bv[0:L, :], bv.r(), vbr)
            yield
            bo = pm()
            for h in range(4):
                pc = h // 2; r0 = (h % 2) * 64
                mm(bo[0:L, h * 128:(h + 1) * 128], sc[0:L, h, 0:L], vbf[0:L, h * 128:(h + 1) * 128], True, False,
                   sc.r() + vbr, qr(bo, h * 128, (h + 1) * 128))
                mm(bo[0:L, h * 128:(h + 1) * 128], BCM[:, h, c0:c0 + L], GSB[:, pc, :], False, True,
                   BCM.r(h) + GSB.r(), qr(bo, h * 128, (h + 1) * 128))
            yield
            memset("pool", col[0:L, 0:4], 0.0, cr)
            junk = FM32[4]
            for h in range(4):
                act(junk[0:L, h * 128:(h + 1) * 128], bo[0:L, h * 128:(h + 1) * 128], AF.Square, qr(bo, h * 128, (h + 1) * 128),
                    junk.r() + cr, accum=col[0:L, h:h + 1])
            ts("dve", col[0:L, 4:8], col[0:L, 0:4], 1.0 / 128, ALU.mult, cr, cr, s2=EPS, op1=ALU.add)
            rsq(col[0:L, 4:8], cr)
            yield
            bg = proj_tm(SL(S_GG), 0, 512, c0, L, bank=pd())
            yield
            thg = FM32[3]
            sigm(thg[0:L, :], bg[0:L, :], bg.r(), thg.r())
            tt("dve", thg[0:L, :], thg[0:L, :], bg[0:L, :], ALU.mult, thg.r() + bg.r(), thg.r())
            yield
            t2 = FM32[5]
            tt("dve", t2[0:L, :].rearrange("p (h v) -> p h v", h=4), bo[0:L, :].rearrange("p (h v) -> p h v", h=4),
               col[0:L, 4:8].unsqueeze(2).to_broadcast([L, 4, 128]), ALU.mult, bo.r() + cr, t2.r())
            go = FM16[2]; gor = FM16[2].r()
            tt("dve", go[0:L, 0:512], t2[0:L, :], thg[0:L, :], ALU.mult, t2.r() + thg.r(), gor)
            yield
            to_mix((go, gor), L, c0, 12, P_GNW, l, bank=pm())
            yield
            bkd = pm(); bkdb = pbf(bkd)
            for pc in range(2):
                tr(bkdb[0:L, pc * 128:(pc + 1) * 128], GKD[pc][0][:, c0:c0 + L], IDB[:, :], GKD[pc][1] + IDB.r(), qr(bkd, pc * 64, pc * 64 + 64))
            kdt = FM16[1]; kdr = FM16[1].r()
            cp("act", kdt[0:L, 0:256], bkdb[0:L, 0:256], qr(bkd, 0, 128), kdr)
            yield
            bd = pm()
            for h in range(4):
                pc = h // 2; r0 = (h % 2) * 64
                mm(bd[r0:r0 + 64, pc * 128:(pc + 1) * 128], kdt[0:L, pc * 128 + r0:pc * 128 + r0 + 64], vbf[0:L, h * 128:(h + 1) * 128],
                   True, True, kdr + vbr, qr(bd, pc * 128, (pc + 1) * 128))
            for pc in range(2):
                stt("dve", GS[l][:, pc, :], GS[l][:, pc, :], GAL[:, pc, ci:ci + 1], bd[:, pc * 128:(pc + 1) * 128],
                    ALU.mult, ALU.add, GS[l].r() + GAL.r() + qr(bd, pc * 128, (pc + 1) * 128), GS[l].r())
            yield
            cp("dve", GSB[:, :, :], GS[l][:, :, :], GS[l].r(), GSB.r())
            if is_sample:
                gla_store_state(l, o_s["gla"][l, seq])
            yield
        if fin:
            gla_store_state(l, o_p["gla"][l])

    tiles = []
    if SAMPLE:
        tiles.append(("s", None))
    for t in range(NT):
        tiles.append(("p", t))
    for kind, t in tiles:
        for l in range(NL):
            for j in range(NSLAB):
                slab_seq.append((l, j))

    def zero_states():
        for l in range(NL):
            memset("pool", HS[l][:], 0.0, HS[l].r())
            memset("pool", CSS[l][:], 0.0, CSS[l].r())
            memset("pool", CM[l][:], 0.0, CM[l].r())
            memset("pool", EM[l][:], 1.0, EM[l].r())
            memset("pool", RGH[l][:], 0.0, RGH[l].r())
            memset("pool", CSR[l][:], 0.0, CSR[l].r())
            memset("pool", GS[l][:], 0.0, GS[l].r())
        memset("pool", HSB[:], 0.0, HSB.r())
        memset("pool", CMB[:], 0.0, CMB.r())
        memset("pool", GSB[:], 0.0, GSB.r())

    zero_states()
    gi = 0
    STOP = cfg.get("STOP", 0)
    if STOP == 1:
        tiles = []
    for kind, t in tiles:
        if kind == "s":
            gi = run_tile(xs, ys, [(0, 16), (16, 16)], [(0, 16, 0), (16, 16, 1)], 32, True, gi, False)
            zero_states()
        else:
            gi = run_tile(xp[t * 512:(t + 1) * 512, :], yp[t * 512:(t + 1) * 512, :],
                          [(0, 128), (128, 128), (256, 128), (384, 128)], [(0, 512, None)], 512, False, gi, t == NT - 1)

    fin_toks = [(k, v) for k, v in P.cnt.items() if not k.startswith("E_") and v > 0]
    P.wait_all("sp", fin_toks)
    P.emit()
    P.close()
    return nc, P


_CACHE = {}


def _get_program(cfg_key, cfg):
    if cfg_key not in _CACHE:
        _CACHE[cfg_key] = build(cfg)
    return _CACHE[cfg_key]


WEIGHT_NAMES = ["ln_in_g", "ln_in_b", "w_in", "ssd_conv_w", "ssd_conv_b", "ssd_dt_bias", "ssd_A_log", "ssd_D", "ssd_norm_w",
                "mlstm_if_b", "mlstm_norm_w", "rg_conv_w", "rg_conv_b", "rg_gate_a_w", "rg_gate_a_b", "rg_gate_x_w",
                "rg_gate_x_b", "rg_lambda", "gla_gate_w2", "gla_gate_b", "gla_norm_w", "w_out", "ln1_g", "ln1_b",
                "mlp_w1", "mlp_b1", "mlp_w2", "mlp_b2", "ln2_g", "ln2_b"]
STATE_IN = [("state_ssd_h", "i_ssd_h"), ("state_ssd_conv", "i_ssd_conv"), ("state_mlstm_C", "i_mC"), ("state_mlstm_n", "i_mn"),
            ("state_mlstm_m", "i_mm"), ("state_rglru_h", "i_rgh"), ("state_rglru_conv", "i_rgconv"), ("state_gla_S", "i_gla")]
OUT_KEYS = ["ssd_h", "ssd_conv", "mC", "mn", "mm", "rgh", "rgconv", "gla"]


def run(inputs, cfg=None, ncores=NCORE):
    cfg = dict(cfg or {})
    NCORE_ = ncores
    NT = cfg.get("NT", SEQ // 512)
    nc, P = _get_program(tuple(sorted(cfg.items())), cfg)
    cst = make_consts()
    in_maps = []
    for c in range(NCORE_):
        m = {"xp": np.ascontiguousarray(inputs["x_prompt"][c, :NT * 512]),
             "xs": np.ascontiguousarray(inputs["x_sample"][2 * c:2 * c + 2].reshape(32, D)),
             "consts": cst}
        for src, dst in STATE_IN:
            m[dst] = np.ascontiguousarray(inputs[src][:, 2 * c:2 * c + 2])
        for w in WEIGHT_NAMES:
            m[w] = np.ascontiguousarray(inputs[w])
        in_maps.append(m)
    res = run_bass_kernel_spmd(nc, in_maps, core_ids=list(range(NCORE_)))
    R = res.results
    if NCORE_ < NCORE:
        R = list(R) + [R[0]] * (NCORE - NCORE_)
    y_prompt = np.stack([R[c]["yp"] for c in range(NCORE)], 0)
    y_sample = np.concatenate([R[c]["ys"].reshape(2, 16, D) for c in range(NCORE)], 0)
    outs = [y_prompt, y_sample]
    for k in OUT_KEYS:
        outs.append(np.stack([R[c]["p_" + k] for c in range(NCORE)], 1))
    for k in OUT_KEYS:
        outs.append(np.concatenate([R[c]["s_" + k] for c in range(NCORE)], 1))
    return tuple(np.asarray(o, np.float32) for o in outs), res


def kernel(**inputs):
    inputs = {k: np.asarray(v) for k, v in inputs.items()}
    outs, _ = run(inputs, {})
    return outs
```
